# Optimizing a Trainium2 kernel written in Bass

```python
import jax, jax.numpy as jnp
from jax import lax
import numpy as np

D_MODEL = 2048
BATCH = 2
SEQ = 8192
DEPTH = 1

GRID_W = 64
CTX_LEN = 256
N_MOD = 6
SSD_HEAD_DIM = 64
SSD_HEADS = 32
SSD_WIDTH = SSD_HEADS * SSD_HEAD_DIM
SSD_GROUPS = 4
SSD_HPG = SSD_HEADS // SSD_GROUPS
SSD_STATE = 128
SSD_BC = SSD_GROUPS * SSD_STATE
SSD_CHUNK = 128
CONV_W = 4
CONV_PAD_LEFT = 2
LRU_WIDTH = D_MODEL
LRU_HEADS = 16
LRU_HEAD_DIM = LRU_WIDTH // LRU_HEADS
LRU_C = 8.0
FFN_HIDDEN = ((8 * D_MODEL + 2) // 3 + 255) // 256 * 256
IN_SIZES = (SSD_WIDTH, SSD_WIDTH, SSD_BC, SSD_BC, 2 * SSD_HEADS, LRU_WIDTH, LRU_WIDTH)
IN_COLS = sum(IN_SIZES)
IN_SPLITS = tuple(int(s) for s in np.cumsum(IN_SIZES)[:-1])
EPS = 1e-6

kernel_name = "hybrid_ssd_rglru_prefix_dit_block"


def rmsnorm(t, g):
    tf = t.astype(jnp.float32)
    tf = tf * lax.rsqrt(jnp.mean(tf * tf, axis=-1, keepdims=True) + EPS)
    return (tf * g.astype(jnp.float32)).astype(t.dtype)


def modulate(h, shift, scale):
    return h * (1 + scale) + shift


def flip(t):
    return jnp.flip(t, axis=1)


def dwconv_centred(t, w, bias):
    L = t.shape[1]
    tp = jnp.pad(t, ((0, 0), (CONV_PAD_LEFT, CONV_W - 1 - CONV_PAD_LEFT), (0, 0)))
    out = bias
    for k in range(CONV_W):
        out = out + w[k] * tp[:, k:k + L]
    return out


def to_col_major(t, rows):
    b_, L, C = t.shape
    return t.reshape(b_, rows, GRID_W, C).transpose(0, 2, 1, 3).reshape(b_, L, C)


def to_row_major(t, rows):
    b_, L, C = t.shape
    return t.reshape(b_, GRID_W, rows, C).transpose(0, 2, 1, 3).reshape(b_, L, C)


def _ssd_chunks(x, dt, A, B):
    b_, L, G, E, P = x.shape
    nc = L // SSD_CHUNK
    x = x.reshape(b_, nc, SSD_CHUNK, G, E, P)
    dt = dt.reshape(b_, nc, SSD_CHUNK, G, E)
    B = B.reshape(b_, nc, SSD_CHUNK, G, SSD_STATE)
    xdt = x * dt[..., None]
    a_cum = jnp.cumsum(dt * A, axis=2)
    decay_to_end = jnp.exp(a_cum[:, :, -1:] - a_cum)
    states = jnp.einsum('bcqgn,bcqge,bcqgep->bcgepn', B, decay_to_end, xdt)
    chunk_decay = jnp.exp(a_cum[:, :, -1])
    return xdt, B, a_cum, states, chunk_decay


def _carry_states(states, chunk_decay, h0):
    def step(h, inp):
        s, d = inp
        return d[..., None, None] * h + s, h
    final, entering = lax.scan(step, h0, (jnp.moveaxis(states, 1, 0), jnp.moveaxis(chunk_decay, 1, 0)))
    return jnp.moveaxis(entering, 0, 1), final


def ssd_scan(x, dt, A, B, C, h0):
    b_, L, G, E, P = x.shape
    xdt, Bc, a_cum, states, chunk_decay = _ssd_chunks(x, dt, A, B)
    entering, final = _carry_states(states, chunk_decay, h0)
    Cc = C.reshape(Bc.shape)
    idx = jnp.arange(SSD_CHUNK)
    lower = (idx[:, None] >= idx[None, :])[None, None, :, :, None, None]
    seg = a_cum[:, :, :, None] - a_cum[:, :, None, :]
    Lmat = jnp.exp(jnp.where(lower, seg, -jnp.inf))
    CB = jnp.einsum('bcign,bcjgn->bcijg', Cc, Bc)
    y_diag = jnp.einsum('bcijg,bcijge,bcjgep->bcigep', CB, Lmat, xdt)
    y_off = jnp.einsum('bcign,bcige,bcgepn->bcigep', Cc, jnp.exp(a_cum), entering)
    return (y_diag + y_off).reshape(b_, L, G, E, P), final


def ssd_final_state(x, dt, A, B, h0):
    _, _, _, states, chunk_decay = _ssd_chunks(x, dt, A, B)
    _, final = _carry_states(states, chunk_decay, h0)
    return final


def ssd_inputs(xs, bs, cs, dt_raw, lp):
    f32 = jnp.float32
    xbc = jax.nn.silu(dwconv_centred(jnp.concatenate([xs, bs, cs], axis=-1), lp['ssd_conv_w'], lp['ssd_conv_b']))
    xs, bs, cs = jnp.split(xbc, [SSD_WIDTH, SSD_WIDTH + SSD_BC], axis=-1)
    b_, L = xs.shape[:2]
    xs = xs.reshape(b_, L, SSD_GROUPS, SSD_HPG, SSD_HEAD_DIM).astype(f32)
    bs = bs.reshape(b_, L, SSD_GROUPS, SSD_STATE).astype(f32)
    cs = cs.reshape(b_, L, SSD_GROUPS, SSD_STATE).astype(f32)
    dt = jax.nn.softplus(dt_raw.astype(f32).reshape(b_, L, 2, SSD_HEADS) + lp['ssd_dt_bias'].astype(f32))
    dt = dt.reshape(b_, L, 2, SSD_GROUPS, SSD_HPG)
    A = -jnp.exp(lp['ssd_a_log'].astype(f32)).reshape(2, SSD_GROUPS, SSD_HPG)
    return xs, bs, cs, dt, A


def lru_coeffs(xr, w_a, b_a, w_x, b_x, lam):
    b_, L, W = xr.shape
    xh = xr.reshape(b_, L, LRU_HEADS, LRU_HEAD_DIM)
    f32 = jnp.float32
    r = jax.nn.sigmoid(jnp.einsum('blhi,hij->blhj', xh, w_a.astype(f32)).reshape(b_, L, W) + b_a.astype(f32))
    i = jax.nn.sigmoid(jnp.einsum('blhi,hij->blhj', xh, w_x.astype(f32)).reshape(b_, L, W) + b_x.astype(f32))
    log_a = -LRU_C * r * jax.nn.softplus(-lam.astype(f32))
    a = jnp.exp(log_a)
    return a, jnp.sqrt(-jnp.expm1(2 * log_a)) * (i * xr)


def linear_scan(a, u, h0):
    u = u.at[:, 0].add(a[:, 0] * h0)
    def comb(lhs, rhs):
        return lhs[0] * rhs[0], rhs[0] * lhs[1] + rhs[1]
    _, h = lax.associative_scan(comb, (a, u), axis=1)
    return h


def lru_bidir(xr_seq, lp, h0_f, h0_b):
    a_f, u_f = lru_coeffs(xr_seq, lp['lru_w_a'][0], lp['lru_b_a'][0], lp['lru_w_x'][0], lp['lru_b_x'][0], lp['lru_lambda'][0])
    a_b, u_b = lru_coeffs(xr_seq, lp['lru_w_a'][1], lp['lru_b_a'][1], lp['lru_w_x'][1], lp['lru_b_x'][1], lp['lru_lambda'][1])
    h_f = linear_scan(a_f, u_f, h0_f)
    h_b_rev = linear_scan(flip(a_b), flip(u_b), h0_b)
    return h_f + flip(h_b_rev), h_f[:, -1], h_b_rev[:, -1]


def mixer_full(h, lp, init, rows):
    z, xs, bs, cs, dt_raw, xr, yr = jnp.split(h @ lp['w_in'], IN_SPLITS, axis=-1)
    b_, L = h.shape[:2]
    xsh, bsh, csh, dt, A = ssd_inputs(xs, bs, cs, dt_raw, lp)
    y_f, s_f = ssd_scan(xsh, dt[:, :, 0], A[0], bsh, csh, init[0])
    y_b, s_b = ssd_scan(flip(xsh), flip(dt[:, :, 1]), A[1], flip(bsh), flip(csh), init[1])
    d_skip = lp['ssd_d'].astype(jnp.float32).reshape(SSD_GROUPS, SSD_HPG)[..., None]
    y = (y_f + flip(y_b) + d_skip * xsh).reshape(b_, L, SSD_WIDTH)
    y = rmsnorm(y * jax.nn.silu(z.astype(jnp.float32)), lp['ssd_norm'])
    o_s = y.astype(h.dtype) @ lp['w_out_ssd']
    xr_seq = xr if rows is None else to_col_major(xr, rows)
    xr_seq = dwconv_centred(xr_seq, lp['lru_conv_w'], lp['lru_conv_b']).astype(jnp.float32)
    r_out, f_f, f_b = lru_bidir(xr_seq, lp, init[2], init[3])
    if rows is not None:
        r_out = to_row_major(r_out, rows)
    o_r = (r_out * jax.nn.gelu(yr.astype(jnp.float32))).astype(h.dtype) @ lp['w_out_lru']
    g_s, g_r = jnp.split(jax.nn.sigmoid(h @ lp['w_gate'] + lp['b_gate']), 2, axis=-1)
    out = (g_s * o_s + g_r * o_r) @ lp['w_o']
    return out, (s_f, s_b, f_f, f_b)


def mixer_states(hc, lp):
    _, xs, bs, cs, dt_raw, xr, _ = jnp.split(hc @ lp['w_in'], IN_SPLITS, axis=-1)
    b_ = hc.shape[0]
    xsh, bsh, _, dt, A = ssd_inputs(xs, bs, cs, dt_raw, lp)
    s0 = jnp.zeros((b_, SSD_GROUPS, SSD_HPG, SSD_HEAD_DIM, SSD_STATE), jnp.float32)
    s_f = ssd_final_state(xsh, dt[:, :, 0], A[0], bsh, s0)
    s_b = ssd_final_state(flip(xsh), flip(dt[:, :, 1]), A[1], flip(bsh), s0)
    xr_seq = dwconv_centred(xr, lp['lru_conv_w'], lp['lru_conv_b']).astype(jnp.float32)
    h0 = jnp.zeros((b_, LRU_WIDTH), jnp.float32)
    _, f_f, f_b = lru_bidir(xr_seq, lp, h0, h0)
    return (s_f, s_b, f_f, f_b)


def zero_states(b_):
    s0 = jnp.zeros((b_, SSD_GROUPS, SSD_HPG, SSD_HEAD_DIM, SSD_STATE), jnp.float32)
    h0 = jnp.zeros((b_, LRU_WIDTH), jnp.float32)
    return (s0, s0, h0, h0)


def swiglu(h, w13, w2):
    g, u = jnp.split(h @ w13, 2, axis=-1)
    return (jax.nn.silu(g) * u) @ w2


def setup_inputs(seed: int = 0) -> dict:
    key = jax.random.key(seed)
    ks = jax.random.split(key, 30)
    D = D_MODEL
    f32 = jnp.float32

    def nrm(k, shape, fan_in, s=1.0):
        return jax.random.normal(k, shape, f32) * (s * fan_in ** -0.5)

    def small(k, shape, s=0.02):
        return jax.random.normal(k, shape, f32) * s

    dt0 = jnp.exp(jax.random.uniform(ks[10], (DEPTH, 2, SSD_HEADS), f32, np.log(1e-3), np.log(1e-1)))
    a_target = jax.random.uniform(ks[21], (DEPTH, 2, LRU_WIDTH), f32, 0.9, 0.999)
    sig_lam = a_target ** (1.0 / LRU_C)
    return {
        "x": jax.random.normal(ks[0], (BATCH, SEQ, D), f32),
        "c": jax.random.normal(ks[1], (BATCH, D), f32),
        "ctx": jax.random.normal(ks[2], (BATCH, CTX_LEN, D), f32),
        "c_ctx": jax.random.normal(ks[3], (D,), f32),
        "w_ada": nrm(ks[4], (DEPTH, D, N_MOD * D), D, 0.5),
        "b_ada": small(ks[5], (DEPTH, N_MOD * D)),
        "norm_mix": 1.0 + small(ks[6], (DEPTH, D)),
        "norm_ffn": 1.0 + small(ks[7], (DEPTH, D)),
        "w_in": nrm(ks[8], (DEPTH, D, IN_COLS), D),
        "ssd_conv_w": nrm(ks[9], (DEPTH, CONV_W, SSD_WIDTH + 2 * SSD_BC), CONV_W),
        "ssd_conv_b": small(ks[11], (DEPTH, SSD_WIDTH + 2 * SSD_BC)),
        "ssd_dt_bias": dt0 + jnp.log(-jnp.expm1(-dt0)),
        "ssd_a_log": jnp.log(jax.random.uniform(ks[12], (DEPTH, 2, SSD_HEADS), f32, 1.0, 16.0)),
        "ssd_d": 1.0 + small(ks[13], (DEPTH, SSD_HEADS)),
        "ssd_norm": 1.0 + small(ks[14], (DEPTH, SSD_WIDTH)),
        "w_out_ssd": nrm(ks[15], (DEPTH, SSD_WIDTH, D), SSD_WIDTH),
        "lru_conv_w": nrm(ks[16], (DEPTH, CONV_W, LRU_WIDTH), CONV_W),
        "lru_conv_b": small(ks[17], (DEPTH, LRU_WIDTH)),
        "lru_w_a": nrm(ks[18], (DEPTH, 2, LRU_HEADS, LRU_HEAD_DIM, LRU_HEAD_DIM), LRU_HEAD_DIM),
        "lru_b_a": small(ks[19], (DEPTH, 2, LRU_WIDTH)),
        "lru_w_x": nrm(ks[20], (DEPTH, 2, LRU_HEADS, LRU_HEAD_DIM, LRU_HEAD_DIM), LRU_HEAD_DIM),
        "lru_b_x": small(ks[22], (DEPTH, 2, LRU_WIDTH)),
        "lru_lambda": jnp.log(sig_lam) - jnp.log1p(-sig_lam),
        "w_out_lru": nrm(ks[23], (DEPTH, LRU_WIDTH, D), LRU_WIDTH),
        "w_gate": nrm(ks[24], (DEPTH, D, 2 * D), D),
        "b_gate": small(ks[25], (DEPTH, 2 * D)),
        "w_o": nrm(ks[26], (DEPTH, D, D), D),
        "ffn_w13": nrm(ks[27], (DEPTH, D, 2 * FFN_HIDDEN), D),
        "ffn_w2": nrm(ks[28], (DEPTH, FFN_HIDDEN, D), FFN_HIDDEN),
        "final_norm": 1.0 + small(ks[29], (D,)),
    }


def reference(x, c, ctx, c_ctx, w_ada, b_ada, norm_mix, norm_ffn, w_in, ssd_conv_w, ssd_conv_b,
              ssd_dt_bias, ssd_a_log, ssd_d, ssd_norm, w_out_ssd, lru_conv_w, lru_conv_b,
              lru_w_a, lru_b_a, lru_w_x, lru_b_x, lru_lambda, w_out_lru, w_gate, b_gate, w_o,
              ffn_w13, ffn_w2, final_norm):
    rows = x.shape[1] // GRID_W
    for l in range(DEPTH):
        lp = dict(w_in=w_in[l], ssd_conv_w=ssd_conv_w[l], ssd_conv_b=ssd_conv_b[l],
                  ssd_dt_bias=ssd_dt_bias[l], ssd_a_log=ssd_a_log[l], ssd_d=ssd_d[l],
                  ssd_norm=ssd_norm[l], w_out_ssd=w_out_ssd[l], lru_conv_w=lru_conv_w[l],
                  lru_conv_b=lru_conv_b[l], lru_w_a=lru_w_a[l], lru_b_a=lru_b_a[l],
                  lru_w_x=lru_w_x[l], lru_b_x=lru_b_x[l], lru_lambda=lru_lambda[l],
                  w_out_lru=w_out_lru[l], w_gate=w_gate[l], b_gate=b_gate[l], w_o=w_o[l])
        mod = jax.nn.silu(c) @ w_ada[l] + b_ada[l]
        sh_m, sc_m, g_m, sh_f, sc_f, g_f = jnp.split(mod[:, None, :], N_MOD, axis=-1)
        mod_c = jax.nn.silu(c_ctx) @ w_ada[l] + b_ada[l]
        csh_m, csc_m, cg_m, csh_f, csc_f, cg_f = jnp.split(mod_c, N_MOD, axis=-1)
        hc = modulate(rmsnorm(ctx, norm_mix[l]), csh_m, csc_m)
        if l + 1 < DEPTH:
            mix_c, ctx_states = mixer_full(hc, lp, zero_states(ctx.shape[0]), None)
            ctx = ctx + cg_m * mix_c
            ctx = ctx + cg_f * swiglu(modulate(rmsnorm(ctx, norm_ffn[l]), csh_f, csc_f), ffn_w13[l], ffn_w2[l])
        else:
            ctx_states = mixer_states(hc, lp)
        h = modulate(rmsnorm(x, norm_mix[l]), sh_m, sc_m)
        mix, _ = mixer_full(h, lp, ctx_states, rows)
        x = x + g_m * mix
        h = modulate(rmsnorm(x, norm_ffn[l]), sh_f, sc_f)
        x = x + g_f * swiglu(h, ffn_w13[l], ffn_w2[l])
    return rmsnorm(x, final_norm)
```

```python
import numpy as np
from contextlib import ExitStack
import concourse.bass as bass
import concourse.mybir as mybir
from concourse.bass_utils import run_bass_kernel_spmd

F32 = mybir.dt.float32
BF16 = mybir.dt.bfloat16
AF = mybir.ActivationFunctionType
ALU = mybir.AluOpType
AX = mybir.AxisListType

D = 2048
SEQ = 8192
CTX = 256
NKC = 16
FFN = 5632
EPS = 1e-6
OZ, OXS, OB, OC, ODT, OXR, OYR = 0, 2048, 4096, 4608, 5120, 5184, 7232


class Buf:
    def __init__(self, t, parent=None):
        self.t = t
        self.w = {}
        self.r = {}
        self.dsem = None
        self.dcnt = 0
        self.parent = parent
        self.kids = {}

    def __getitem__(self, k):
        return self.t[k]

    def part(self, key):
        if key not in self.kids:
            self.kids[key] = Buf(self.t, parent=self)
        return self.kids[key]

    def parts(self, keys):
        return [self.part(k) for k in keys]

    def wsets(self):
        out = [self.w]
        if self.parent is not None:
            out.append(self.parent.w)
        out.extend(k.w for k in self.kids.values())
        return out

    def rsets(self):
        out = [self.r]
        if self.parent is not None:
            out.append(self.parent.r)
        out.extend(k.r for k in self.kids.values())
        return out


class KB:
    def __init__(self, nc, stack):
        self.nc = nc
        self.gstack = stack
        self.eng = {'pe': nc.tensor, 'act': nc.scalar, 'dve': nc.vector, 'pool': nc.gpsimd, 'sp': nc.sync}
        self.sem = {}
        self.cnt = {}
        self.allsems = {}
        for e in ['pe', 'act', 'dve', 'pool']:
            self.sem[e] = stack.enter_context(nc.semaphore("s_" + e))
            self.cnt[e] = 0
        self.waited = {e: {} for e in self.eng}
        self.free_dsems = []
        self.nd = 0
        self.rr = 0

    def sb(self, st, name, shape, dt):
        return Buf(st.enter_context(self.nc.sbuf_tensor(name, list(shape), dt)))

    def ps(self, st, name, shape, dt=F32):
        return Buf(st.enter_context(self.nc.psum_tensor(name, list(shape), dt)))

    def _dsem(self, b):
        if b.dsem is None:
            if self.free_dsems:
                b.dsem, b.dcnt = self.free_dsems.pop()
            else:
                self.nd += 1
                b.dsem = self.gstack.enter_context(self.nc.semaphore("d%d" % self.nd))
                b.dcnt = 0
        return b.dsem

    def release(self, bufs):
        for b in bufs:
            if b.dsem is not None:
                self.free_dsems.append((b.dsem, b.dcnt))
                b.dsem = None

    def _deps(self, e, reads, writes):
        deps = {}
        for b in reads:
            for ws in b.wsets():
                for s, v in ws.items():
                    if deps.get(s, 0) < v:
                        deps[s] = v
        for b in writes:
            for ws in b.wsets() + b.rsets():
                for s, v in ws.items():
                    if deps.get(s, 0) < v:
                        deps[s] = v
        own = self.sem.get(e)
        wd = self.waited[e]
        for s, v in deps.items():
            if e == 'pe' and s is own:
                continue
            if wd.get(s, 0) < v:
                self.eng[e].wait_ge(s, v)
                wd[s] = v

    def op(self, e, fn, reads=(), writes=()):
        self._deps(e, reads, writes)
        ins = fn(self.eng[e])
        self.cnt[e] += 1
        s = self.sem[e]
        ins.then_inc(s, 1)
        v = self.cnt[e]
        self.allsems[s] = v
        for b in reads:
            b.r[s] = v
        for b in writes:
            b.w[s] = v
        return ins

    def dma(self, q, out, in_, sembuf, reads=(), writes=(), **kw):
        self._deps(q, reads, writes)
        ins = self.eng[q].dma_start(out=out, in_=in_, **kw)
        s = self._dsem(sembuf)
        sembuf.dcnt += 16
        ins.then_inc(s, 16)
        v = sembuf.dcnt
        self.allsems[s] = v
        for b in reads:
            b.r[s] = v
        for b in writes:
            b.w[s] = v
        return ins

    def barrier(self, engines=None):
        for e in (engines or list(self.eng)):
            wd = self.waited[e]
            for s, v in self.allsems.items():
                if wd.get(s, 0) < v:
                    self.eng[e].wait_ge(s, v)
                    wd[s] = v

    def alt(self, choices=('act', 'dve')):
        self.rr += 1
        return choices[self.rr % len(choices)]


class Pool:
    def __init__(self, kb, st, name, shape, dt, n, psum=False):
        mk = kb.ps if psum else kb.sb
        self.bufs = [mk(st, "%s%d" % (name, i), shape, dt) for i in range(n)]
        self.i = 0

    def get(self):
        b = self.bufs[self.i % len(self.bufs)]
        self.i += 1
        return b


def evac_affine(kb, e, out_ap, in_ap, scale_ap, bias_ap, reads, writes):
    if e == 'act':
        kb.op('act', lambda g: g.activation(out=out_ap, in_=in_ap, func=AF.Identity, bias=bias_ap, scale=scale_ap),
              reads=reads, writes=writes)
    else:
        kb.op('dve', lambda g: g.tensor_scalar(out=out_ap, in0=in_ap, scalar1=scale_ap, scalar2=bias_ap,
                                               op0=ALU.mult, op1=ALU.add), reads=reads, writes=writes)


def build_program(dbg=None):
    nc = bass.Bass("TRN2", target_bir_lowering=False)
    gs = ExitStack()
    kb = KB(nc, gs)

    def din(name, shape, dt=F32):
        return Buf(nc.dram_tensor(name, list(shape), dt, kind="ExternalInput").ap())

    def dscr(name, shape, dt=F32):
        kind = "ExternalOutput" if (dbg and name in dbg) else "Internal"
        return Buf(nc.dram_tensor(name, list(shape), dt, kind=kind).ap())

    xb = din("xb", [SEQ, D])
    xo = din("xo", [2048, D])
    ctxb = din("ctxb", [CTX, D])
    cvec = din("cvec", [128, NKC, 2])
    w_ada = din("w_ada", [D, 6 * D])
    b_ada = din("b_ada", [128, 96])
    nrm = din("nrm", [128, 3, NKC])
    fin_norm = din("fin_norm", [1, D])
    ident = din("ident", [128, 128])
    w_in = din("w_in", [D, 9280])
    cw_ssd = din("cw_ssd", [128, 24, 5])
    dtp = din("dtp", [128, 2, 2])
    dskb = din("dskb", [128, 32])
    masks = din("masks", [128, 2, 128])
    sel = din("sel", [128, 4])
    cmask = din("cmask", [128, 2, 64])
    endsel = din("endsel", [128, 2, 128])
    dmask = din("dmask", [32, 4, 8])
    lcw = din("lcw", [128, 16, 5])
    lgw = din("lgw", [16, 128, 4, 128])
    lgb = din("lgb", [128, 16, 4])
    llam = din("llam", [128, 16, 2])
    w_out_ssd = din("w_out_ssd", [D, D])
    w_out_lru = din("w_out_lru", [D, D])
    w_gate = din("w_gate", [D, 2 * D])
    bgate = din("bgate", [128, 32])
    w_o = din("w_o", [D, D])
    w13 = din("w13", [D, 2 * FFN])
    w2 = din("w2", [FFN, D])
    out = Buf(nc.dram_tensor("out", [2048, D], F32, kind="ExternalOutput").ap())

    gvec = dscr("gvec", [2, D])
    pre_ssd = dscr("pre_ssd", [3136, SEQ + 4])
    pre_ssd_c = dscr("pre_ssd_c", [3136, CTX + 4])
    xsb_tm = dscr("xsb_tm", [SEQ, 2560], BF16)
    xsb_tm_c = dscr("xsb_tm_c", [CTX, 2560], BF16)
    bc_fm = dscr("bc_fm", [1024, SEQ], BF16)
    bc_fm_c = dscr("bc_fm_c", [1024, CTX], BF16)
    ac_scr = dscr("ac_scr", [2, 4096])
    y_own = dscr("y_own", [2048, D])
    xsb_own = dscr("xsb_own", [2048, 2560], BF16)
    bc_own = dscr("bc_own", [1024, 2048], BF16)
    pdt_own = dscr("pdt_own", [64, 2048])
    pre_xr = dscr("pre_xr", [2048, SEQ + 4])
    pre_xr_c = dscr("pre_xr_c", [2048, CTX + 4])
    r_own = dscr("r_own", [2048, 2048])
    sz_scr = dscr("sz_scr", [2048, D])
    ynT_scr = dscr("ynT_scr", [D, 2048], BF16)
    rgT_scr = dscr("rgT_scr", [D, 2048], BF16)
    gT_scr = dscr("gT_scr", [2 * D, 2048], BF16)
    mT_scr = dscr("mT_scr", [D, 2048], BF16)
    x1_scr = dscr("x1_scr", [2048, D])
    aT_scr = dscr("aT_scr", [FFN, 2048], BF16)
    x2_scr = dscr("x2_scr", [2048, D])

    identb = kb.sb(gs, "identb", [128, 128], F32)
    modT = kb.sb(gs, "modT", [128, 96, 2], F32)
    scsh = kb.sb(gs, "scsh", [128, 6, NKC], F32)
    nrmb = kb.sb(gs, "nrmb", [128, 3, NKC], F32)
    kb.dma('sp', identb[:], ident[:, :], identb, reads=[ident], writes=[identb])
    kb.dma('sp', nrmb[:], nrm[:, :, :], nrmb, reads=[nrm], writes=[nrmb])

    with ExitStack() as st:
        st.enter_context(nc.named_scope("st_A"))
        sv = kb.sb(st, "sv", [128, NKC, 2], F32)
        svs = kb.sb(st, "svs", [128, NKC, 2], F32)
        bad = kb.sb(st, "bad", [128, 96], F32)
        wpool = Pool(kb, st, "wada", [128, NKC, 512], F32, 2)
        pm = kb.ps(st, "pm", [128, 96, 2], F32)
        kb.dma('sp', sv[:], cvec[:, :, :], sv, reads=[cvec], writes=[sv])
        kb.dma('sp', bad[:], b_ada[:, :], bad, reads=[b_ada], writes=[bad])
        kb.op('act', lambda g: g.activation(out=svs[:], in_=sv[:], func=AF.Silu), reads=[sv], writes=[svs])
        wv = w_ada.t.rearrange("(kc p) n -> p kc n", p=128)
        for cg in range(24):
            wt = wpool.get()
            kb.dma('sp', wt[:], wv[:, :, cg * 512:(cg + 1) * 512], wt, reads=[w_ada], writes=[wt])
            for j in range(4):
                cb = cg * 4 + j
                for kc in range(NKC):
                    kb.op('pe', lambda g: g.matmul(pm[:, cb, :], wt[:, kc, j * 128:(j + 1) * 128], svs[:, kc, :],
                                                   start=(kc == 0), stop=(kc == NKC - 1)),
                          reads=[wt, svs], writes=[pm])
        kb.op('dve', lambda g: g.tensor_tensor(out=modT[:], in0=pm[:], in1=bad[:].unsqueeze(2).to_broadcast([128, 96, 2]),
                                               op=ALU.add), reads=[pm, bad], writes=[modT])
        for (dst, nidx, scblk, v) in ((0, 0, 16, 0), (2, 0, 16, 1), (4, 1, 64, 0)):
            kb.op('dve', lambda g: g.scalar_tensor_tensor(out=scsh[:, dst, :], in0=modT[:, scblk:scblk + 16, v], scalar=1.0,
                                                          in1=nrmb[:, nidx, :], op0=ALU.add, op1=ALU.mult),
                  reads=[modT, nrmb], writes=[scsh])
        for (dst, shblk, v) in ((1, 0, 0), (3, 0, 1), (5, 48, 0)):
            kb.op('dve', lambda g: g.tensor_copy(out=scsh[:, dst, :], in_=modT[:, shblk:shblk + 16, v]),
                  reads=[modT], writes=[scsh])
        gv = gvec.t.rearrange("g (kc p) -> p g kc", p=128)
        gtmp = kb.sb(st, "gtmp", [128, 2, NKC], F32)
        kb.op('dve', lambda g: g.tensor_copy(out=gtmp[:, 0, :], in_=modT[:, 32:48, 0]), reads=[modT], writes=[gtmp])
        kb.op('dve', lambda g: g.tensor_copy(out=gtmp[:, 1, :], in_=modT[:, 80:96, 0]), reads=[modT], writes=[gtmp])
        kb.dma('sp', gv, gtmp[:], gtmp, reads=[gtmp], writes=[gvec], allow_slow_non_contiguous=True)
        kb.barrier()
        kb.release([sv, svs, bad, gtmp] + wpool.bufs)

    def make_hT(pools, src_ap, src_buf, hT, col0, sci, shi):
        xn = make_xn(pools, src_ap, src_buf)
        xn_T(pools, xn, hT, col0, lambda kc: scsh[:, sci, kc:kc + 1], lambda kc: scsh[:, shi, kc:kc + 1], [scsh])

    def make_xn(pools, src_ap, src_buf):
        xt = pools['xt'].get()
        kb.dma('sp', xt[:], src_ap, xt, reads=[src_buf], writes=[xt])
        return norm_rows(pools, xt)

    def norm_rows(pools, xt):
        sq = pools['sq'].get()
        ss = pools['ss'].get()
        kb.op('dve', lambda g: g.memset(ss[:], 0.0), writes=[ss])
        kb.op('act', lambda g: g.activation(out=sq[:], in_=xt[:], func=AF.Square, accum_out=ss[:, 0:1]),
              reads=[xt], writes=[sq, ss])
        kb.op('dve', lambda g: g.tensor_scalar(out=ss[:, 1:2], in0=ss[:, 0:1], scalar1=1.0 / D, scalar2=EPS,
                                               op0=ALU.mult, op1=ALU.add), reads=[ss], writes=[ss])
        kb.op('act', lambda g: g.activation(out=ss[:, 3:4], in_=ss[:, 1:2], func=AF.Sqrt), reads=[ss], writes=[ss])
        kb.op('dve', lambda g: g.reciprocal(out=ss[:, 2:3], in_=ss[:, 3:4]), reads=[ss], writes=[ss])
        xn = pools['xn'].get()
        kb.op('dve', lambda g: g.tensor_scalar(out=xn[:], in0=xt[:], scalar1=ss[:, 2:3], scalar2=None, op0=ALU.mult),
              reads=[xt, ss], writes=[xn])
        return xn

    def xn_T(pools, xn, hT, col0, scale_fn, shift_fn, pbufs):
        for q in range(4):
            pt = pools['pt'].get()
            for j in range(4):
                kc = q * 4 + j
                kb.op('pe', lambda g: g.transpose(out=pt[:, j * 128:(j + 1) * 128], in_=xn[:, kc * 128:(kc + 1) * 128],
                                                  identity=identb[:]), reads=[xn, identb], writes=[pt])
            e = kb.alt()
            for j in range(4):
                kc = q * 4 + j
                dst = hT[:, kc, col0:col0 + 128]
                src = pt[:, j * 128:(j + 1) * 128]
                if shift_fn is not None:
                    evac_affine(kb, e, dst, src, scale_fn(kc), shift_fn(kc), reads=[pt] + pbufs, writes=[hT.part(kc)])
                elif e == 'dve':
                    kb.op('dve', lambda g: g.tensor_scalar(out=dst, in0=src, scalar1=scale_fn(kc), scalar2=None, op0=ALU.mult),
                          reads=[pt] + pbufs, writes=[hT.part(kc)])
                else:
                    kb.op('act', lambda g: g.activation(out=dst, in_=src, func=AF.Copy, scale=scale_fn(kc)),
                          reads=[pt] + pbufs, writes=[hT.part(kc)])

    def norm_T(pools, xt, hT, col0, scale_fn, shift_fn, pbufs):
        xn = norm_rows(pools, xt)
        xn_T(pools, xn, hT, col0, scale_fn, shift_fn, pbufs)

    def build_hT(pools, src, hT, ntiles, sci, shi):
        xn = make_xn(pools, src.t[0:128, :], src)
        for tt in range(ntiles):
            nxt = make_xn(pools, src.t[(tt + 1) * 128:(tt + 2) * 128, :], src) if tt + 1 < ntiles else None
            xn_T(pools, xn, hT, tt * 128, lambda kc: scsh[:, sci, kc:kc + 1], lambda kc: scsh[:, shi, kc:kc + 1], [scsh])
            xn = nxt

    def norm_pools(st, tag, nxn=1, npt=2, nxt=2):
        return {
            'xt': Pool(kb, st, tag + "xt", [128, D], F32, nxt),
            'sq': Pool(kb, st, tag + "sq", [128, D], BF16, 1),
            'ss': Pool(kb, st, tag + "ss", [128, 4], F32, 3),
            'xn': Pool(kb, st, tag + "xn", [128, D], F32, nxn),
            'pt': Pool(kb, st, tag + "pt", [128, 512], F32, npt, psum=True),
        }

    def load_weight(st_pool, w_dram, col_lo, ncols, wsb, dst_lo, row_lo=0, nkc=NKC, kc_dst=0):
        wv_ = w_dram.t[row_lo:row_lo + nkc * 128, :].rearrange("(kc p) n -> p kc n", p=128)
        c = 0
        while c < ncols:
            n = min(128, ncols - c)
            stg = st_pool.get()
            kb.dma('sp', stg[:, 0:nkc, 0:n], wv_[:, :, col_lo + c:col_lo + c + n], stg, reads=[w_dram], writes=[stg])
            e = kb.alt(('act', 'dve'))
            dst = wsb[:, kc_dst:kc_dst + nkc, dst_lo + c:dst_lo + c + n]
            wpart = wsb.part((dst_lo + c) // 128)
            if e == 'act':
                kb.op('act', lambda g: g.copy(out=dst, in_=stg[:, 0:nkc, 0:n]), reads=[stg], writes=[wpart])
            else:
                kb.op(e, lambda g: g.tensor_copy(out=dst, in_=stg[:, 0:nkc, 0:n]), reads=[stg], writes=[wpart])
            c += n

    NT_C = dbg.get("nt_c", 16) if dbg else 16

    def proj_stage(tag, wcol0, ncols, jobs, zero_pads):
        nblk = (ncols + 127) // 128
        with ExitStack() as st:
            st.enter_context(nc.named_scope("st_" + tag))
            pools = norm_pools(st, tag, nxn=2, npt=3, nxt=3)
            wstg = Pool(kb, st, tag + "wstg", [128, NKC, 128], F32, 2)
            wsb = kb.sb(st, tag + "w", [128, NKC, ncols], BF16)
            load_weight(wstg, w_in, wcol0, ncols, wsb, 0)
            hTp = Pool(kb, st, tag + "hT", [128, NKC, 512], BF16, 2)
            pg = Pool(kb, st, tag + "pg", [128, 512], F32, 5, psum=True)
            ostg = Pool(kb, st, tag + "ostg", [128, 512], F32, 4)
            zt = kb.sb(st, tag + "zt", [128, 4], F32)
            kb.op('dve', lambda g: g.memset(zt[:], 0.0), writes=[zt])
            for (dst, L) in zero_pads:
                for rb in range(nblk):
                    r0 = rb * 128
                    nr = min(128, ncols - r0)
                    kb.dma('pool', dst.t[r0:r0 + nr, 0:2], zt[0:nr, 0:2], zt, reads=[zt], writes=[dst])
                    kb.dma('pool', dst.t[r0:r0 + nr, L + 2:L + 4], zt[0:nr, 2:4], zt, reads=[zt], writes=[dst])
            hTs = {}

            tiles = []
            for ji, jb in enumerate(jobs):
                for i in range(jb[2] // 128):
                    tiles.append((ji, i))
            xns = {}

            def prep_a(k):
                if k < len(tiles):
                    ji, i = tiles[k]
                    xns[k] = make_xn(pools, jobs[ji][1](i), jobs[ji][0])

            def prep_b(k):
                ji, i = tiles[k]
                (src, apfn, nt, dst, dcol, sci, shi) = jobs[ji]
                if i == 0:
                    hTs[ji] = hTp.get()
                xn_T(pools, xns.pop(k), hTs[ji], i * 128, lambda kc: scsh[:, sci, kc:kc + 1], lambda kc: scsh[:, shi, kc:kc + 1], [scsh])

            tk = 0
            prep_a(0)
            n0 = jobs[0][2] // 128
            for k in range(n0):
                prep_a(k + 1)
                prep_b(k)
            tk = n0
            for ji, (src, apfn, nt, dst, dcol, sci, shi) in enumerate(jobs):
                hT = hTs[ji]
                nxt_tiles = (jobs[ji + 1][2] // 128) if ji + 1 < len(jobs) else 0
                step = max(1, nblk // max(1, nxt_tiles))
                emitted = 0
                for cb in range(nblk):
                    c0 = cb * 128
                    ncol = min(128, ncols - c0)
                    pp = pg.get()
                    for kc in range(NKC):
                        kb.op('pe', lambda g: g.matmul(pp[0:ncol, 0:nt], wsb[:, kc, c0:c0 + ncol], hT[:, kc, 0:nt],
                                                       start=(kc == 0), stop=(kc == NKC - 1)), reads=[wsb.part(cb), hT.part(kc)], writes=[pp])
                    og = ostg.get()
                    if kb.alt() == 'act':
                        kb.op('act', lambda g: g.copy(out=og[0:ncol, 0:nt], in_=pp[0:ncol, 0:nt]), reads=[pp], writes=[og])
                    else:
                        kb.op('dve', lambda g: g.tensor_copy(out=og[0:ncol, 0:nt], in_=pp[0:ncol, 0:nt]), reads=[pp], writes=[og])
                    kb.dma('pool', dst.t[c0:c0 + ncol, dcol:dcol + nt], og[0:ncol, 0:nt], og, reads=[og], writes=[dst])
                    if emitted < nxt_tiles and (cb + 1) % step == 0:
                        prep_a(tk + 1)
                        prep_b(tk)
                        tk += 1
                        emitted += 1
                while emitted < nxt_tiles:
                    prep_a(tk + 1)
                    prep_b(tk)
                    tk += 1
                    emitted += 1
            kb.barrier()
            kb.release(sum([p.bufs for p in pools.values()], []) + wstg.bufs + hTp.bufs + ostg.bufs + [zt])

    def rm_tile(src, t0):
        return lambda i: src.t[t0 + i * 128:t0 + (i + 1) * 128, :]
    jobsC = [(ctxb, rm_tile(ctxb, 0), 256, pre_ssd_c, 2, 2, 3)]
    jobsC += [(xb, rm_tile(xb, s_ * 512), 512, pre_ssd, 2 + s_ * 512, 0, 1) for s_ in range(NT_C)]
    proj_stage("C", OXS, 3136, jobsC, [(pre_ssd, SEQ), (pre_ssd_c, CTX)])

    with ExitStack() as st:
        st.enter_context(nc.named_scope("st_C2"))
        cw = kb.sb(st, "cw", [128, 24, 5], F32)
        kb.dma('sp', cw[:], cw_ssd[:, :, :], cw, reads=[cw_ssd], writes=[cw])
        identbf = kb.sb(st, "identbf", [128, 128], BF16)
        kb.op('dve', lambda g: g.tensor_copy(out=identbf[:], in_=identb[:]), reads=[identb], writes=[identbf])
        prep = Pool(kb, st, "cpre", [128, 2051], F32, 3)
        accp = Pool(kb, st, "cacc", [128, 2048], F32, 3)
        ctmp = kb.sb(st, "ctmp", [128, 2048], F32)
        xcp = Pool(kb, st, "cxc", [128, 2048], BF16, 3)
        ptp = Pool(kb, st, "cpt", [128, 1024], BF16, 4, psum=True)
        tmp_ = Pool(kb, st, "ctm", [128, 16, 128], BF16, 3)
        seqs = [(pre_ssd_c, CTX, xsb_tm_c, bc_fm_c), (pre_ssd, min(SEQ, NT_C * 512), xsb_tm, bc_fm)]
        units = []
        for (pre, L, xsb, bcf) in seqs:
            TG = min(L, 2048)
            for tg in range(L // TG):
                for blk in range(24):
                    units.append((pre, xsb, bcf, TG, tg * TG, blk))

        def c2_front(u):
            (pre, xsb, bcf, TG, t0, blk) = u
            pt_ = prep.get()
            kb.dma('sp', pt_[:, 0:TG + 3], pre.t[blk * 128:(blk + 1) * 128, t0:t0 + TG + 3], pt_, reads=[pre], writes=[pt_])
            acc = accp.get()
            kb.op('act', lambda g: g.activation(out=acc[:, 0:TG], in_=pt_[:, 0:TG], func=AF.Identity,
                                                scale=cw[:, blk, 0:1], bias=cw[:, blk, 4:5]), reads=[pt_, cw], writes=[acc])
            return (pt_, acc)

        def c2_back(u, fr):
            (pre, xsb, bcf, TG, t0, blk) = u
            pt_, acc = fr
            for k in range(1, 4):
                kb.op('dve', lambda g: g.scalar_tensor_tensor(out=acc[:, 0:TG], in0=pt_[:, k:k + TG], scalar=cw[:, blk, k:k + 1],
                                                              in1=acc[:, 0:TG], op0=ALU.mult, op1=ALU.add),
                      reads=[pt_, cw, acc], writes=[acc])
            xc = xcp.get()
            kb.op('act', lambda g: g.activation(out=xc[:, 0:TG], in_=acc[:, 0:TG], func=AF.Silu), reads=[acc], writes=[xc])
            if blk >= 16:
                kb.dma('pool', bcf.t[(blk - 16) * 128:(blk - 15) * 128, t0:t0 + TG], xc[:, 0:TG], xc, reads=[xc], writes=[bcf])
            if blk < 20:
                tm = tmp_.get()
                ntt = TG // 128
                for q in range((ntt + 3) // 4):
                    pp = ptp.get()
                    nj = min(4, ntt - q * 4)
                    for j in range(nj):
                        tt = q * 4 + j
                        kb.op('pe', lambda g: g.transpose(out=pp[:, j * 128:(j + 1) * 128], in_=xc[:, tt * 128:(tt + 1) * 128],
                                                          identity=identbf[:]), reads=[xc, identbf], writes=[pp])
                    src_v = pp[:, 0:nj * 128].rearrange("p (a b) -> p a b", a=nj)
                    if kb.alt() == 'act':
                        kb.op('act', lambda g: g.copy(out=tm[:, q * 4:q * 4 + nj, :], in_=src_v), reads=[pp], writes=[tm.part(q)])
                    else:
                        kb.op('dve', lambda g: g.tensor_copy(out=tm[:, q * 4:q * 4 + nj, :], in_=src_v), reads=[pp], writes=[tm.part(q)])
                kb.dma('pool', xsb.t[t0:t0 + TG, blk * 128:(blk + 1) * 128].rearrange("(tt p) c -> p tt c", p=128),
                       tm[:, 0:ntt, :], tm, reads=[tm], writes=[xsb])

        fr = c2_front(units[0])
        for ui in range(len(units)):
            nfr = c2_front(units[ui + 1]) if ui + 1 < len(units) else None
            c2_back(units[ui], fr)
            fr = nfr
        kb.barrier()
        kb.release([cw] + prep.bufs + xcp.bufs + tmp_.bufs)

    with ExitStack() as st:
        st.enter_context(nc.named_scope("st_D0"))
        selb = kb.sb(st, "D0sel", [128, 4], F32)
        kb.dma('sp', selb[:], sel[:, :], selb, reads=[sel], writes=[selb])
        idsel = kb.sb(st, "idsel", [128, 4, 128], BF16)
        idsel32 = kb.sb(st, "idsel32", [128, 4, 128], F32)
        for b_ in range(4):
            kb.op('dve', lambda g: g.tensor_scalar(out=idsel[:, b_, :], in0=identb[:], scalar1=selb[:, b_:b_ + 1], scalar2=None, op0=ALU.mult),
                  reads=[identb, selb], writes=[idsel])
            kb.op('dve', lambda g: g.tensor_scalar(out=idsel32[:, b_, :], in0=identb[:], scalar1=selb[:, b_:b_ + 1], scalar2=None, op0=ALU.mult),
                  reads=[identb, selb], writes=[idsel32])
        xcp_ = Pool(kb, st, "D0xc", [128, 2560], BF16, 8)
        bcp_ = Pool(kb, st, "D0bc", [128, 8, 128], BF16, 8)
        pdp_ = Pool(kb, st, "D0pd", [64, 128], F32, 8)
        xop_ = Pool(kb, st, "D0xo", [128, 2560], BF16, 2)
        bop_ = Pool(kb, st, "D0bo", [128, 8, 128], BF16, 2)
        pop_ = Pool(kb, st, "D0po", [64, 128], F32, 2)
        psel = Pool(kb, st, "D0ps", [128, 512], F32, 4, psum=True)
        bc_v = bc_fm.t.rearrange("(b p) t -> p b t", p=128)
        bco_v = bc_own.t.rearrange("(b p) t -> p b t", p=128)

        def evac(dst_ap, src_ap, rd, wr):
            if kb.alt() == 'act':
                kb.op('act', lambda g: g.copy(out=dst_ap, in_=src_ap), reads=rd, writes=wr)
            else:
                kb.op('dve', lambda g: g.tensor_copy(out=dst_ap, in_=src_ap), reads=rd, writes=wr)

        for i in range(16):
            xcs = []; bcs = []; pds = []
            for b_ in range(4):
                c = 16 * b_ + i
                xc = xcp_.get(); bc = bcp_.get(); pd_ = pdp_.get()
                kb.dma('sp', xc[:], xsb_tm.t[c * 128:(c + 1) * 128, :], xc, reads=[xsb_tm], writes=[xc])
                kb.dma('sp', bc[:], bc_v[:, :, c * 128:(c + 1) * 128], bc, reads=[bc_fm], writes=[bc])
                kb.dma('sp', pd_[:], pre_ssd.t[3072:3136, 2 + c * 128:2 + (c + 1) * 128], pd_, reads=[pre_ssd], writes=[pd_])
                xcs.append(xc); bcs.append(bc); pds.append(pd_)
            xo_ = xop_.get()
            for q in range(5):
                pp = psel.get()
                for b_ in range(4):
                    kb.op('pe', lambda g: g.matmul(pp[:], idsel[:, b_, :], xcs[b_][:, q * 512:(q + 1) * 512], start=(b_ == 0), stop=(b_ == 3)),
                          reads=[idsel, xcs[b_]], writes=[pp])
                evac(xo_[:, q * 512:(q + 1) * 512], pp[:], [pp], [xo_.part(q)])
            kb.dma('pool', xsb_own.t[i * 128:(i + 1) * 128, :], xo_[:], xo_, reads=[xo_], writes=[xsb_own])
            bo_ = bop_.get()
            for q in range(2):
                pp = psel.get()
                for b_ in range(4):
                    kb.op('pe', lambda g: g.matmul(pp[:], idsel[:, b_, :], bcs[b_][:, q * 4:(q + 1) * 4, :].rearrange("p a t -> p (a t)"),
                                                   start=(b_ == 0), stop=(b_ == 3)), reads=[idsel, bcs[b_]], writes=[pp])
                evac(bo_[:, q * 4:(q + 1) * 4, :].rearrange("p a t -> p (a t)"), pp[:], [pp], [bo_.part(q)])
            kb.dma('pool', bco_v[:, :, i * 128:(i + 1) * 128], bo_[:], bo_, reads=[bo_], writes=[bc_own])
            po_ = pop_.get()
            pp = psel.get()
            for b_ in range(4):
                kb.op('pe', lambda g: g.matmul(pp[0:64, 0:128], idsel32[0:64, b_, 0:64], pds[b_][:], start=(b_ == 0), stop=(b_ == 3)),
                      reads=[idsel32, pds[b_]], writes=[pp])
            evac(po_[:], pp[0:64, 0:128], [pp], [po_])
            kb.dma('pool', pdt_own.t[:, i * 128:(i + 1) * 128], po_[:], po_, reads=[po_], writes=[pdt_own])
        kb.barrier()
        kb.release([selb] + xcp_.bufs + bcp_.bufs + pdp_.bufs + xop_.bufs + bop_.bufs + pop_.bufs)

    NCH_D = dbg.get("nch_d", 64) if dbg else 64
    with ExitStack() as st:
        st.enter_context(nc.named_scope("st_D"))
        dtpb = kb.sb(st, "dtpb", [128, 2, 2], F32)
        dsk = kb.sb(st, "dsk", [128, 32], F32)
        msk = kb.sb(st, "msk", [128, 2, 128], F32)
        cmk = kb.sb(st, "cmk", [128, 2, 64], F32)
        esel = kb.sb(st, "esel", [128, 2, 128], F32)
        kb.dma('sp', dtpb[:], dtp[:, :, :], dtpb, reads=[dtp], writes=[dtpb])
        kb.dma('sp', dsk[:], dskb[:, :], dsk, reads=[dskb], writes=[dsk])
        kb.dma('sp', msk[:], masks[:, :, :], msk, reads=[masks], writes=[msk])
        kb.dma('sp', cmk[:], cmask[:, :, :], cmk, reads=[cmask], writes=[cmk])
        kb.dma('sp', esel[:], endsel[:, :, :], esel, reads=[endsel], writes=[esel])
        dmk = kb.sb(st, "dmk", [32, 4, 8], F32)
        kb.dma('sp', dmk[:], dmask[:, :, :], dmk, reads=[dmask], writes=[dmk])
        arowp = Pool(kb, st, "darow", [128, 32, 128], F32, 2)
        aneg = kb.sb(st, "aneg", [128, 2], F32)
        ones = kb.sb(st, "ones", [128, 128], F32)
        kb.op('dve', lambda g: g.memset(ones[:], 1.0), writes=[ones])
        kb.op('act', lambda g: g.activation(out=aneg[:], in_=dtpb[:, :, 1], func=AF.Exp), reads=[dtpb], writes=[aneg])
        kb.op('dve', lambda g: g.tensor_scalar(out=aneg[:], in0=aneg[:], scalar1=-1.0, scalar2=None, op0=ALU.mult),
              reads=[aneg], writes=[aneg])
        Hs = [kb.sb(st, "Hst%d" % d_, [128, 2048], F32) for d_ in range(2)]
        Hbfs = [kb.sb(st, "Hbf%d" % d_, [128, 2048], BF16) for d_ in range(2)]
        xsbp = Pool(kb, st, "dxsb", [128, 2560], BF16, 6)
        bcp = Pool(kb, st, "dbc", [128, 8, 128], BF16, 2)
        pdtp = Pool(kb, st, "dpdt", [128, 128], F32, 6)
        smallp = Pool(kb, st, "dsm", [128, 128], F32, 18)
        T4p = Pool(kb, st, "dT4", [128, 128], F32, 3)
        tm4p = Pool(kb, st, "dtm4", [128, 128], F32, 3)
        decp = Pool(kb, st, "ddec", [128, 32], F32, 3)
        xdtp = Pool(kb, st, "dxdt", [128, 2048], BF16, 2)
        xwp = Pool(kb, st, "dxw", [128, 2048], BF16, 3)
        xsdp = Pool(kb, st, "dxsd", [128, 2048], BF16, 2)
        identbf = kb.sb(st, "identbfD", [128, 128], BF16)
        kb.op('dve', lambda g: g.tensor_copy(out=identbf[:], in_=identb[:]), reads=[identb], writes=[identbf])
        cbmp = Pool(kb, st, "dcbm", [128, 4, 128], F32, 2)
        Ep = Pool(kb, st, "dE", [128, 8, 128], F32, 2)
        MTp = Pool(kb, st, "dMT", [128, 8, 128], BF16, 2)
        ychp = Pool(kb, st, "dych", [128, 2048], F32, 2)
        yprp = Pool(kb, st, "dypr", [128, 2048], F32, 2)
        tep = Pool(kb, st, "dte", [128, 512], F32, 3)
        pT = Pool(kb, st, "pT", [128, 512], F32, 2, psum=True)
        PS = {}
        slot = [0]
        Ptile = kb.sb(st, "Ptile", [128, 32], F32)
        dtesp = Pool(kb, st, "ddtes", [128, 32], F32, 3)

        def dt_chain(d, pdt_ap, pdt_buf, mask_ap):
            pdt = pdtp.get()
            for r in range(4):
                kb.dma('sp', pdt[r * 32:(r + 1) * 32, :], pdt_ap, pdt, reads=[pdt_buf], writes=[pdt])
            yield
            e1 = smallp.get(); dt_ = smallp.get(); dtA = smallp.get(); cum = smallp.get(); ac = smallp.get()
            ex = smallp.get()
            kb.op('act', lambda g: g.activation(out=e1[:], in_=pdt[:], func=AF.Exp, bias=dtpb[:, d, 0:1], scale=1.0),
                  reads=[pdt, dtpb], writes=[e1])
            yield
            kb.op('act', lambda g: g.activation(out=dt_[:], in_=e1[:], func=AF.Ln, bias=1.0, scale=1.0), reads=[e1], writes=[dt_])
            yield
            if mask_ap is not None:
                kb.op('dve', lambda g: g.tensor_scalar(out=dt_[:], in0=dt_[:], scalar1=mask_ap, scalar2=None, op0=ALU.mult),
                      reads=[dt_, cmk], writes=[dt_])
                yield
            kb.op('dve', lambda g: g.tensor_scalar(out=dtA[:], in0=dt_[:], scalar1=aneg[:, d:d + 1], scalar2=None, op0=ALU.mult),
                  reads=[dt_, aneg], writes=[dtA])
            yield
            kb.op('dve', lambda g: g.tensor_tensor_scan(out=cum[:], data0=ones[:], data1=dtA[:], initial=0.0,
                                                        op0=ALU.mult, op1=ALU.add), reads=[ones, dtA], writes=[cum])
            yield
            if d == 0:
                ac = cum
            else:
                kb.op('dve', lambda g: g.scalar_tensor_tensor(out=ac[:], in0=dtA[:], scalar=cum[:, 127:128], in1=cum[:],
                                                              op0=ALU.add, op1=ALU.subtract), reads=[dtA, cum], writes=[ac])
                yield
            kb.op('act', lambda g: g.activation(out=ex[:], in_=ac[:], func=AF.Exp, bias=cum[:, 127:128], scale=-1.0),
                  reads=[ac, cum], writes=[ex])
            yield
            T4 = T4p.get()
            kb.op('act', lambda g: g.copy(out=T4[0:32, :], in_=ac[0:32, :]), reads=[ac], writes=[T4])
            yield
            kb.op('act', lambda g: g.copy(out=T4[32:64, :], in_=dt_[32:64, :]), reads=[dt_], writes=[T4])
            yield
            kb.op('dve', lambda g: g.tensor_tensor(out=T4[64:96, :], in0=dt_[64:96, :], in1=ex[64:96, :], op=ALU.mult),
                  reads=[dt_, ex], writes=[T4])
            yield
            kb.op('act', lambda g: g.activation(out=T4[96:128, :], in_=ac[96:128, :], func=AF.Exp), reads=[ac], writes=[T4])
            yield
            ptt = pT.get()
            kb.op('pe', lambda g: g.transpose(out=ptt[:, 0:128], in_=T4[:], identity=identb[:]), reads=[T4, identb], writes=[ptt])
            yield
            tm4 = tm4p.get()
            kb.op('act', lambda g: g.copy(out=tm4[:], in_=ptt[:, 0:128]), reads=[ptt], writes=[tm4])
            yield
            kb.op('pe', lambda g: g.matmul(ptt[:, 128:160], esel[:, d, :], tm4[:, 0:32], start=True, stop=True),
                  reads=[esel, tm4], writes=[ptt])
            yield
            dec = decp.get()
            kb.op('act', lambda g: g.activation(out=dec[:], in_=ptt[:, 128:160], func=AF.Exp), reads=[ptt], writes=[dec])
            yield
            return ac, tm4, dec

        def ssd_state_chunk(d, pdt_ap, pdt_buf, xsb, c, mask_ap, first, last):
            t0 = c * 128
            xsbt = xsbp.get()
            kb.dma('sp', xsbt[:], xsb.t[t0:t0 + 128, :], xsbt, reads=[xsb], writes=[xsbt])
            yield
            ac, tm4, dec = yield from dt_chain(d, pdt_ap, pdt_buf, mask_ap)
            dtes = dtesp.get()
            kb.op('dve', lambda g: g.tensor_tensor(out=dtes[:], in0=tm4[:, 64:96], in1=Ptile[:], op=ALU.mult), reads=[tm4, Ptile], writes=[dtes])
            yield
            kb.op('dve', lambda g: g.tensor_tensor(out=Ptile[:], in0=Ptile[:], in1=dec[:], op=ALU.mult), reads=[Ptile, dec], writes=[Ptile])
            yield
            xw = xwp.get()
            kb.op('dve', lambda g: g.tensor_tensor(out=xw[:].rearrange("p (h e) -> p h e", h=32),
                                                   in0=xsbt[:, 0:2048].rearrange("p (h e) -> p h e", h=32),
                                                   in1=dtes[:].unsqueeze(2).to_broadcast([128, 32, 64]), op=ALU.mult),
                  reads=[xsbt, dtes], writes=[xw])
            yield
            for g_ in range(4):
                gc = slice(g_ * 512, (g_ + 1) * 512)
                kb.op('pe', lambda g: g.matmul(PS['H'][g_][:], xsbt[:, 2048 + g_ * 128:2048 + (g_ + 1) * 128], xw[:, gc],
                                               start=first, stop=last), reads=[xsbt, xw], writes=[PS['H'][g_]])
                yield

        def run_gen(gen):
            try:
                while True:
                    next(gen)
            except StopIteration as e:
                return e.value

        def run_chains(gens, nactive, stagger):
            pending = list(gens)
            active = []
            steps = 0
            while pending or active:
                if pending and len(active) < nactive and (not active or steps >= stagger):
                    active.append(pending.pop(0))
                    steps = 0
                for gch in list(active):
                    try:
                        next(gch)
                    except StopIteration:
                        active.remove(gch)
                steps += 1

        def pdt_of(pre, d, c):
            return pre.t[3072 + 32 * d:3072 + 32 * d + 32, 2 + c * 128:2 + (c + 1) * 128]

        def ssd_chunk(d, pdt_ap, pdt_buf, xsb, bcf, c, with_y, mask_ap, rmw=False):
            H = Hs[d]
            Hbf = Hbfs[d]
            pCB, pOFF, pDG, pS = PS['CB'], PS['OFF'], PS['DG'], PS['S']
            t0 = c * 128
            xsbt = xsbp.get()
            kb.dma('sp', xsbt[:], xsb.t[t0:t0 + 128, :], xsbt, reads=[xsb], writes=[xsbt])
            ac, tm4, dec = run_gen(dt_chain(d, pdt_ap, pdt_buf, mask_ap))
            xs3 = xsbt[:, 0:2048].rearrange("p (h e) -> p h e", h=32)
            xw = xwp.get()
            kb.op('dve', lambda g: g.tensor_tensor(out=xw[:].rearrange("p (h e) -> p h e", h=32), in0=xs3,
                                                    in1=tm4[:, 64:96].unsqueeze(2).to_broadcast([128, 32, 64]), op=ALU.mult),
                  reads=[xsbt, tm4], writes=[xw])
            if with_y:
                sl = slot[0] % 2
                slot[0] += 1
                kb.dma('pool', ac_scr.t[sl:sl + 1, :].rearrange("o (h t) -> (o h) t", h=32), ac[0:32, :], ac, reads=[ac], writes=[ac_scr])
                arow = arowp.get()
                kb.dma('sp', arow[:].rearrange("p h t -> p (h t)"), ac_scr.t[sl:sl + 1, :].partition_broadcast(128), arow,
                       reads=[ac_scr], writes=[arow])
                kb.op('act', lambda g: g.copy(out=Hbf[:], in_=H[:]), reads=[H], writes=[Hbf])
                bct = bcp.get()
                kb.dma('sp', bct[:], bcf.t.rearrange("(b p) t -> p b t", p=128)[:, :, t0:t0 + 128], bct, reads=[bcf], writes=[bct])
                xdt = xdtp.get()
                kb.op('dve', lambda g: g.tensor_tensor(out=xdt[:].rearrange("p (h e) -> p h e", h=32), in0=xs3,
                                                       in1=tm4[:, 32:64].unsqueeze(2).to_broadcast([128, 32, 64]), op=ALU.mult),
                      reads=[xsbt, tm4], writes=[xdt])
                if d == 0:
                    xsd = xsdp.get()
                    kb.op('dve', lambda g: g.tensor_tensor(out=xsd[:].rearrange("p (h e) -> p h e", h=32), in0=xs3,
                                                           in1=dsk[:].unsqueeze(2).to_broadcast([128, 32, 64]), op=ALU.mult),
                          reads=[xsbt, dsk], writes=[xsd])
                for g_ in range(4):
                    kb.op('pe', lambda g: g.matmul(pCB[:, g_ * 128:(g_ + 1) * 128], bct[:, g_, :], bct[:, 4 + g_, :],
                                                   start=True, stop=True), reads=[bct], writes=[pCB])
                cbm = cbmp.get()
                kb.op('dve', lambda g: g.tensor_tensor(out=cbm[:], in0=pCB[:].rearrange("p (a b) -> p a b", a=4),
                                                       in1=msk[:, d:d + 1, :].to_broadcast([128, 4, 128]), op=ALU.mult),
                      reads=[pCB, msk], writes=[cbm])
                ych = ychp.get()
                if rmw:
                    ypr = yprp.get()
                    kb.dma('sp', ypr[:], y_own.t[t0:t0 + 128, :], ypr, reads=[y_own], writes=[ypr])
            for g_ in range(4):
                gc = slice(g_ * 512, (g_ + 1) * 512)
                if with_y:
                    E = Ep.get()
                    for h in range(8):
                        gh = g_ * 8 + h
                        kb.op('act', lambda g: g.activation(out=E[:, h, :], in_=arow[:, gh, :], func=AF.Relu, bias=tm4[:, gh:gh + 1], scale=-1.0),
                              reads=[arow, tm4], writes=[E])
                    kb.op('act', lambda g: g.activation(out=E[:], in_=E[:], func=AF.Exp, scale=-1.0), reads=[E], writes=[E])
                    MT = MTp.get()
                    kb.op('dve', lambda g: g.tensor_tensor(out=MT[:], in0=E[:], in1=cbm[:, g_:g_ + 1, :].to_broadcast([128, 8, 128]),
                                                           op=ALU.mult), reads=[E, cbm], writes=[MT])
                    po = pOFF.get()
                    kb.op('pe', lambda g: g.matmul(po[:], bct[:, 4 + g_, :], Hbf[:, gc], start=True, stop=True),
                          reads=[bct, Hbf], writes=[po])
                    pd = pDG.get()
                    if d == 0:
                        kb.op('pe', lambda g: g.matmul(pd[:], identbf[:], xsd[:, gc], start=True, stop=False), reads=[identbf, xsd], writes=[pd])
                    for h in range(8):
                        gh = g_ * 8 + h
                        kb.op('pe', lambda g: g.matmul(pd[:, h * 64:(h + 1) * 64], MT[:, h, :], xdt[:, gh * 64:(gh + 1) * 64],
                                                       start=(d == 1), stop=True), reads=[MT, xdt], writes=[pd])
                    te = tep.get()
                    kb.op('dve', lambda g: g.tensor_tensor(out=te[:].rearrange("p (h e) -> p h e", h=8),
                                                           in0=po[:].rearrange("p (h e) -> p h e", h=8),
                                                           in1=tm4[:, 96 + g_ * 8:96 + g_ * 8 + 8].unsqueeze(2).to_broadcast([128, 8, 64]),
                                                           op=ALU.mult), reads=[po, tm4], writes=[te])
                    if not rmw:
                        kb.op('dve', lambda g: g.tensor_tensor(out=ych[:, gc], in0=te[:], in1=pd[:], op=ALU.add), reads=[te, pd], writes=[ych])
                    else:
                        kb.op('dve', lambda g: g.tensor_tensor(out=te[:], in0=te[:], in1=pd[:], op=ALU.add), reads=[te, pd], writes=[te])
                        kb.op('dve', lambda g: g.tensor_tensor(out=ych[:, gc], in0=te[:], in1=ypr[:, gc], op=ALU.add),
                              reads=[te, ypr], writes=[ych])
                psb = pS.get()
                kb.op('pe', lambda g: g.matmul(psb[:], xsbt[:, 2048 + g_ * 128:2048 + (g_ + 1) * 128], xw[:, gc], start=True, stop=True),
                      reads=[xsbt, xw], writes=[psb])
                kb.op('dve', lambda g: g.tensor_tensor(out=H[:, gc].rearrange("p (h e) -> p h e", h=8),
                                                       in0=H[:, gc].rearrange("p (h e) -> p h e", h=8),
                                                       in1=dec[:, g_ * 8:g_ * 8 + 8].unsqueeze(2).to_broadcast([128, 8, 64]),
                                                       op=ALU.mult), reads=[H, dec], writes=[H])
                kb.op('dve', lambda g: g.tensor_tensor(out=H[:, gc], in0=H[:, gc], in1=psb[:], op=ALU.add), reads=[H, psb], writes=[H])
            if with_y:
                kb.dma('pool', y_own.t[t0:t0 + 128, :], ych[:], ych, reads=[ych], writes=[y_own])

        for d in range(2):
            with ExitStack() as st3:
                st3.enter_context(nc.named_scope("st_Dst%d" % d))
                PS['H'] = [kb.ps(st3, "psH%d_%d" % (d, g_), [128, 512], F32) for g_ in range(4)]
                kb.op('dve', lambda g: g.memset(Ptile[:], 1.0), writes=[Ptile])
                if d == 0:
                    seq = [('lat', c) for c in range(NCH_D - 1, -1, -1)] + [('ctx', 1), ('ctx', 0)]
                else:
                    seq = [('lat', c) for c in range(NCH_D)] + [('ctx', 0), ('ctx', 1)]
                gens = []
                for si, (kind, c) in enumerate(seq):
                    if kind == 'lat':
                        gens.append(ssd_state_chunk(d, pdt_of(pre_ssd, d, c), pre_ssd, xsb_tm, c, cmk[:, d, c:c + 1], si == 0, si == len(seq) - 1))
                    else:
                        gens.append(ssd_state_chunk(d, pdt_of(pre_ssd_c, d, c), pre_ssd_c, xsb_tm_c, c, None, si == 0, si == len(seq) - 1))
                run_chains(gens, 3, 8)
                for g_ in range(4):
                    gc = slice(g_ * 512, (g_ + 1) * 512)
                    if g_ % 2 == 0:
                        kb.op('act', lambda g: g.copy(out=Hs[d][:, gc], in_=PS['H'][g_][:]), reads=[PS['H'][g_]], writes=[Hs[d]])
                    else:
                        kb.op('dve', lambda g: g.tensor_copy(out=Hs[d][:, gc], in_=PS['H'][g_][:]), reads=[PS['H'][g_]], writes=[Hs[d]])
                kb.barrier(['pe', 'act', 'dve'])
        with ExitStack() as st3:
            st3.enter_context(nc.named_scope("st_Down"))
            PS['CB'] = kb.ps(st3, "pCB", [128, 512], F32)
            PS['OFF'] = Pool(kb, st3, "pOFF", [128, 512], F32, 1, psum=True)
            PS['DG'] = Pool(kb, st3, "pDG", [128, 512], F32, 2, psum=True)
            PS['S'] = Pool(kb, st3, "pS", [128, 512], F32, 2, psum=True)
            for i in range(16):
                for d in range(2):
                    c = i if d == 0 else 15 - i
                    ssd_chunk(d, pdt_own.t[32 * d:32 * d + 32, c * 128:(c + 1) * 128], pdt_own, xsb_own, bc_own, c, True, None, rmw=(i >= 8))
            kb.barrier(['pe', 'act', 'dve'])
        kb.barrier()
        kb.release(xsbp.bufs + bcp.bufs + pdtp.bufs + smallp.bufs + arowp.bufs + ychp.bufs + yprp.bufs + [dtpb, dsk, msk, cmk, esel])

    NT_E = dbg.get("nt_e", 16) if dbg else 16
    xb_cm = xb.t.rearrange("(r w) d -> w r d", w=64)
    def cm_tile(s_):
        return lambda i: xb_cm[4 * s_ + i]
    jobsE = [(ctxb, rm_tile(ctxb, 0), 256, pre_xr_c, 2, 2, 3)]
    jobsE += [(xb, cm_tile(s_), 512, pre_xr, 2 + s_ * 512, 0, 1) for s_ in range(NT_E)]
    proj_stage("E", OXR, 2048, jobsE, [(pre_xr, SEQ), (pre_xr_c, CTX)])

    NH_F = dbg.get("nh_f", 16) if dbg else 16
    with ExitStack() as st:
        st.enter_context(nc.named_scope("st_F"))
        lcwb = kb.sb(st, "lcwb", [128, 16, 5], F32)
        lgbb = kb.sb(st, "lgbb", [128, 16, 4], F32)
        lamb = kb.sb(st, "lamb", [128, 16, 2], F32)
        selb = kb.sb(st, "selb", [128, 4], F32)
        kb.dma('sp', lcwb[:], lcw[:, :, :], lcwb, reads=[lcw], writes=[lcwb])
        kb.dma('sp', lgbb[:], lgb[:, :, :], lgbb, reads=[lgb], writes=[lgbb])
        kb.dma('sp', lamb[:], llam[:, :, :], lamb, reads=[llam], writes=[lamb])
        kb.dma('sp', selb[:], sel[:, :], selb, reads=[sel], writes=[selb])
        chalf = kb.sb(st, "chalf", [128, 16, 2], F32)
        hbias = kb.sb(st, "hbias", [128, 16, 4], F32)
        kb.op('act', lambda g: g.activation(out=chalf[:], in_=lamb[:], func=AF.Exp, scale=-1.0), reads=[lamb], writes=[chalf])
        kb.op('act', lambda g: g.activation(out=chalf[:], in_=chalf[:], func=AF.Ln, bias=1.0, scale=1.0), reads=[chalf], writes=[chalf])
        kb.op('dve', lambda g: g.tensor_scalar(out=chalf[:], in0=chalf[:], scalar1=-4.0, scalar2=None, op0=ALU.mult),
              reads=[chalf], writes=[chalf])
        kb.op('dve', lambda g: g.tensor_scalar(out=hbias[:], in0=lgbb[:], scalar1=0.5, scalar2=None, op0=ALU.mult),
              reads=[lgbb], writes=[hbias])
        rbuf = kb.sb(st, "rbuf", [128, SEQ], F32)
        abuf = kb.sb(st, "abuf", [128, SEQ], F32)
        ubuf = kb.sb(st, "ubuf", [128, SEQ], F32)
        PF = 1024
        prep_ = Pool(kb, st, "fpre", [128, PF + 3], F32, 3)
        xcvp = Pool(kb, st, "fxcv", [128, PF], F32, 3)
        xcvbp = Pool(kb, st, "fxcvb", [128, PF], BF16, 3)
        afp = Pool(kb, st, "faf", [128, PF], F32, 3)
        ufp = Pool(kb, st, "fuf", [128, PF], F32, 3)
        thap = Pool(kb, st, "ftha", [128, PF], F32, 2)
        thxp = Pool(kb, st, "fthx", [128, PF], F32, 2)
        na2p = Pool(kb, st, "fna2", [128, PF], F32, 2)
        vp = Pool(kb, st, "fv", [128, PF], F32, 2)
        hbtp = Pool(kb, st, "fhbt", [128, PF], F32, 2)
        wgsp = Pool(kb, st, "fwgs", [128, 4, 128], F32, 2)
        wgp = Pool(kb, st, "fwg", [128, 4, 128], BF16, 2)
        fstp = Pool(kb, st, "ffst", [128, 4], F32, 2)
        rop = Pool(kb, st, "fro", [128, 2048], F32, 1)
        pga = Pool(kb, st, "pga", [128, 512], F32, 4, psum=True)
        pgx = Pool(kb, st, "pgx", [128, 512], F32, 4, psum=True)

        def lru_front(hd, pre, t0, P):
            pt_ = prep_.get()
            kb.dma('sp', pt_[:, 0:P + 3], pre.t[hd * 128:(hd + 1) * 128, t0:t0 + P + 3], pt_, reads=[pre], writes=[pt_])
            xcv = xcvp.get()
            kb.op('dve', lambda g: g.tensor_scalar(out=xcv[:, 0:P], in0=pt_[:, 0:P], scalar1=lcwb[:, hd, 0:1], scalar2=lcwb[:, hd, 4:5],
                                                   op0=ALU.mult, op1=ALU.add), reads=[pt_, lcwb], writes=[xcv])
            for k in range(1, 4):
                kb.op('dve', lambda g: g.scalar_tensor_tensor(out=xcv[:, 0:P], in0=pt_[:, k:k + P], scalar=lcwb[:, hd, k:k + 1],
                                                              in1=xcv[:, 0:P], op0=ALU.mult, op1=ALU.add),
                      reads=[pt_, lcwb, xcv], writes=[xcv])
            xcvb = xcvbp.get()
            kb.op('dve', lambda g: g.tensor_copy(out=xcvb[:, 0:P], in_=xcv[:, 0:P]), reads=[xcv], writes=[xcvb])
            return (xcv, xcvb)

        def lru_back(hd, wg, xcv, P, dsts):
            xcv, xcvb = xcv
            sqjobs = []
            for d in range(2):
                tha = thap.get()
                thx = thxp.get()
                for q in range((P + 511) // 512):
                    n = min(512, P - q * 512)
                    cs = slice(q * 512, q * 512 + n)
                    pa = pga.get()
                    px = pgx.get()
                    kb.op('pe', lambda g: g.matmul(pa[:, 0:n], wg[:, 2 * d, :], xcvb[:, cs], start=True, stop=True),
                          reads=[wg, xcvb], writes=[pa])
                    kb.op('pe', lambda g: g.matmul(px[:, 0:n], wg[:, 2 * d + 1, :], xcvb[:, cs], start=True, stop=True),
                          reads=[wg, xcvb], writes=[px])
                    kb.op('act', lambda g: g.activation(out=tha[:, cs], in_=pa[:, 0:n], func=AF.Tanh,
                                                        bias=hbias[:, hd, 2 * d:2 * d + 1], scale=0.5), reads=[pa, hbias], writes=[tha])
                    kb.op('act', lambda g: g.activation(out=thx[:, cs], in_=px[:, 0:n], func=AF.Tanh,
                                                        bias=hbias[:, hd, 2 * d + 1:2 * d + 2], scale=0.5), reads=[px, hbias], writes=[thx])
                a_ap, u_ap, dbufs = dsts[d]
                kb.op('act', lambda g: g.activation(out=a_ap, in_=tha[:, 0:P], func=AF.Exp, bias=chalf[:, hd, d:d + 1],
                                                    scale=chalf[:, hd, d:d + 1]), reads=[tha, chalf], writes=dbufs)
                na2 = na2p.get()
                kb.op('act', lambda g: g.activation(out=na2[:, 0:P], in_=a_ap, func=AF.Square), reads=dbufs, writes=[na2])
                sqjobs.append((na2, thx, u_ap, dbufs))
            for (na2, thx, u_ap, dbufs) in sqjobs:
                kb.op('act', lambda g: g.activation(out=na2[:, 0:P], in_=na2[:, 0:P], func=AF.Sqrt, bias=0.25, scale=-0.25),
                      reads=[na2], writes=[na2])
            for (na2, thx, u_ap, dbufs) in sqjobs:
                v = vp.get()
                kb.op('dve', lambda g: g.scalar_tensor_tensor(out=v[:, 0:P], in0=thx[:, 0:P], scalar=1.0, in1=xcv[:, 0:P],
                                                              op0=ALU.add, op1=ALU.mult), reads=[thx, xcv], writes=[v])
                kb.op('dve', lambda g: g.tensor_tensor(out=u_ap, in0=v[:, 0:P], in1=na2[:, 0:P], op=ALU.mult),
                      reads=[v, na2], writes=dbufs)

        lgw_v = lgw.t
        npc = (NT_E * 512) // PF
        jobs = []
        for hd in range(NH_F):
            jobs.append((hd, 'ctx', 0))
            for p in (range(npc) if hd % 2 == 0 else range(npc - 1, -1, -1)):
                jobs.append((hd, 'lat', p))
        state = {}

        def front(job):
            hd, kind, p = job
            if kind == 'ctx':
                wgs = wgsp.get()
                kb.dma('sp', wgs[:], lgw_v[hd], wgs, reads=[lgw], writes=[wgs])
                wg = wgp.get()
                kb.op('act', lambda g: g.copy(out=wg[:], in_=wgs[:]), reads=[wgs], writes=[wg])
                state[('wg', hd)] = wg
                state[('fst', hd)] = fstp.get()
                return lru_front(hd, pre_xr_c, 0, CTX)
            return lru_front(hd, pre_xr, p * PF, PF)

        def back(job, xcv):
            hd, kind, p = job
            wg = state[('wg', hd)]
            fst = state[('fst', hd)]
            dp = hd % 2
            ds = 1 - dp
            if kind == 'ctx':
                af = afp.get(); uf = ufp.get(); ab = afp.get(); ub = ufp.get()
                lru_back(hd, wg, xcv, CTX, [(af[:, 0:CTX], uf[:, 0:CTX], [af, uf]), (ab[:, 0:CTX], ub[:, 0:CTX], [ab, ub])])
                hb = hbtp.get()
                kb.op('dve', lambda g: g.tensor_tensor_scan(out=hb[:, 0:CTX], data0=af[:, 0:CTX], data1=uf[:, 0:CTX], initial=0.0,
                                                            op0=ALU.mult, op1=ALU.add), reads=[af, uf], writes=[hb])
                kb.op('act', lambda g: g.copy(out=fst[:, 0:1], in_=hb[:, CTX - 1:CTX]), reads=[hb], writes=[fst])
                hb2 = hbtp.get()
                kb.op('dve', lambda g: g.tensor_tensor_scan(out=hb2[:, 0:CTX][:, ::-1], data0=ab[:, 0:CTX][:, ::-1],
                                                            data1=ub[:, 0:CTX][:, ::-1], initial=0.0, op0=ALU.mult, op1=ALU.add),
                      reads=[ab, ub], writes=[hb2])
                kb.op('act', lambda g: g.copy(out=fst[:, 1:2], in_=hb2[:, 0:1]), reads=[hb2], writes=[fst])
                return
            cs = slice(p * PF, (p + 1) * PF)
            af = afp.get(); uf = ufp.get()
            dsts = [None, None]
            dsts[dp] = (af[:], uf[:], [af, uf])
            dsts[ds] = (abuf[:, cs], ubuf[:, cs], [abuf.part(p), ubuf.part(p)])
            lru_back(hd, wg, xcv, PF, dsts)
            if dp == 0:
                init = fst[:, 0:1] if p == 0 else rbuf[:, p * PF - 1:p * PF]
                rd = [af, uf, fst] + ([rbuf.part(p - 1)] if p > 0 else [])
                kb.op('dve', lambda g: g.tensor_tensor_scan(out=rbuf[:, cs], data0=af[:], data1=uf[:], initial=init,
                                                            op0=ALU.mult, op1=ALU.add), reads=rd, writes=[rbuf.part(p)])
                last = (p == npc - 1)
            else:
                init = fst[:, 1:2] if p == npc - 1 else rbuf[:, (p + 1) * PF:(p + 1) * PF + 1]
                rd = [af, uf, fst] + ([rbuf.part(p + 1)] if p < npc - 1 else [])
                kb.op('dve', lambda g: g.tensor_tensor_scan(out=rbuf[:, cs][:, ::-1], data0=af[:][:, ::-1], data1=uf[:][:, ::-1],
                                                            initial=init, op0=ALU.mult, op1=ALU.add), reads=rd, writes=[rbuf.part(p)])
                last = (p == 0)
            if last:
                prev = None
                order2 = list(range(npc - 1, -1, -1)) if ds == 1 else list(range(npc))
                for p2 in order2:
                    cs2 = slice(p2 * PF, (p2 + 1) * PF)
                    hb = hbtp.get()
                    rd = [abuf.part(p2), ubuf.part(p2), fst] + ([prev] if prev is not None else [])
                    if ds == 1:
                        init = fst[:, 1:2] if prev is None else prev[:, 0:1]
                        kb.op('dve', lambda g: g.tensor_tensor_scan(out=hb[:][:, ::-1], data0=abuf[:, cs2][:, ::-1], data1=ubuf[:, cs2][:, ::-1],
                                                                    initial=init, op0=ALU.mult, op1=ALU.add), reads=rd, writes=[hb])
                    else:
                        init = fst[:, 0:1] if prev is None else prev[:, PF - 1:PF]
                        kb.op('dve', lambda g: g.tensor_tensor_scan(out=hb[:], data0=abuf[:, cs2], data1=ubuf[:, cs2],
                                                                    initial=init, op0=ALU.mult, op1=ALU.add), reads=rd, writes=[hb])
                    kb.op('dve', lambda g: g.tensor_tensor(out=rbuf[:, cs2], in0=rbuf[:, cs2], in1=hb[:], op=ALU.add),
                          reads=[rbuf.part(p2), hb], writes=[rbuf.part(p2)])
                    prev = hb
                ro = rop.get()
                r4 = rbuf[:].rearrange("p (j b i) -> p j b i", j=64, b=4)
                rov = ro[:].rearrange("p (i j) -> p j i", j=64)
                kb.op('dve', lambda g: g.tensor_scalar(out=rov, in0=r4[:, :, 0, :], scalar1=selb[:, 0:1], scalar2=None, op0=ALU.mult),
                      reads=[rbuf, selb], writes=[ro])
                for b_ in range(1, 4):
                    kb.op('dve', lambda g: g.scalar_tensor_tensor(out=rov, in0=r4[:, :, b_, :], scalar=selb[:, b_:b_ + 1], in1=rov,
                                                                  op0=ALU.mult, op1=ALU.add), reads=[rbuf, selb, ro], writes=[ro])
                kb.dma('pool', r_own.t[hd * 128:(hd + 1) * 128, :], ro[:], ro, reads=[ro], writes=[r_own])

        cur = front(jobs[0])
        for ji in range(len(jobs)):
            nxt = front(jobs[ji + 1]) if ji + 1 < len(jobs) else None
            back(jobs[ji], cur)
            cur = nxt
        kb.barrier()
        kb.release(prep_.bufs + wgsp.bufs + rop.bufs + [lcwb, lgbb, lamb, selb])

    def fm_view(scr):
        return scr.t.rearrange("(kc p) t -> p kc t", p=128)

    with ExitStack() as st:
        st.enter_context(nc.named_scope("st_G"))
        pools = norm_pools(st, "G", nxn=2, npt=2, nxt=4)
        hTo = kb.sb(st, "hTo", [128, NKC, 2048], BF16)
        build_hT(pools, xo, hTo, 16, 0, 1)
        wstg = Pool(kb, st, "Gwstg", [128, NKC, 128], F32, 3)
        pg = Pool(kb, st, "Gpg", [128, 512], F32, 4, psum=True)
        wzp = Pool(kb, st, "Gwz", [128, NKC, 512], BF16, 2)
        szp = Pool(kb, st, "Gsz", [128, 512], F32, 3)
        for cg in range(4):
            wz = wzp.get()
            load_weight(wstg, w_in, OZ + cg * 512, 512, wz, 0)
            for tt in range(16):
                pp = pg.get()
                for kc in range(NKC):
                    kb.op('pe', lambda g: g.matmul(pp[:], hTo[:, kc, tt * 128:(tt + 1) * 128], wz[:, kc, :],
                                                   start=(kc == 0), stop=(kc == NKC - 1)), reads=[hTo.part(kc), wz], writes=[pp])
                sz = szp.get()
                kb.op('act', lambda g: g.activation(out=sz[:], in_=pp[:], func=AF.Silu), reads=[pp], writes=[sz])
                kb.dma('pool', sz_scr.t[tt * 128:(tt + 1) * 128, cg * 512:(cg + 1) * 512], sz[:], sz, reads=[sz], writes=[sz_scr])
        bgb = kb.sb(st, "bgb", [128, 32], F32)
        kb.dma('sp', bgb[:], bgate[:, :], bgb, reads=[bgate], writes=[bgb])
        wbp = Pool(kb, st, "Gwb", [128, NKC, 128], BF16, 2)
        gop = Pool(kb, st, "Ggo", [128, 512], BF16, 3)
        for cb in range(32):
            wb = wbp.get()
            load_weight(wstg, w_gate, cb * 128, 128, wb, 0)
            for tg in range(4):
                pp = pg.get()
                for kc in range(NKC):
                    kb.op('pe', lambda g: g.matmul(pp[:], wb[:, kc, :], hTo[:, kc, tg * 512:(tg + 1) * 512],
                                                   start=(kc == 0), stop=(kc == NKC - 1)), reads=[hTo.part(kc), wb], writes=[pp])
                go = gop.get()
                kb.op('act', lambda g: g.activation(out=go[:], in_=pp[:], func=AF.Sigmoid, bias=bgb[:, cb:cb + 1], scale=1.0),
                      reads=[pp, bgb], writes=[go])
                kb.dma('pool', gT_scr.t[cb * 128:(cb + 1) * 128, tg * 512:(tg + 1) * 512], go[:], go, reads=[go], writes=[gT_scr])
        rlp = Pool(kb, st, "Grl", [128, 512], F32, 2)
        t1p = Pool(kb, st, "Gt1", [128, 512], F32, 2)
        t2p = Pool(kb, st, "Gt2", [128, 512], F32, 2)
        for cb in range(16):
            wb = wbp.get()
            load_weight(wstg, w_in, OYR + cb * 128, 128, wb, 0)
            for tg in range(4):
                ts_ = slice(tg * 512, (tg + 1) * 512)
                pp = pg.get()
                for kc in range(NKC):
                    kb.op('pe', lambda g: g.matmul(pp[:], wb[:, kc, :], hTo[:, kc, ts_],
                                                   start=(kc == 0), stop=(kc == NKC - 1)), reads=[hTo.part(kc), wb], writes=[pp])
                rl = rlp.get()
                kb.dma('sp', rl[:], r_own.t[cb * 128:(cb + 1) * 128, ts_], rl, reads=[r_own], writes=[rl])
                t1 = t1p.get()
                t2 = t2p.get()
                kb.op('act', lambda g: g.activation(out=t1[:], in_=pp[:], func=AF.Square), reads=[pp], writes=[t1])
                kb.op('dve', lambda g: g.tensor_scalar(out=t1[:], in0=t1[:], scalar1=0.044715, scalar2=1.0, op0=ALU.mult, op1=ALU.add),
                      reads=[t1], writes=[t1])
                kb.op('dve', lambda g: g.tensor_tensor(out=t1[:], in0=t1[:], in1=pp[:], op=ALU.mult), reads=[t1, pp], writes=[t1])
                kb.op('act', lambda g: g.activation(out=t1[:], in_=t1[:], func=AF.Tanh, scale=0.7978845608028654), reads=[t1], writes=[t1])
                kb.op('dve', lambda g: g.scalar_tensor_tensor(out=t2[:], in0=t1[:], scalar=1.0, in1=pp[:], op0=ALU.add, op1=ALU.mult),
                      reads=[t1, pp], writes=[t2])
                go = gop.get()
                kb.op('dve', lambda g: g.scalar_tensor_tensor(out=go[:], in0=t2[:], scalar=0.5, in1=rl[:], op0=ALU.mult, op1=ALU.mult),
                      reads=[t2, rl], writes=[go])
                kb.dma('pool', rgT_scr.t[cb * 128:(cb + 1) * 128, ts_], go[:], go, reads=[go], writes=[rgT_scr])
        kb.barrier()
        kb.release(pools['xt'].bufs + wstg.bufs + szp.bufs + gop.bufs + rlp.bufs + [bgb])

    with ExitStack() as st:
        st.enter_context(nc.named_scope("st_G1b"))
        pools = norm_pools(st, "N")
        selb = kb.sb(st, "Nsel", [128, 4], F32)
        kb.dma('sp', selb[:], sel[:, :], selb, reads=[sel], writes=[selb])
        yop = Pool(kb, st, "Nyo", [128, D], F32, 2)
        ynp = Pool(kb, st, "Nyn", [128, NKC, 128], BF16, 2)
        for tt in range(16):
            szt = pools['xt'].get()
            kb.dma('sp', szt[:], sz_scr.t[tt * 128:(tt + 1) * 128, :], szt, reads=[sz_scr], writes=[szt])
            yo = yop.get()
            kb.dma('sp', yo[:], y_own.t[tt * 128:(tt + 1) * 128, :], yo, reads=[y_own], writes=[yo])
            kb.op('dve', lambda g: g.tensor_tensor(out=yo[:], in0=yo[:], in1=szt[:], op=ALU.mult), reads=[yo, szt], writes=[yo])
            yn = ynp.get()
            norm_T(pools, yo, yn, 0, lambda kc: nrmb[:, 2, kc:kc + 1], None, [nrmb])
            kb.dma('pool', fm_view(ynT_scr)[:, :, tt * 128:(tt + 1) * 128], yn[:], yn, reads=[yn], writes=[ynT_scr])
        kb.barrier()
        kb.release(pools['xt'].bufs + yop.bufs + ynp.bufs + [selb])

    with ExitStack() as st:
        st.enter_context(nc.named_scope("st_H"))
        ynT = kb.sb(st, "HynT", [128, NKC, 2048], BF16)
        rgT = kb.sb(st, "HrgT", [128, NKC, 2048], BF16)
        for kc in range(NKC):
            kb.dma('sp', ynT[:, kc, :], ynT_scr.t[kc * 128:(kc + 1) * 128, :], ynT, reads=[ynT_scr], writes=[ynT])
            kb.dma('sp', rgT[:, kc, :], rgT_scr.t[kc * 128:(kc + 1) * 128, :], rgT, reads=[rgT_scr], writes=[rgT])
        wstg = Pool(kb, st, "Hwstg", [128, NKC, 128], F32, 3)
        wbp = Pool(kb, st, "Hwb", [128, NKC, 128], BF16, 4)
        pgs = Pool(kb, st, "Hpgs", [128, 512], F32, 3, psum=True)
        pgr = Pool(kb, st, "Hpgr", [128, 512], F32, 3, psum=True)
        gsp = Pool(kb, st, "Hgs", [128, 512], BF16, 2)
        grp = Pool(kb, st, "Hgr", [128, 512], BF16, 2)
        m1p = Pool(kb, st, "Hm1", [128, 512], F32, 2)
        m2p = Pool(kb, st, "Hm2", [128, 512], F32, 2)
        mop = Pool(kb, st, "Hmo", [128, 512], BF16, 3)
        for cb in range(16):
            ws = wbp.get()
            load_weight(wstg, w_out_ssd, cb * 128, 128, ws, 0)
            wr = wbp.get()
            load_weight(wstg, w_out_lru, cb * 128, 128, wr, 0)
            for tg in range(4):
                ts_ = slice(tg * 512, (tg + 1) * 512)
                p1 = pgs.get()
                p2 = pgr.get()
                for kc in range(NKC):
                    kb.op('pe', lambda g: g.matmul(p1[:], ws[:, kc, :], ynT[:, kc, ts_], start=(kc == 0), stop=(kc == NKC - 1)),
                          reads=[ws, ynT], writes=[p1])
                for kc in range(NKC):
                    kb.op('pe', lambda g: g.matmul(p2[:], wr[:, kc, :], rgT[:, kc, ts_], start=(kc == 0), stop=(kc == NKC - 1)),
                          reads=[wr, rgT], writes=[p2])
                gs_ = gsp.get()
                gr_ = grp.get()
                kb.dma('sp', gs_[:], gT_scr.t[cb * 128:(cb + 1) * 128, ts_], gs_, reads=[gT_scr], writes=[gs_])
                kb.dma('sp', gr_[:], gT_scr.t[D + cb * 128:D + (cb + 1) * 128, ts_], gr_, reads=[gT_scr], writes=[gr_])
                m1 = m1p.get()
                m2 = m2p.get()
                kb.op('dve', lambda g: g.tensor_tensor(out=m1[:], in0=p1[:], in1=gs_[:], op=ALU.mult), reads=[p1, gs_], writes=[m1])
                kb.op('dve', lambda g: g.tensor_tensor(out=m2[:], in0=p2[:], in1=gr_[:], op=ALU.mult), reads=[p2, gr_], writes=[m2])
                mo = mop.get()
                kb.op('dve', lambda g: g.tensor_tensor(out=mo[:], in0=m1[:], in1=m2[:], op=ALU.add), reads=[m1, m2], writes=[mo])
                kb.dma('pool', mT_scr.t[cb * 128:(cb + 1) * 128, ts_], mo[:], mo, reads=[mo], writes=[mT_scr])
        kb.barrier()
        kb.release([ynT, rgT] + wstg.bufs + gsp.bufs + grp.bufs + mop.bufs)

    def gate_bcast(st, tag):
        gmb = kb.sb(st, tag + "gmb", [128, 2, D], F32)
        for a_ in range(2):
            kb.dma('sp', gmb[:, a_, :], gvec.t[a_:a_ + 1, :].partition_broadcast(128), gmb, reads=[gvec], writes=[gmb])
        return gmb

    def tm_gemm_residual(tag, AT, nkc, w_dram, res_src, gidx, dst, tok0, ntt, wstg, cwid=512):
        with ExitStack() as st2:
            st2.enter_context(nc.named_scope("st_" + tag))
            gmb = gate_bcast(st2, tag)
            wsp = Pool(kb, st2, tag + "ws", [128, nkc, cwid], BF16, 2)
            pgp = Pool(kb, st2, tag + "pg", [128, 512], F32, 4, psum=True)
            xrp = Pool(kb, st2, tag + "xr", [128, 512], F32, 3)
            xop = Pool(kb, st2, tag + "xo", [128, 512], F32, 3)
            for cg in range(D // cwid):
                cs = slice(cg * cwid, (cg + 1) * cwid)
                wsl = wsp.get()
                k0 = 0
                while k0 < nkc:
                    nk = min(16, nkc - k0)
                    load_weight(wstg, w_dram, cg * cwid, cwid, wsl, 0, row_lo=k0 * 128, nkc=nk, kc_dst=k0)
                    k0 += nk
                for tt in range(ntt):
                    pp = pgp.get()
                    for kc in range(nkc):
                        kb.op('pe', lambda g: g.matmul(pp[:, 0:cwid], AT[:, kc, tt * 128:(tt + 1) * 128], wsl[:, kc, :],
                                                       start=(kc == 0), stop=(kc == nkc - 1)), reads=[AT, wsl], writes=[pp])
                    xr_ = xrp.get()
                    r0 = tok0 + tt * 128
                    kb.dma('sp', xr_[:, 0:cwid], res_src.t[r0:r0 + 128, cs], xr_, reads=[res_src], writes=[xr_])
                    xo_ = xop.get()
                    kb.op('dve', lambda g: g.tensor_tensor(out=xo_[:, 0:cwid], in0=pp[:, 0:cwid], in1=gmb[:, gidx, cs], op=ALU.mult),
                          reads=[pp, gmb], writes=[xo_])
                    kb.op('dve', lambda g: g.tensor_tensor(out=xo_[:, 0:cwid], in0=xo_[:, 0:cwid], in1=xr_[:, 0:cwid], op=ALU.add),
                          reads=[xo_, xr_], writes=[xo_])
                    kb.dma('pool', dst.t[r0:r0 + 128, cs], xo_[:, 0:cwid], xo_, reads=[xo_], writes=[dst])
            kb.barrier()
            kb.release([gmb] + xrp.bufs + xop.bufs)

    with ExitStack() as st:
        mT = kb.sb(st, "HmT", [128, NKC, 2048], BF16)
        for kc in range(NKC):
            kb.dma('sp', mT[:, kc, :], mT_scr.t[kc * 128:(kc + 1) * 128, :], mT, reads=[mT_scr], writes=[mT])
        wstg = Pool(kb, st, "H3wstg", [128, NKC, 128], F32, 3)
        tm_gemm_residual("H3", mT, NKC, w_o, xo, 0, x1_scr, 0, 16, wstg)
        kb.release([mT] + wstg.bufs)

    with ExitStack() as st:
        st.enter_context(nc.named_scope("st_I"))
        pools = norm_pools(st, "I", nxn=2, npt=2, nxt=4)
        h2T = kb.sb(st, "h2T", [128, NKC, 2048], BF16)
        build_hT(pools, x1_scr, h2T, 16, 4, 5)
        wstg = Pool(kb, st, "Iwstg", [128, NKC, 128], F32, 3)
        wbp = Pool(kb, st, "Iwb", [128, NKC, 128], BF16, 4)
        pgg = Pool(kb, st, "Ipgg", [128, 512], F32, 3, psum=True)
        pgu = Pool(kb, st, "Ipgu", [128, 512], F32, 3, psum=True)
        sgp = Pool(kb, st, "Isg", [128, 512], F32, 3)
        aop = Pool(kb, st, "Iao", [128, 512], BF16, 3)
        for cb in range(FFN // 128):
            wg_ = wbp.get()
            load_weight(wstg, w13, cb * 128, 128, wg_, 0)
            wu_ = wbp.get()
            load_weight(wstg, w13, FFN + cb * 128, 128, wu_, 0)
            for tg in range(4):
                ts_ = slice(tg * 512, (tg + 1) * 512)
                p1 = pgg.get()
                p2 = pgu.get()
                for kc in range(NKC):
                    kb.op('pe', lambda g: g.matmul(p1[:], wg_[:, kc, :], h2T[:, kc, ts_], start=(kc == 0), stop=(kc == NKC - 1)),
                          reads=[wg_, h2T.part(kc)], writes=[p1])
                for kc in range(NKC):
                    kb.op('pe', lambda g: g.matmul(p2[:], wu_[:, kc, :], h2T[:, kc, ts_], start=(kc == 0), stop=(kc == NKC - 1)),
                          reads=[wu_, h2T.part(kc)], writes=[p2])
                sg = sgp.get()
                kb.op('act', lambda g: g.activation(out=sg[:], in_=p1[:], func=AF.Silu), reads=[p1], writes=[sg])
                ao = aop.get()
                kb.op('dve', lambda g: g.tensor_tensor(out=ao[:], in0=p2[:], in1=sg[:], op=ALU.mult), reads=[p2, sg], writes=[ao])
                kb.dma('pool', aT_scr.t[cb * 128:(cb + 1) * 128, ts_], ao[:], ao, reads=[ao], writes=[aT_scr])
        kb.barrier()
        kb.release(pools['xt'].bufs + wstg.bufs + aop.bufs)

    NKF = FFN // 128
    for half in range(2):
        with ExitStack() as st:
            aT = kb.sb(st, "IaT%d" % half, [128, NKF, 1024], BF16)
            for kc in range(NKF):
                kb.dma('sp', aT[:, kc, :], aT_scr.t[kc * 128:(kc + 1) * 128, half * 1024:(half + 1) * 1024], aT, reads=[aT_scr], writes=[aT])
            wstg = Pool(kb, st, "I2wstg%d" % half, [128, NKC, 128], F32, 3)
            tm_gemm_residual("I2%d" % half, aT, NKF, w2, x1_scr, 1, x2_scr, half * 1024, 8, wstg, cwid=256)
            kb.release([aT] + wstg.bufs)

    with ExitStack() as st:
        st.enter_context(nc.named_scope("st_J"))
        fnb = kb.sb(st, "fnb", [128, D], F32)
        kb.dma('sp', fnb[:], fin_norm.t.partition_broadcast(128), fnb, reads=[fin_norm], writes=[fnb])
        xtp = Pool(kb, st, "Jxt", [128, D], F32, 3)
        sqp = Pool(kb, st, "Jsq", [128, D], BF16, 1)
        ssp = Pool(kb, st, "Jss", [128, 4], F32, 2)
        otp = Pool(kb, st, "Jot", [128, D], F32, 3)
        for tt in range(16):
            xt = xtp.get()
            kb.dma('sp', xt[:], x2_scr.t[tt * 128:(tt + 1) * 128, :], xt, reads=[x2_scr], writes=[xt])
            sq = sqp.get()
            ss = ssp.get()
            kb.op('dve', lambda g: g.memset(ss[:], 0.0), writes=[ss])
            kb.op('act', lambda g: g.activation(out=sq[:], in_=xt[:], func=AF.Square, accum_out=ss[:, 0:1]), reads=[xt], writes=[sq, ss])
            kb.op('dve', lambda g: g.tensor_scalar(out=ss[:, 1:2], in0=ss[:, 0:1], scalar1=1.0 / D, scalar2=EPS,
                                                   op0=ALU.mult, op1=ALU.add), reads=[ss], writes=[ss])
            kb.op('act', lambda g: g.activation(out=ss[:, 3:4], in_=ss[:, 1:2], func=AF.Sqrt), reads=[ss], writes=[ss])
            kb.op('dve', lambda g: g.reciprocal(out=ss[:, 2:3], in_=ss[:, 3:4]), reads=[ss], writes=[ss])
            ot = otp.get()
            kb.op('dve', lambda g: g.scalar_tensor_tensor(out=ot[:], in0=xt[:], scalar=ss[:, 2:3], in1=fnb[:], op0=ALU.mult, op1=ALU.mult),
                  reads=[xt, ss, fnb], writes=[ot])
            kb.dma('pool', out.t[tt * 128:(tt + 1) * 128, :], ot[:], ot, reads=[ot], writes=[out])
        kb.barrier()
        kb.release([fnb] + xtp.bufs + otp.bufs)

    kb.barrier()
    gs.close()
    return nc


def _fm(v, n=16):
    return np.ascontiguousarray(np.asarray(v, np.float32).reshape(n, 128).T)


def _endsel():
    e = np.zeros((128, 2, 128), np.float32)
    e[127, 0, :] = 1.0
    e[0, 1, :] = 1.0
    return e


def prep_core(inp, core):
    b, k = core // 4, core % 4
    f = lambda a: np.ascontiguousarray(np.asarray(a, np.float32))
    m = {
        "xb": f(inp["x"][b]), "xo": f(inp["x"][b, 2048 * k:2048 * (k + 1)]), "ctxb": f(inp["ctx"][b]),
        "cvec": f(np.stack([_fm(inp["c"][b]), _fm(inp["c_ctx"])], axis=-1)),
        "w_ada": f(inp["w_ada"][0]), "b_ada": _fm(inp["b_ada"][0], 96),
        "nrm": f(np.stack([_fm(inp["norm_mix"][0]), _fm(inp["norm_ffn"][0]), _fm(inp["ssd_norm"][0])], axis=1)),
        "fin_norm": f(inp["final_norm"][None, :]), "ident": np.eye(128, dtype=np.float32), "w_in": f(inp["w_in"][0]),
        "cw_ssd": f(np.concatenate([inp["ssd_conv_w"][0], inp["ssd_conv_b"][0][None]], 0).reshape(5, 24, 128).transpose(2, 1, 0)),
        "dtp": f(np.tile(np.stack([inp["ssd_dt_bias"][0], inp["ssd_a_log"][0]], -1).transpose(1, 0, 2), (4, 1, 1))),
        "dskb": f(np.tile(inp["ssd_d"][0][None, :], (128, 1))),
        "masks": f(np.stack([np.triu(np.ones((128, 128), np.float32)), np.tril(np.ones((128, 128), np.float32))], 1)),
        "sel": f(np.tile(np.eye(4, dtype=np.float32)[k][None, :], (128, 1))),
        "cmask": f(np.tile(np.stack([(np.arange(64) < 16 * k), (np.arange(64) >= 16 * (k + 1))], 0).astype(np.float32)[None], (128, 1, 1))),
        "endsel": _endsel(),
        "dmask": np.eye(32, dtype=np.float32).reshape(32, 4, 8),
        "lcw": f(np.concatenate([inp["lru_conv_w"][0], inp["lru_conv_b"][0][None]], 0).reshape(5, 16, 128).transpose(2, 1, 0)),
        "lgw": f(np.stack([inp["lru_w_a"][0][0], inp["lru_w_x"][0][0], inp["lru_w_a"][0][1], inp["lru_w_x"][0][1]], axis=2)),
        "lgb": f(np.stack([inp["lru_b_a"][0][0], inp["lru_b_x"][0][0], inp["lru_b_a"][0][1], inp["lru_b_x"][0][1]], axis=-1)
                 .reshape(16, 128, 4).transpose(1, 0, 2)),
        "llam": f(np.asarray(inp["lru_lambda"][0]).T.reshape(16, 128, 2).transpose(1, 0, 2)),
        "w_out_ssd": f(inp["w_out_ssd"][0]), "w_out_lru": f(inp["w_out_lru"][0]), "w_gate": f(inp["w_gate"][0]),
        "bgate": _fm(inp["b_gate"][0], 32), "w_o": f(inp["w_o"][0]), "w13": f(inp["ffn_w13"][0]), "w2": f(inp["ffn_w2"][0]),
    }
    return m


def kernel(**inputs):
    inp = {k_: np.asarray(v) for k_, v in inputs.items()}
    nc = build_program(None)
    in_maps = [prep_core(inp, c) for c in range(8)]
    res = run_bass_kernel_spmd(nc, in_maps, core_ids=list(range(8)))
    out = np.zeros((2, SEQ, D), np.float32)
    for c in range(8):
        b, k = c // 4, c % 4
        out[b, 2048 * k:2048 * (k + 1)] = res.results[c]["out"]
    return out
```

```python
import numpy as np
from contextlib import ExitStack
import concourse.bass as bass
import concourse.mybir as mybir
from concourse.bass_utils import run_bass_kernel_spmd

F32 = mybir.dt.float32
BF16 = mybir.dt.bfloat16
AF = mybir.ActivationFunctionType
ALU = mybir.AluOpType
AX = mybir.AxisListType

D = 2048
SEQ = 8192
CTX = 256
NKC = 16
FFN = 5632
EPS = 1e-6
OZ, OXS, OB, OC, ODT, OXR, OYR = 0, 2048, 4096, 4608, 5120, 5184, 7232


class Buf:
    def __init__(self, t, parent=None):
        self.t = t
        self.w = {}
        self.r = {}
        self.dsem = None
        self.dcnt = 0
        self.parent = parent
        self.kids = {}

    def __getitem__(self, k):
        return self.t[k]

    def part(self, key):
        if key not in self.kids:
            self.kids[key] = Buf(self.t, parent=self)
        return self.kids[key]

    def parts(self, keys):
        return [self.part(k) for k in keys]

    def wsets(self):
        out = [self.w]
        if self.parent is not None:
            out.append(self.parent.w)
        out.extend(k.w for k in self.kids.values())
        return out

    def rsets(self):
        out = [self.r]
        if self.parent is not None:
            out.append(self.parent.r)
        out.extend(k.r for k in self.kids.values())
        return out


class KB:
    def __init__(self, nc, stack):
        self.nc = nc
        self.gstack = stack
        self.eng = {'pe': nc.tensor, 'act': nc.scalar, 'dve': nc.vector, 'pool': nc.gpsimd, 'sp': nc.sync}
        self.sem = {}
        self.cnt = {}
        self.allsems = {}
        for e in ['pe', 'act', 'dve', 'pool']:
            self.sem[e] = stack.enter_context(nc.semaphore("s_" + e))
            self.cnt[e] = 0
        self.waited = {e: {} for e in self.eng}
        self.free_dsems = []
        self.nd = 0
        self.rr = 0

    def sb(self, st, name, shape, dt):
        return Buf(st.enter_context(self.nc.sbuf_tensor(name, list(shape), dt)))

    def ps(self, st, name, shape, dt=F32):
        return Buf(st.enter_context(self.nc.psum_tensor(name, list(shape), dt)))

    def _dsem(self, b):
        if b.dsem is None:
            if self.free_dsems:
                b.dsem, b.dcnt = self.free_dsems.pop()
            else:
                self.nd += 1
                b.dsem = self.gstack.enter_context(self.nc.semaphore("d%d" % self.nd))
                b.dcnt = 0
        return b.dsem

    def release(self, bufs):
        for b in bufs:
            if b.dsem is not None:
                self.free_dsems.append((b.dsem, b.dcnt))
                b.dsem = None

    def _deps(self, e, reads, writes):
        deps = {}
        for b in reads:
            for ws in b.wsets():
                for s, v in ws.items():
                    if deps.get(s, 0) < v:
                        deps[s] = v
        for b in writes:
            for ws in b.wsets() + b.rsets():
                for s, v in ws.items():
                    if deps.get(s, 0) < v:
                        deps[s] = v
        own = self.sem.get(e)
        wd = self.waited[e]
        for s, v in deps.items():
            if e == 'pe' and s is own:
                continue
            if wd.get(s, 0) < v:
                self.eng[e].wait_ge(s, v)
                wd[s] = v

    def op(self, e, fn, reads=(), writes=()):
        self._deps(e, reads, writes)
        ins = fn(self.eng[e])
        self.cnt[e] += 1
        s = self.sem[e]
        ins.then_inc(s, 1)
        v = self.cnt[e]
        self.allsems[s] = v
        for b in reads:
            b.r[s] = v
        for b in writes:
            b.w[s] = v
        return ins

    def dma(self, q, out, in_, sembuf, reads=(), writes=(), **kw):
        self._deps(q, reads, writes)
        ins = self.eng[q].dma_start(out=out, in_=in_, **kw)
        s = self._dsem(sembuf)
        sembuf.dcnt += 16
        ins.then_inc(s, 16)
        v = sembuf.dcnt
        self.allsems[s] = v
        for b in reads:
            b.r[s] = v
        for b in writes:
            b.w[s] = v
        return ins

    def barrier(self, engines=None):
        for e in (engines or list(self.eng)):
            wd = self.waited[e]
            for s, v in self.allsems.items():
                if wd.get(s, 0) < v:
                    self.eng[e].wait_ge(s, v)
                    wd[s] = v

    def alt(self, choices=('act', 'dve')):
        self.rr += 1
        return choices[self.rr % len(choices)]


class Pool:
    def __init__(self, kb, st, name, shape, dt, n, psum=False):
        mk = kb.ps if psum else kb.sb
        self.bufs = [mk(st, "%s%d" % (name, i), shape, dt) for i in range(n)]
        self.i = 0

    def get(self):
        b = self.bufs[self.i % len(self.bufs)]
        self.i += 1
        return b


def evac_affine(kb, e, out_ap, in_ap, scale_ap, bias_ap, reads, writes):
    if e == 'act':
        kb.op('act', lambda g: g.activation(out=out_ap, in_=in_ap, func=AF.Identity, bias=bias_ap, scale=scale_ap),
              reads=reads, writes=writes)
    else:
        kb.op('dve', lambda g: g.tensor_scalar(out=out_ap, in0=in_ap, scalar1=scale_ap, scalar2=bias_ap,
                                               op0=ALU.mult, op1=ALU.add), reads=reads, writes=writes)


def build_program(dbg=None):
    nc = bass.Bass("TRN2", target_bir_lowering=False)
    gs = ExitStack()
    kb = KB(nc, gs)

    def din(name, shape, dt=F32):
        return Buf(nc.dram_tensor(name, list(shape), dt, kind="ExternalInput").ap())

    def dscr(name, shape, dt=F32):
        kind = "ExternalOutput" if (dbg and name in dbg) else "Internal"
        return Buf(nc.dram_tensor(name, list(shape), dt, kind=kind).ap())

    xb = din("xb", [SEQ, D])
    xo = din("xo", [2048, D])
    ctxb = din("ctxb", [CTX, D])
    cvec = din("cvec", [128, NKC, 2])
    w_ada = din("w_ada", [D, 6 * D])
    b_ada = din("b_ada", [128, 96])
    nrm = din("nrm", [128, 3, NKC])
    fin_norm = din("fin_norm", [1, D])
    ident = din("ident", [128, 128])
    w_in = din("w_in", [D, 9280])
    cw_ssd = din("cw_ssd", [128, 24, 5])
    dtp = din("dtp", [128, 2, 2])
    dskb = din("dskb", [128, 32])
    masks = din("masks", [128, 2, 128])
    sel = din("sel", [128, 4])
    cmask = din("cmask", [128, 2, 64])
    endsel = din("endsel", [128, 2, 128])
    dmask = din("dmask", [32, 4, 8])
    lcw = din("lcw", [128, 16, 5])
    lgw = din("lgw", [16, 128, 4, 128])
    lgb = din("lgb", [128, 16, 4])
    llam = din("llam", [128, 16, 2])
    w_out_ssd = din("w_out_ssd", [D, D])
    w_out_lru = din("w_out_lru", [D, D])
    w_gate = din("w_gate", [D, 2 * D])
    bgate = din("bgate", [128, 32])
    w_o = din("w_o", [D, D])
    w13 = din("w13", [D, 2 * FFN])
    w2 = din("w2", [FFN, D])
    out = Buf(nc.dram_tensor("out", [2048, D], F32, kind="ExternalOutput").ap())

    gvec = dscr("gvec", [2, D])
    pre_ssd = dscr("pre_ssd", [3136, SEQ + 4])
    pre_ssd_c = dscr("pre_ssd_c", [3136, CTX + 4])
    xsb_tm = dscr("xsb_tm", [SEQ, 2560], BF16)
    xsb_tm_c = dscr("xsb_tm_c", [CTX, 2560], BF16)
    bc_fm = dscr("bc_fm", [1024, SEQ], BF16)
    bc_fm_c = dscr("bc_fm_c", [1024, CTX], BF16)
    ac_scr = dscr("ac_scr", [2, 4096])
    y_own = dscr("y_own", [2048, D])
    xsb_own = dscr("xsb_own", [2048, 2560], BF16)
    bc_own = dscr("bc_own", [1024, 2048], BF16)
    pdt_own = dscr("pdt_own", [64, 2048])
    pre_xr = dscr("pre_xr", [2048, SEQ + 4])
    pre_xr_c = dscr("pre_xr_c", [2048, CTX + 4])
    r_own = dscr("r_own", [2048, 2048])
    sz_scr = dscr("sz_scr", [2048, D])
    ynT_scr = dscr("ynT_scr", [D, 2048], BF16)
    rgT_scr = dscr("rgT_scr", [D, 2048], BF16)
    gT_scr = dscr("gT_scr", [2 * D, 2048], BF16)
    mT_scr = dscr("mT_scr", [D, 2048], BF16)
    x1_scr = dscr("x1_scr", [2048, D])
    aT_scr = dscr("aT_scr", [FFN, 2048], BF16)
    x2_scr = dscr("x2_scr", [2048, D])

    identb = kb.sb(gs, "identb", [128, 128], F32)
    modT = kb.sb(gs, "modT", [128, 96, 2], F32)
    scsh = kb.sb(gs, "scsh", [128, 6, NKC], F32)
    nrmb = kb.sb(gs, "nrmb", [128, 3, NKC], F32)
    kb.dma('sp', identb[:], ident[:, :], identb, reads=[ident], writes=[identb])
    kb.dma('sp', nrmb[:], nrm[:, :, :], nrmb, reads=[nrm], writes=[nrmb])

    with ExitStack() as st:
        st.enter_context(nc.named_scope("st_A"))
        sv = kb.sb(st, "sv", [128, NKC, 2], F32)
        svs = kb.sb(st, "svs", [128, NKC, 2], F32)
        bad = kb.sb(st, "bad", [128, 96], F32)
        wpool = Pool(kb, st, "wada", [128, NKC, 512], F32, 2)
        pm = kb.ps(st, "pm", [128, 96, 2], F32)
        kb.dma('sp', sv[:], cvec[:, :, :], sv, reads=[cvec], writes=[sv])
        kb.dma('sp', bad[:], b_ada[:, :], bad, reads=[b_ada], writes=[bad])
        kb.op('act', lambda g: g.activation(out=svs[:], in_=sv[:], func=AF.Silu), reads=[sv], writes=[svs])
        wv = w_ada.t.rearrange("(kc p) n -> p kc n", p=128)
        for cg in range(24):
            wt = wpool.get()
            kb.dma('sp', wt[:], wv[:, :, cg * 512:(cg + 1) * 512], wt, reads=[w_ada], writes=[wt])
            for j in range(4):
                cb = cg * 4 + j
                for kc in range(NKC):
                    kb.op('pe', lambda g: g.matmul(pm[:, cb, :], wt[:, kc, j * 128:(j + 1) * 128], svs[:, kc, :],
                                                   start=(kc == 0), stop=(kc == NKC - 1)),
                          reads=[wt, svs], writes=[pm])
        kb.op('dve', lambda g: g.tensor_tensor(out=modT[:], in0=pm[:], in1=bad[:].unsqueeze(2).to_broadcast([128, 96, 2]),
                                               op=ALU.add), reads=[pm, bad], writes=[modT])
        for (dst, nidx, scblk, v) in ((0, 0, 16, 0), (2, 0, 16, 1), (4, 1, 64, 0)):
            kb.op('dve', lambda g: g.scalar_tensor_tensor(out=scsh[:, dst, :], in0=modT[:, scblk:scblk + 16, v], scalar=1.0,
                                                          in1=nrmb[:, nidx, :], op0=ALU.add, op1=ALU.mult),
                  reads=[modT, nrmb], writes=[scsh])
        for (dst, shblk, v) in ((1, 0, 0), (3, 0, 1), (5, 48, 0)):
            kb.op('dve', lambda g: g.tensor_copy(out=scsh[:, dst, :], in_=modT[:, shblk:shblk + 16, v]),
                  reads=[modT], writes=[scsh])
        gv = gvec.t.rearrange("g (kc p) -> p g kc", p=128)
        gtmp = kb.sb(st, "gtmp", [128, 2, NKC], F32)
        kb.op('dve', lambda g: g.tensor_copy(out=gtmp[:, 0, :], in_=modT[:, 32:48, 0]), reads=[modT], writes=[gtmp])
        kb.op('dve', lambda g: g.tensor_copy(out=gtmp[:, 1, :], in_=modT[:, 80:96, 0]), reads=[modT], writes=[gtmp])
        kb.dma('sp', gv, gtmp[:], gtmp, reads=[gtmp], writes=[gvec], allow_slow_non_contiguous=True)
        kb.barrier()
        kb.release([sv, svs, bad, gtmp] + wpool.bufs)

    def make_hT(pools, src_ap, src_buf, hT, col0, sci, shi):
        xn = make_xn(pools, src_ap, src_buf)
        xn_T(pools, xn, hT, col0, lambda kc: scsh[:, sci, kc:kc + 1], lambda kc: scsh[:, shi, kc:kc + 1], [scsh])

    def make_xn(pools, src_ap, src_buf):
        xt = pools['xt'].get()
        kb.dma('sp', xt[:], src_ap, xt, reads=[src_buf], writes=[xt])
        return norm_rows(pools, xt)

    def norm_rows(pools, xt):
        sq = pools['sq'].get()
        ss = pools['ss'].get()
        kb.op('dve', lambda g: g.memset(ss[:], 0.0), writes=[ss])
        kb.op('act', lambda g: g.activation(out=sq[:], in_=xt[:], func=AF.Square, accum_out=ss[:, 0:1]),
              reads=[xt], writes=[sq, ss])
        kb.op('dve', lambda g: g.tensor_scalar(out=ss[:, 1:2], in0=ss[:, 0:1], scalar1=1.0 / D, scalar2=EPS,
                                               op0=ALU.mult, op1=ALU.add), reads=[ss], writes=[ss])
        kb.op('act', lambda g: g.activation(out=ss[:, 3:4], in_=ss[:, 1:2], func=AF.Sqrt), reads=[ss], writes=[ss])
        kb.op('dve', lambda g: g.reciprocal(out=ss[:, 2:3], in_=ss[:, 3:4]), reads=[ss], writes=[ss])
        xn = pools['xn'].get()
        kb.op('dve', lambda g: g.tensor_scalar(out=xn[:], in0=xt[:], scalar1=ss[:, 2:3], scalar2=None, op0=ALU.mult),
              reads=[xt, ss], writes=[xn])
        return xn

    def xn_T(pools, xn, hT, col0, scale_fn, shift_fn, pbufs):
        for q in range(4):
            pt = pools['pt'].get()
            for j in range(4):
                kc = q * 4 + j
                kb.op('pe', lambda g: g.transpose(out=pt[:, j * 128:(j + 1) * 128], in_=xn[:, kc * 128:(kc + 1) * 128],
                                                  identity=identb[:]), reads=[xn, identb], writes=[pt])
            e = kb.alt()
            for j in range(4):
                kc = q * 4 + j
                dst = hT[:, kc, col0:col0 + 128]
                src = pt[:, j * 128:(j + 1) * 128]
                if shift_fn is not None:
                    evac_affine(kb, e, dst, src, scale_fn(kc), shift_fn(kc), reads=[pt] + pbufs, writes=[hT.part(kc)])
                elif e == 'dve':
                    kb.op('dve', lambda g: g.tensor_scalar(out=dst, in0=src, scalar1=scale_fn(kc), scalar2=None, op0=ALU.mult),
                          reads=[pt] + pbufs, writes=[hT.part(kc)])
                else:
                    kb.op('act', lambda g: g.activation(out=dst, in_=src, func=AF.Copy, scale=scale_fn(kc)),
                          reads=[pt] + pbufs, writes=[hT.part(kc)])

    def norm_T(pools, xt, hT, col0, scale_fn, shift_fn, pbufs):
        xn = norm_rows(pools, xt)
        xn_T(pools, xn, hT, col0, scale_fn, shift_fn, pbufs)

    def build_hT(pools, src, hT, ntiles, sci, shi):
        xn = make_xn(pools, src.t[0:128, :], src)
        for tt in range(ntiles):
            nxt = make_xn(pools, src.t[(tt + 1) * 128:(tt + 2) * 128, :], src) if tt + 1 < ntiles else None
            xn_T(pools, xn, hT, tt * 128, lambda kc: scsh[:, sci, kc:kc + 1], lambda kc: scsh[:, shi, kc:kc + 1], [scsh])
            xn = nxt

    def norm_pools(st, tag, nxn=1, npt=2, nxt=2):
        return {
            'xt': Pool(kb, st, tag + "xt", [128, D], F32, nxt),
            'sq': Pool(kb, st, tag + "sq", [128, D], BF16, 1),
            'ss': Pool(kb, st, tag + "ss", [128, 4], F32, 3),
            'xn': Pool(kb, st, tag + "xn", [128, D], F32, nxn),
            'pt': Pool(kb, st, tag + "pt", [128, 512], F32, npt, psum=True),
        }

    def load_weight(st_pool, w_dram, col_lo, ncols, wsb, dst_lo, row_lo=0, nkc=NKC, kc_dst=0):
        wv_ = w_dram.t[row_lo:row_lo + nkc * 128, :].rearrange("(kc p) n -> p kc n", p=128)
        c = 0
        while c < ncols:
            n = min(128, ncols - c)
            stg = st_pool.get()
            kb.dma('sp', stg[:, 0:nkc, 0:n], wv_[:, :, col_lo + c:col_lo + c + n], stg, reads=[w_dram], writes=[stg])
            e = kb.alt(('act', 'dve'))
            dst = wsb[:, kc_dst:kc_dst + nkc, dst_lo + c:dst_lo + c + n]
            wpart = wsb.part((dst_lo + c) // 128)
            if e == 'act':
                kb.op('act', lambda g: g.copy(out=dst, in_=stg[:, 0:nkc, 0:n]), reads=[stg], writes=[wpart])
            else:
                kb.op(e, lambda g: g.tensor_copy(out=dst, in_=stg[:, 0:nkc, 0:n]), reads=[stg], writes=[wpart])
            c += n

    NT_C = dbg.get("nt_c", 16) if dbg else 16

    def proj_stage(tag, wcol0, ncols, jobs, zero_pads):
        nblk = (ncols + 127) // 128
        with ExitStack() as st:
            st.enter_context(nc.named_scope("st_" + tag))
            pools = norm_pools(st, tag, nxn=2, npt=3, nxt=3)
            wstg = Pool(kb, st, tag + "wstg", [128, NKC, 128], F32, 2)
            wsb = kb.sb(st, tag + "w", [128, NKC, ncols], BF16)
            load_weight(wstg, w_in, wcol0, ncols, wsb, 0)
            hTp = Pool(kb, st, tag + "hT", [128, NKC, 512], BF16, 2)
            pg = Pool(kb, st, tag + "pg", [128, 512], F32, 5, psum=True)
            ostg = Pool(kb, st, tag + "ostg", [128, 512], F32, 4)
            zt = kb.sb(st, tag + "zt", [128, 4], F32)
            kb.op('dve', lambda g: g.memset(zt[:], 0.0), writes=[zt])
            for (dst, L) in zero_pads:
                for rb in range(nblk):
                    r0 = rb * 128
                    nr = min(128, ncols - r0)
                    kb.dma('pool', dst.t[r0:r0 + nr, 0:2], zt[0:nr, 0:2], zt, reads=[zt], writes=[dst])
                    kb.dma('pool', dst.t[r0:r0 + nr, L + 2:L + 4], zt[0:nr, 2:4], zt, reads=[zt], writes=[dst])
            hTs = {}

            tiles = []
            for ji, jb in enumerate(jobs):
                for i in range(jb[2] // 128):
                    tiles.append((ji, i))
            xns = {}

            def prep_a(k):
                if k < len(tiles):
                    ji, i = tiles[k]
                    xns[k] = make_xn(pools, jobs[ji][1](i), jobs[ji][0])

            def prep_b(k):
                ji, i = tiles[k]
                (src, apfn, nt, dst, dcol, sci, shi) = jobs[ji]
                if i == 0:
                    hTs[ji] = hTp.get()
                xn_T(pools, xns.pop(k), hTs[ji], i * 128, lambda kc: scsh[:, sci, kc:kc + 1], lambda kc: scsh[:, shi, kc:kc + 1], [scsh])

            tk = 0
            prep_a(0)
            n0 = jobs[0][2] // 128
            for k in range(n0):
                prep_a(k + 1)
                prep_b(k)
            tk = n0
            for ji, (src, apfn, nt, dst, dcol, sci, shi) in enumerate(jobs):
                hT = hTs[ji]
                nxt_tiles = (jobs[ji + 1][2] // 128) if ji + 1 < len(jobs) else 0
                step = max(1, nblk // max(1, nxt_tiles))
                emitted = 0
                for cb in range(nblk):
                    c0 = cb * 128
                    ncol = min(128, ncols - c0)
                    pp = pg.get()
                    for kc in range(NKC):
                        kb.op('pe', lambda g: g.matmul(pp[0:ncol, 0:nt], wsb[:, kc, c0:c0 + ncol], hT[:, kc, 0:nt],
                                                       start=(kc == 0), stop=(kc == NKC - 1)), reads=[wsb.part(cb), hT.part(kc)], writes=[pp])
                    og = ostg.get()
                    if kb.alt() == 'act':
                        kb.op('act', lambda g: g.copy(out=og[0:ncol, 0:nt], in_=pp[0:ncol, 0:nt]), reads=[pp], writes=[og])
                    else:
                        kb.op('dve', lambda g: g.tensor_copy(out=og[0:ncol, 0:nt], in_=pp[0:ncol, 0:nt]), reads=[pp], writes=[og])
                    kb.dma('pool', dst.t[c0:c0 + ncol, dcol:dcol + nt], og[0:ncol, 0:nt], og, reads=[og], writes=[dst])
                    if emitted < nxt_tiles and (cb + 1) % step == 0:
                        prep_a(tk + 1)
                        prep_b(tk)
                        tk += 1
                        emitted += 1
                while emitted < nxt_tiles:
                    prep_a(tk + 1)
                    prep_b(tk)
                    tk += 1
                    emitted += 1
            kb.barrier()
            kb.release(sum([p.bufs for p in pools.values()], []) + wstg.bufs + hTp.bufs + ostg.bufs + [zt])

    def rm_tile(src, t0):
        return lambda i: src.t[t0 + i * 128:t0 + (i + 1) * 128, :]
    jobsC = [(ctxb, rm_tile(ctxb, 0), 256, pre_ssd_c, 2, 2, 3)]
    jobsC += [(xb, rm_tile(xb, s_ * 512), 512, pre_ssd, 2 + s_ * 512, 0, 1) for s_ in range(NT_C)]
    proj_stage("C", OXS, 3136, jobsC, [(pre_ssd, SEQ), (pre_ssd_c, CTX)])

    with ExitStack() as st:
        st.enter_context(nc.named_scope("st_C2"))
        cw = kb.sb(st, "cw", [128, 24, 5], F32)
        kb.dma('sp', cw[:], cw_ssd[:, :, :], cw, reads=[cw_ssd], writes=[cw])
        identbf = kb.sb(st, "identbf", [128, 128], BF16)
        kb.op('dve', lambda g: g.tensor_copy(out=identbf[:], in_=identb[:]), reads=[identb], writes=[identbf])
        prep = Pool(kb, st, "cpre", [128, 2051], F32, 3)
        accp = Pool(kb, st, "cacc", [128, 2048], F32, 3)
        ctmp = kb.sb(st, "ctmp", [128, 2048], F32)
        xcp = Pool(kb, st, "cxc", [128, 2048], BF16, 3)
        ptp = Pool(kb, st, "cpt", [128, 1024], BF16, 4, psum=True)
        tmp_ = Pool(kb, st, "ctm", [128, 16, 128], BF16, 3)
        seqs = [(pre_ssd_c, CTX, xsb_tm_c, bc_fm_c), (pre_ssd, min(SEQ, NT_C * 512), xsb_tm, bc_fm)]
        units = []
        for (pre, L, xsb, bcf) in seqs:
            TG = min(L, 2048)
            for tg in range(L // TG):
                for blk in range(24):
                    units.append((pre, xsb, bcf, TG, tg * TG, blk))

        def c2_front(u):
            (pre, xsb, bcf, TG, t0, blk) = u
            pt_ = prep.get()
            kb.dma('sp', pt_[:, 0:TG + 3], pre.t[blk * 128:(blk + 1) * 128, t0:t0 + TG + 3], pt_, reads=[pre], writes=[pt_])
            acc = accp.get()
            kb.op('act', lambda g: g.activation(out=acc[:, 0:TG], in_=pt_[:, 0:TG], func=AF.Identity,
                                                scale=cw[:, blk, 0:1], bias=cw[:, blk, 4:5]), reads=[pt_, cw], writes=[acc])
            return (pt_, acc)

        def c2_back(u, fr):
            (pre, xsb, bcf, TG, t0, blk) = u
            pt_, acc = fr
            for k in range(1, 4):
                kb.op('dve', lambda g: g.scalar_tensor_tensor(out=acc[:, 0:TG], in0=pt_[:, k:k + TG], scalar=cw[:, blk, k:k + 1],
                                                              in1=acc[:, 0:TG], op0=ALU.mult, op1=ALU.add),
                      reads=[pt_, cw, acc], writes=[acc])
            xc = xcp.get()
            kb.op('act', lambda g: g.activation(out=xc[:, 0:TG], in_=acc[:, 0:TG], func=AF.Silu), reads=[acc], writes=[xc])
            if blk >= 16:
                kb.dma('pool', bcf.t[(blk - 16) * 128:(blk - 15) * 128, t0:t0 + TG], xc[:, 0:TG], xc, reads=[xc], writes=[bcf])
            if blk < 20:
                tm = tmp_.get()
                ntt = TG // 128
                for q in range((ntt + 3) // 4):
                    pp = ptp.get()
                    nj = min(4, ntt - q * 4)
                    for j in range(nj):
                        tt = q * 4 + j
                        kb.op('pe', lambda g: g.transpose(out=pp[:, j * 128:(j + 1) * 128], in_=xc[:, tt * 128:(tt + 1) * 128],
                                                          identity=identbf[:]), reads=[xc, identbf], writes=[pp])
                    src_v = pp[:, 0:nj * 128].rearrange("p (a b) -> p a b", a=nj)
                    if kb.alt() == 'act':
                        kb.op('act', lambda g: g.copy(out=tm[:, q * 4:q * 4 + nj, :], in_=src_v), reads=[pp], writes=[tm.part(q)])
                    else:
                        kb.op('dve', lambda g: g.tensor_copy(out=tm[:, q * 4:q * 4 + nj, :], in_=src_v), reads=[pp], writes=[tm.part(q)])
                kb.dma('pool', xsb.t[t0:t0 + TG, blk * 128:(blk + 1) * 128].rearrange("(tt p) c -> p tt c", p=128),
                       tm[:, 0:ntt, :], tm, reads=[tm], writes=[xsb])

        fr = c2_front(units[0])
        for ui in range(len(units)):
            nfr = c2_front(units[ui + 1]) if ui + 1 < len(units) else None
            c2_back(units[ui], fr)
            fr = nfr
        kb.barrier()
        kb.release([cw] + prep.bufs + xcp.bufs + tmp_.bufs)

    with ExitStack() as st:
        st.enter_context(nc.named_scope("st_D0"))
        selb = kb.sb(st, "D0sel", [128, 4], F32)
        kb.dma('sp', selb[:], sel[:, :], selb, reads=[sel], writes=[selb])
        idsel = kb.sb(st, "idsel", [128, 4, 128], BF16)
        idsel32 = kb.sb(st, "idsel32", [128, 4, 128], F32)
        for b_ in range(4):
            kb.op('dve', lambda g: g.tensor_scalar(out=idsel[:, b_, :], in0=identb[:], scalar1=selb[:, b_:b_ + 1], scalar2=None, op0=ALU.mult),
                  reads=[identb, selb], writes=[idsel])
            kb.op('dve', lambda g: g.tensor_scalar(out=idsel32[:, b_, :], in0=identb[:], scalar1=selb[:, b_:b_ + 1], scalar2=None, op0=ALU.mult),
                  reads=[identb, selb], writes=[idsel32])
        xcp_ = Pool(kb, st, "D0xc", [128, 2560], BF16, 8)
        bcp_ = Pool(kb, st, "D0bc", [128, 8, 128], BF16, 8)
        pdp_ = Pool(kb, st, "D0pd", [64, 128], F32, 8)
        xop_ = Pool(kb, st, "D0xo", [128, 2560], BF16, 2)
        bop_ = Pool(kb, st, "D0bo", [128, 8, 128], BF16, 2)
        pop_ = Pool(kb, st, "D0po", [64, 128], F32, 2)
        psel = Pool(kb, st, "D0ps", [128, 512], F32, 4, psum=True)
        bc_v = bc_fm.t.rearrange("(b p) t -> p b t", p=128)
        bco_v = bc_own.t.rearrange("(b p) t -> p b t", p=128)

        def evac(dst_ap, src_ap, rd, wr):
            if kb.alt() == 'act':
                kb.op('act', lambda g: g.copy(out=dst_ap, in_=src_ap), reads=rd, writes=wr)
            else:
                kb.op('dve', lambda g: g.tensor_copy(out=dst_ap, in_=src_ap), reads=rd, writes=wr)

        for i in range(16):
            xcs = []; bcs = []; pds = []
            for b_ in range(4):
                c = 16 * b_ + i
                xc = xcp_.get(); bc = bcp_.get(); pd_ = pdp_.get()
                kb.dma('sp', xc[:], xsb_tm.t[c * 128:(c + 1) * 128, :], xc, reads=[xsb_tm], writes=[xc])
                kb.dma('sp', bc[:], bc_v[:, :, c * 128:(c + 1) * 128], bc, reads=[bc_fm], writes=[bc])
                kb.dma('sp', pd_[:], pre_ssd.t[3072:3136, 2 + c * 128:2 + (c + 1) * 128], pd_, reads=[pre_ssd], writes=[pd_])
                xcs.append(xc); bcs.append(bc); pds.append(pd_)
            xo_ = xop_.get()
            for q in range(5):
                pp = psel.get()
                for b_ in range(4):
                    kb.op('pe', lambda g: g.matmul(pp[:], idsel[:, b_, :], xcs[b_][:, q * 512:(q + 1) * 512], start=(b_ == 0), stop=(b_ == 3)),
                          reads=[idsel, xcs[b_]], writes=[pp])
                evac(xo_[:, q * 512:(q + 1) * 512], pp[:], [pp], [xo_.part(q)])
            kb.dma('pool', xsb_own.t[i * 128:(i + 1) * 128, :], xo_[:], xo_, reads=[xo_], writes=[xsb_own])
            bo_ = bop_.get()
            for q in range(2):
                pp = psel.get()
                for b_ in range(4):
                    kb.op('pe', lambda g: g.matmul(pp[:], idsel[:, b_, :], bcs[b_][:, q * 4:(q + 1) * 4, :].rearrange("p a t -> p (a t)"),
                                                   start=(b_ == 0), stop=(b_ == 3)), reads=[idsel, bcs[b_]], writes=[pp])
                evac(bo_[:, q * 4:(q + 1) * 4, :].rearrange("p a t -> p (a t)"), pp[:], [pp], [bo_.part(q)])
            kb.dma('pool', bco_v[:, :, i * 128:(i + 1) * 128], bo_[:], bo_, reads=[bo_], writes=[bc_own])
            po_ = pop_.get()
            pp = psel.get()
            for b_ in range(4):
                kb.op('pe', lambda g: g.matmul(pp[0:64, 0:128], idsel32[0:64, b_, 0:64], pds[b_][:], start=(b_ == 0), stop=(b_ == 3)),
                      reads=[idsel32, pds[b_]], writes=[pp])
            evac(po_[:], pp[0:64, 0:128], [pp], [po_])
            kb.dma('pool', pdt_own.t[:, i * 128:(i + 1) * 128], po_[:], po_, reads=[po_], writes=[pdt_own])
        kb.barrier()
        kb.release([selb] + xcp_.bufs + bcp_.bufs + pdp_.bufs + xop_.bufs + bop_.bufs + pop_.bufs)

    NCH_D = dbg.get("nch_d", 64) if dbg else 64
    with ExitStack() as st:
        st.enter_context(nc.named_scope("st_D"))
        dtpb = kb.sb(st, "dtpb", [128, 2, 2], F32)
        dsk = kb.sb(st, "dsk", [128, 32], F32)
        msk = kb.sb(st, "msk", [128, 2, 128], F32)
        cmk = kb.sb(st, "cmk", [128, 2, 64], F32)
        esel = kb.sb(st, "esel", [128, 2, 128], F32)
        kb.dma('sp', dtpb[:], dtp[:, :, :], dtpb, reads=[dtp], writes=[dtpb])
        kb.dma('sp', dsk[:], dskb[:, :], dsk, reads=[dskb], writes=[dsk])
        kb.dma('sp', msk[:], masks[:, :, :], msk, reads=[masks], writes=[msk])
        kb.dma('sp', cmk[:], cmask[:, :, :], cmk, reads=[cmask], writes=[cmk])
        kb.dma('sp', esel[:], endsel[:, :, :], esel, reads=[endsel], writes=[esel])
        dmk = kb.sb(st, "dmk", [32, 4, 8], F32)
        kb.dma('sp', dmk[:], dmask[:, :, :], dmk, reads=[dmask], writes=[dmk])
        arowp = Pool(kb, st, "darow", [128, 32, 128], F32, 2)
        aneg = kb.sb(st, "aneg", [128, 2], F32)
        ones = kb.sb(st, "ones", [128, 128], F32)
        kb.op('dve', lambda g: g.memset(ones[:], 1.0), writes=[ones])
        kb.op('act', lambda g: g.activation(out=aneg[:], in_=dtpb[:, :, 1], func=AF.Exp), reads=[dtpb], writes=[aneg])
        kb.op('dve', lambda g: g.tensor_scalar(out=aneg[:], in0=aneg[:], scalar1=-1.0, scalar2=None, op0=ALU.mult),
              reads=[aneg], writes=[aneg])
        Hs = [kb.sb(st, "Hst%d" % d_, [128, 2048], F32) for d_ in range(2)]
        Hbfs = [kb.sb(st, "Hbf%d" % d_, [128, 2048], BF16) for d_ in range(2)]
        xsbp = Pool(kb, st, "dxsb", [128, 2560], BF16, 6)
        bcp = Pool(kb, st, "dbc", [128, 8, 128], BF16, 2)
        pdtp = Pool(kb, st, "dpdt", [128, 128], F32, 6)
        smallp = Pool(kb, st, "dsm", [128, 128], F32, 18)
        T4p = Pool(kb, st, "dT4", [128, 128], F32, 3)
        tm4p = Pool(kb, st, "dtm4", [128, 128], F32, 3)
        decp = Pool(kb, st, "ddec", [128, 32], F32, 3)
        xdtp = Pool(kb, st, "dxdt", [128, 2048], BF16, 2)
        xwp = Pool(kb, st, "dxw", [128, 2048], BF16, 3)
        xsdp = Pool(kb, st, "dxsd", [128, 2048], BF16, 2)
        identbf = kb.sb(st, "identbfD", [128, 128], BF16)
        kb.op('dve', lambda g: g.tensor_copy(out=identbf[:], in_=identb[:]), reads=[identb], writes=[identbf])
        cbmp = Pool(kb, st, "dcbm", [128, 4, 128], F32, 2)
        Ep = Pool(kb, st, "dE", [128, 8, 128], F32, 2)
        MTp = Pool(kb, st, "dMT", [128, 8, 128], BF16, 2)
        ychp = Pool(kb, st, "dych", [128, 2048], F32, 2)
        yprp = Pool(kb, st, "dypr", [128, 2048], F32, 2)
        tep = Pool(kb, st, "dte", [128, 512], F32, 3)
        pT = Pool(kb, st, "pT", [128, 512], F32, 2, psum=True)
        PS = {}
        slot = [0]
        Ptile = kb.sb(st, "Ptile", [128, 32], F32)
        dtesp = Pool(kb, st, "ddtes", [128, 32], F32, 3)

        def dt_chain(d, pdt_ap, pdt_buf, mask_ap):
            pdt = pdtp.get()
            for r in range(4):
                kb.dma('sp', pdt[r * 32:(r + 1) * 32, :], pdt_ap, pdt, reads=[pdt_buf], writes=[pdt])
            yield
            e1 = smallp.get(); dt_ = smallp.get(); dtA = smallp.get(); cum = smallp.get(); ac = smallp.get()
            ex = smallp.get()
            kb.op('act', lambda g: g.activation(out=e1[:], in_=pdt[:], func=AF.Exp, bias=dtpb[:, d, 0:1], scale=1.0),
                  reads=[pdt, dtpb], writes=[e1])
            yield
            kb.op('act', lambda g: g.activation(out=dt_[:], in_=e1[:], func=AF.Ln, bias=1.0, scale=1.0), reads=[e1], writes=[dt_])
            yield
            if mask_ap is not None:
                kb.op('dve', lambda g: g.tensor_scalar(out=dt_[:], in0=dt_[:], scalar1=mask_ap, scalar2=None, op0=ALU.mult),
                      reads=[dt_, cmk], writes=[dt_])
                yield
            kb.op('dve', lambda g: g.tensor_scalar(out=dtA[:], in0=dt_[:], scalar1=aneg[:, d:d + 1], scalar2=None, op0=ALU.mult),
                  reads=[dt_, aneg], writes=[dtA])
            yield
            kb.op('dve', lambda g: g.tensor_tensor_scan(out=cum[:], data0=ones[:], data1=dtA[:], initial=0.0,
                                                        op0=ALU.mult, op1=ALU.add), reads=[ones, dtA], writes=[cum])
            yield
            if d == 0:
                ac = cum
            else:
                kb.op('dve', lambda g: g.scalar_tensor_tensor(out=ac[:], in0=dtA[:], scalar=cum[:, 127:128], in1=cum[:],
                                                              op0=ALU.add, op1=ALU.subtract), reads=[dtA, cum], writes=[ac])
                yield
            kb.op('act', lambda g: g.activation(out=ex[:], in_=ac[:], func=AF.Exp, bias=cum[:, 127:128], scale=-1.0),
                  reads=[ac, cum], writes=[ex])
            yield
            T4 = T4p.get()
            kb.op('act', lambda g: g.copy(out=T4[0:32, :], in_=ac[0:32, :]), reads=[ac], writes=[T4])
            yield
            kb.op('act', lambda g: g.copy(out=T4[32:64, :], in_=dt_[32:64, :]), reads=[dt_], writes=[T4])
            yield
            kb.op('dve', lambda g: g.tensor_tensor(out=T4[64:96, :], in0=dt_[64:96, :], in1=ex[64:96, :], op=ALU.mult),
                  reads=[dt_, ex], writes=[T4])
            yield
            kb.op('act', lambda g: g.activation(out=T4[96:128, :], in_=ac[96:128, :], func=AF.Exp), reads=[ac], writes=[T4])
            yield
            ptt = pT.get()
            kb.op('pe', lambda g: g.transpose(out=ptt[:, 0:128], in_=T4[:], identity=identb[:]), reads=[T4, identb], writes=[ptt])
            yield
            tm4 = tm4p.get()
            kb.op('act', lambda g: g.copy(out=tm4[:], in_=ptt[:, 0:128]), reads=[ptt], writes=[tm4])
            yield
            kb.op('pe', lambda g: g.matmul(ptt[:, 128:160], esel[:, d, :], tm4[:, 0:32], start=True, stop=True),
                  reads=[esel, tm4], writes=[ptt])
            yield
            dec = decp.get()
            kb.op('act', lambda g: g.activation(out=dec[:], in_=ptt[:, 128:160], func=AF.Exp), reads=[ptt], writes=[dec])
            yield
            return ac, tm4, dec

        def ssd_state_chunk(d, pdt_ap, pdt_buf, xsb, c, mask_ap, first, last):
            t0 = c * 128
            xsbt = xsbp.get()
            kb.dma('sp', xsbt[:], xsb.t[t0:t0 + 128, :], xsbt, reads=[xsb], writes=[xsbt])
            yield
            ac, tm4, dec = yield from dt_chain(d, pdt_ap, pdt_buf, mask_ap)
            dtes = dtesp.get()
            kb.op('dve', lambda g: g.tensor_tensor(out=dtes[:], in0=tm4[:, 64:96], in1=Ptile[:], op=ALU.mult), reads=[tm4, Ptile], writes=[dtes])
            yield
            kb.op('dve', lambda g: g.tensor_tensor(out=Ptile[:], in0=Ptile[:], in1=dec[:], op=ALU.mult), reads=[Ptile, dec], writes=[Ptile])
            yield
            xw = xwp.get()
            kb.op('dve', lambda g: g.tensor_tensor(out=xw[:].rearrange("p (h e) -> p h e", h=32),
                                                   in0=xsbt[:, 0:2048].rearrange("p (h e) -> p h e", h=32),
                                                   in1=dtes[:].unsqueeze(2).to_broadcast([128, 32, 64]), op=ALU.mult),
                  reads=[xsbt, dtes], writes=[xw])
            yield
            for g_ in range(4):
                gc = slice(g_ * 512, (g_ + 1) * 512)
                kb.op('pe', lambda g: g.matmul(PS['H'][g_][:], xsbt[:, 2048 + g_ * 128:2048 + (g_ + 1) * 128], xw[:, gc],
                                               start=first, stop=last), reads=[xsbt, xw], writes=[PS['H'][g_]])
                yield

        def run_gen(gen):
            try:
                while True:
                    next(gen)
            except StopIteration as e:
                return e.value

        def run_chains(gens, nactive, stagger):
            pending = list(gens)
            active = []
            steps = 0
            while pending or active:
                if pending and len(active) < nactive and (not active or steps >= stagger):
                    active.append(pending.pop(0))
                    steps = 0
                for gch in list(active):
                    try:
                        next(gch)
                    except StopIteration:
                        active.remove(gch)
                steps += 1

        def pdt_of(pre, d, c):
            return pre.t[3072 + 32 * d:3072 + 32 * d + 32, 2 + c * 128:2 + (c + 1) * 128]

        def ssd_chunk(d, pdt_ap, pdt_buf, xsb, bcf, c, with_y, mask_ap, rmw=False):
            H = Hs[d]
            Hbf = Hbfs[d]
            pCB, pOFF, pDG, pS = PS['CB'], PS['OFF'], PS['DG'], PS['S']
            t0 = c * 128
            xsbt = xsbp.get()
            kb.dma('sp', xsbt[:], xsb.t[t0:t0 + 128, :], xsbt, reads=[xsb], writes=[xsbt])
            ac, tm4, dec = run_gen(dt_chain(d, pdt_ap, pdt_buf, mask_ap))
            xs3 = xsbt[:, 0:2048].rearrange("p (h e) -> p h e", h=32)
            xw = xwp.get()
            kb.op('dve', lambda g: g.tensor_tensor(out=xw[:].rearrange("p (h e) -> p h e", h=32), in0=xs3,
                                                    in1=tm4[:, 64:96].unsqueeze(2).to_broadcast([128, 32, 64]), op=ALU.mult),
                  reads=[xsbt, tm4], writes=[xw])
            if with_y:
                sl = slot[0] % 2
                slot[0] += 1
                kb.dma('pool', ac_scr.t[sl:sl + 1, :].rearrange("o (h t) -> (o h) t", h=32), ac[0:32, :], ac, reads=[ac], writes=[ac_scr])
                arow = arowp.get()
                kb.dma('sp', arow[:].rearrange("p h t -> p (h t)"), ac_scr.t[sl:sl + 1, :].partition_broadcast(128), arow,
                       reads=[ac_scr], writes=[arow])
                kb.op('act', lambda g: g.copy(out=Hbf[:], in_=H[:]), reads=[H], writes=[Hbf])
                bct = bcp.get()
                kb.dma('sp', bct[:], bcf.t.rearrange("(b p) t -> p b t", p=128)[:, :, t0:t0 + 128], bct, reads=[bcf], writes=[bct])
                xdt = xdtp.get()
                kb.op('dve', lambda g: g.tensor_tensor(out=xdt[:].rearrange("p (h e) -> p h e", h=32), in0=xs3,
                                                       in1=tm4[:, 32:64].unsqueeze(2).to_broadcast([128, 32, 64]), op=ALU.mult),
                      reads=[xsbt, tm4], writes=[xdt])
                if d == 0:
                    xsd = xsdp.get()
                    kb.op('dve', lambda g: g.tensor_tensor(out=xsd[:].rearrange("p (h e) -> p h e", h=32), in0=xs3,
                                                           in1=dsk[:].unsqueeze(2).to_broadcast([128, 32, 64]), op=ALU.mult),
                          reads=[xsbt, dsk], writes=[xsd])
                for g_ in range(4):
                    kb.op('pe', lambda g: g.matmul(pCB[:, g_ * 128:(g_ + 1) * 128], bct[:, g_, :], bct[:, 4 + g_, :],
                                                   start=True, stop=True), reads=[bct], writes=[pCB])
                cbm = cbmp.get()
                kb.op('dve', lambda g: g.tensor_tensor(out=cbm[:], in0=pCB[:].rearrange("p (a b) -> p a b", a=4),
                                                       in1=msk[:, d:d + 1, :].to_broadcast([128, 4, 128]), op=ALU.mult),
                      reads=[pCB, msk], writes=[cbm])
                ych = ychp.get()
                if rmw:
                    ypr = yprp.get()
                    kb.dma('sp', ypr[:], y_own.t[t0:t0 + 128, :], ypr, reads=[y_own], writes=[ypr])
            for g_ in range(4):
                gc = slice(g_ * 512, (g_ + 1) * 512)
                if with_y:
                    E = Ep.get()
                    for h in range(8):
                        gh = g_ * 8 + h
                        kb.op('act', lambda g: g.activation(out=E[:, h, :], in_=arow[:, gh, :], func=AF.Relu, bias=tm4[:, gh:gh + 1], scale=-1.0),
                              reads=[arow, tm4], writes=[E])
                    kb.op('act', lambda g: g.activation(out=E[:], in_=E[:], func=AF.Exp, scale=-1.0), reads=[E], writes=[E])
                    MT = MTp.get()
                    kb.op('dve', lambda g: g.tensor_tensor(out=MT[:], in0=E[:], in1=cbm[:, g_:g_ + 1, :].to_broadcast([128, 8, 128]),
                                                           op=ALU.mult), reads=[E, cbm], writes=[MT])
                    po = pOFF.get()
                    kb.op('pe', lambda g: g.matmul(po[:], bct[:, 4 + g_, :], Hbf[:, gc], start=True, stop=True),
                          reads=[bct, Hbf], writes=[po])
                    pd = pDG.get()
                    if d == 0:
                        kb.op('pe', lambda g: g.matmul(pd[:], identbf[:], xsd[:, gc], start=True, stop=False), reads=[identbf, xsd], writes=[pd])
                    for h in range(8):
                        gh = g_ * 8 + h
                        kb.op('pe', lambda g: g.matmul(pd[:, h * 64:(h + 1) * 64], MT[:, h, :], xdt[:, gh * 64:(gh + 1) * 64],
                                                       start=(d == 1), stop=True), reads=[MT, xdt], writes=[pd])
                    te = tep.get()
                    kb.op('dve', lambda g: g.tensor_tensor(out=te[:].rearrange("p (h e) -> p h e", h=8),
                                                           in0=po[:].rearrange("p (h e) -> p h e", h=8),
                                                           in1=tm4[:, 96 + g_ * 8:96 + g_ * 8 + 8].unsqueeze(2).to_broadcast([128, 8, 64]),
                                                           op=ALU.mult), reads=[po, tm4], writes=[te])
                    if not rmw:
                        kb.op('dve', lambda g: g.tensor_tensor(out=ych[:, gc], in0=te[:], in1=pd[:], op=ALU.add), reads=[te, pd], writes=[ych])
                    else:
                        kb.op('dve', lambda g: g.tensor_tensor(out=te[:], in0=te[:], in1=pd[:], op=ALU.add), reads=[te, pd], writes=[te])
                        kb.op('dve', lambda g: g.tensor_tensor(out=ych[:, gc], in0=te[:], in1=ypr[:, gc], op=ALU.add),
                              reads=[te, ypr], writes=[ych])
                psb = pS.get()
                kb.op('pe', lambda g: g.matmul(psb[:], xsbt[:, 2048 + g_ * 128:2048 + (g_ + 1) * 128], xw[:, gc], start=True, stop=True),
                      reads=[xsbt, xw], writes=[psb])
                kb.op('dve', lambda g: g.tensor_tensor(out=H[:, gc].rearrange("p (h e) -> p h e", h=8),
                                                       in0=H[:, gc].rearrange("p (h e) -> p h e", h=8),
                                                       in1=dec[:, g_ * 8:g_ * 8 + 8].unsqueeze(2).to_broadcast([128, 8, 64]),
                                                       op=ALU.mult), reads=[H, dec], writes=[H])
                kb.op('dve', lambda g: g.tensor_tensor(out=H[:, gc], in0=H[:, gc], in1=psb[:], op=ALU.add), reads=[H, psb], writes=[H])
            if with_y:
                kb.dma('pool', y_own.t[t0:t0 + 128, :], ych[:], ych, reads=[ych], writes=[y_own])

        for d in range(2):
            with ExitStack() as st3:
                st3.enter_context(nc.named_scope("st_Dst%d" % d))
                PS['H'] = [kb.ps(st3, "psH%d_%d" % (d, g_), [128, 512], F32) for g_ in range(4)]
                kb.op('dve', lambda g: g.memset(Ptile[:], 1.0), writes=[Ptile])
                if d == 0:
                    seq = [('lat', c) for c in range(NCH_D - 17, -1, -1)] + [('ctx', 1), ('ctx', 0)]
                else:
                    seq = [('lat', c) for c in range(16, NCH_D)] + [('ctx', 0), ('ctx', 1)]
                gens = []
                for si, (kind, c) in enumerate(seq):
                    if kind == 'lat':
                        gens.append(ssd_state_chunk(d, pdt_of(pre_ssd, d, c), pre_ssd, xsb_tm, c, cmk[:, d, c:c + 1], si == 0, si == len(seq) - 1))
                    else:
                        gens.append(ssd_state_chunk(d, pdt_of(pre_ssd_c, d, c), pre_ssd_c, xsb_tm_c, c, None, si == 0, si == len(seq) - 1))
                run_chains(gens, 3, 8)
                for g_ in range(4):
                    gc = slice(g_ * 512, (g_ + 1) * 512)
                    if g_ % 2 == 0:
                        kb.op('act', lambda g: g.copy(out=Hs[d][:, gc], in_=PS['H'][g_][:]), reads=[PS['H'][g_]], writes=[Hs[d]])
                    else:
                        kb.op('dve', lambda g: g.tensor_copy(out=Hs[d][:, gc], in_=PS['H'][g_][:]), reads=[PS['H'][g_]], writes=[Hs[d]])
                kb.barrier(['pe', 'act', 'dve'])
        with ExitStack() as st3:
            st3.enter_context(nc.named_scope("st_Down"))
            PS['CB'] = kb.ps(st3, "pCB", [128, 512], F32)
            PS['OFF'] = Pool(kb, st3, "pOFF", [128, 512], F32, 1, psum=True)
            PS['DG'] = Pool(kb, st3, "pDG", [128, 512], F32, 2, psum=True)
            PS['S'] = Pool(kb, st3, "pS", [128, 512], F32, 2, psum=True)
            for i in range(16):
                for d in range(2):
                    c = i if d == 0 else 15 - i
                    ssd_chunk(d, pdt_own.t[32 * d:32 * d + 32, c * 128:(c + 1) * 128], pdt_own, xsb_own, bc_own, c, True, None, rmw=(i >= 8))
            kb.barrier(['pe', 'act', 'dve'])
        kb.barrier()
        kb.release(xsbp.bufs + bcp.bufs + pdtp.bufs + smallp.bufs + arowp.bufs + ychp.bufs + yprp.bufs + [dtpb, dsk, msk, cmk, esel])

    NT_E = dbg.get("nt_e", 16) if dbg else 16
    xb_cm = xb.t.rearrange("(r w) d -> w r d", w=64)
    def cm_tile(s_):
        return lambda i: xb_cm[4 * s_ + i]
    jobsE = [(ctxb, rm_tile(ctxb, 0), 256, pre_xr_c, 2, 2, 3)]
    jobsE += [(xb, cm_tile(s_), 512, pre_xr, 2 + s_ * 512, 0, 1) for s_ in range(NT_E)]
    proj_stage("E", OXR, 2048, jobsE, [(pre_xr, SEQ), (pre_xr_c, CTX)])

    NH_F = dbg.get("nh_f", 16) if dbg else 16
    with ExitStack() as st:
        st.enter_context(nc.named_scope("st_F"))
        lcwb = kb.sb(st, "lcwb", [128, 16, 5], F32)
        lgbb = kb.sb(st, "lgbb", [128, 16, 4], F32)
        lamb = kb.sb(st, "lamb", [128, 16, 2], F32)
        selb = kb.sb(st, "selb", [128, 4], F32)
        kb.dma('sp', lcwb[:], lcw[:, :, :], lcwb, reads=[lcw], writes=[lcwb])
        kb.dma('sp', lgbb[:], lgb[:, :, :], lgbb, reads=[lgb], writes=[lgbb])
        kb.dma('sp', lamb[:], llam[:, :, :], lamb, reads=[llam], writes=[lamb])
        kb.dma('sp', selb[:], sel[:, :], selb, reads=[sel], writes=[selb])
        chalf = kb.sb(st, "chalf", [128, 16, 2], F32)
        hbias = kb.sb(st, "hbias", [128, 16, 4], F32)
        kb.op('act', lambda g: g.activation(out=chalf[:], in_=lamb[:], func=AF.Exp, scale=-1.0), reads=[lamb], writes=[chalf])
        kb.op('act', lambda g: g.activation(out=chalf[:], in_=chalf[:], func=AF.Ln, bias=1.0, scale=1.0), reads=[chalf], writes=[chalf])
        kb.op('dve', lambda g: g.tensor_scalar(out=chalf[:], in0=chalf[:], scalar1=-4.0, scalar2=None, op0=ALU.mult),
              reads=[chalf], writes=[chalf])
        kb.op('dve', lambda g: g.tensor_scalar(out=hbias[:], in0=lgbb[:], scalar1=0.5, scalar2=None, op0=ALU.mult),
              reads=[lgbb], writes=[hbias])
        rbuf = kb.sb(st, "rbuf", [128, SEQ], F32)
        abuf = kb.sb(st, "abuf", [128, SEQ], F32)
        ubuf = kb.sb(st, "ubuf", [128, SEQ], F32)
        PF = 1024
        prep_ = Pool(kb, st, "fpre", [128, PF + 3], F32, 3)
        xcvp = Pool(kb, st, "fxcv", [128, PF], F32, 3)
        xcvbp = Pool(kb, st, "fxcvb", [128, PF], BF16, 3)
        afp = Pool(kb, st, "faf", [128, PF], F32, 3)
        ufp = Pool(kb, st, "fuf", [128, PF], F32, 3)
        thap = Pool(kb, st, "ftha", [128, PF], F32, 2)
        thxp = Pool(kb, st, "fthx", [128, PF], F32, 2)
        na2p = Pool(kb, st, "fna2", [128, PF], F32, 2)
        vp = Pool(kb, st, "fv", [128, PF], F32, 2)
        hbtp = Pool(kb, st, "fhbt", [128, PF], F32, 2)
        wgsp = Pool(kb, st, "fwgs", [128, 4, 128], F32, 2)
        wgp = Pool(kb, st, "fwg", [128, 4, 128], BF16, 2)
        fstp = Pool(kb, st, "ffst", [128, 4], F32, 2)
        rop = Pool(kb, st, "fro", [128, 2048], F32, 1)
        pga = Pool(kb, st, "pga", [128, 512], F32, 4, psum=True)
        pgx = Pool(kb, st, "pgx", [128, 512], F32, 4, psum=True)

        def lru_front(hd, pre, t0, P):
            pt_ = prep_.get()
            kb.dma('sp', pt_[:, 0:P + 3], pre.t[hd * 128:(hd + 1) * 128, t0:t0 + P + 3], pt_, reads=[pre], writes=[pt_])
            xcv = xcvp.get()
            kb.op('dve', lambda g: g.tensor_scalar(out=xcv[:, 0:P], in0=pt_[:, 0:P], scalar1=lcwb[:, hd, 0:1], scalar2=lcwb[:, hd, 4:5],
                                                   op0=ALU.mult, op1=ALU.add), reads=[pt_, lcwb], writes=[xcv])
            for k in range(1, 4):
                kb.op('dve', lambda g: g.scalar_tensor_tensor(out=xcv[:, 0:P], in0=pt_[:, k:k + P], scalar=lcwb[:, hd, k:k + 1],
                                                              in1=xcv[:, 0:P], op0=ALU.mult, op1=ALU.add),
                      reads=[pt_, lcwb, xcv], writes=[xcv])
            xcvb = xcvbp.get()
            kb.op('dve', lambda g: g.tensor_copy(out=xcvb[:, 0:P], in_=xcv[:, 0:P]), reads=[xcv], writes=[xcvb])
            return (xcv, xcvb)

        def lru_back(hd, wg, xcv, P, dsts):
            xcv, xcvb = xcv
            sqjobs = []
            for d in range(2):
                tha = thap.get()
                thx = thxp.get()
                for q in range((P + 511) // 512):
                    n = min(512, P - q * 512)
                    cs = slice(q * 512, q * 512 + n)
                    pa = pga.get()
                    px = pgx.get()
                    kb.op('pe', lambda g: g.matmul(pa[:, 0:n], wg[:, 2 * d, :], xcvb[:, cs], start=True, stop=True),
                          reads=[wg, xcvb], writes=[pa])
                    kb.op('pe', lambda g: g.matmul(px[:, 0:n], wg[:, 2 * d + 1, :], xcvb[:, cs], start=True, stop=True),
                          reads=[wg, xcvb], writes=[px])
                    kb.op('act', lambda g: g.activation(out=tha[:, cs], in_=pa[:, 0:n], func=AF.Tanh,
                                                        bias=hbias[:, hd, 2 * d:2 * d + 1], scale=0.5), reads=[pa, hbias], writes=[tha])
                    kb.op('act', lambda g: g.activation(out=thx[:, cs], in_=px[:, 0:n], func=AF.Tanh,
                                                        bias=hbias[:, hd, 2 * d + 1:2 * d + 2], scale=0.5), reads=[px, hbias], writes=[thx])
                a_ap, u_ap, dbufs = dsts[d]
                kb.op('act', lambda g: g.activation(out=a_ap, in_=tha[:, 0:P], func=AF.Exp, bias=chalf[:, hd, d:d + 1],
                                                    scale=chalf[:, hd, d:d + 1]), reads=[tha, chalf], writes=dbufs)
                na2 = na2p.get()
                kb.op('act', lambda g: g.activation(out=na2[:, 0:P], in_=a_ap, func=AF.Square), reads=dbufs, writes=[na2])
                sqjobs.append((na2, thx, u_ap, dbufs))
            for (na2, thx, u_ap, dbufs) in sqjobs:
                kb.op('act', lambda g: g.activation(out=na2[:, 0:P], in_=na2[:, 0:P], func=AF.Sqrt, bias=0.25, scale=-0.25),
                      reads=[na2], writes=[na2])
            for (na2, thx, u_ap, dbufs) in sqjobs:
                v = vp.get()
                kb.op('dve', lambda g: g.scalar_tensor_tensor(out=v[:, 0:P], in0=thx[:, 0:P], scalar=1.0, in1=xcv[:, 0:P],
                                                              op0=ALU.add, op1=ALU.mult), reads=[thx, xcv], writes=[v])
                kb.op('dve', lambda g: g.tensor_tensor(out=u_ap, in0=v[:, 0:P], in1=na2[:, 0:P], op=ALU.mult),
                      reads=[v, na2], writes=dbufs)

        lgw_v = lgw.t
        npc = (NT_E * 512) // PF
        jobs = []
        for hd in range(NH_F):
            jobs.append((hd, 'ctx', 0))
            for p in (range(npc) if hd % 2 == 0 else range(npc - 1, -1, -1)):
                jobs.append((hd, 'lat', p))
        state = {}

        def front(job):
            hd, kind, p = job
            if kind == 'ctx':
                wgs = wgsp.get()
                kb.dma('sp', wgs[:], lgw_v[hd], wgs, reads=[lgw], writes=[wgs])
                wg = wgp.get()
                kb.op('act', lambda g: g.copy(out=wg[:], in_=wgs[:]), reads=[wgs], writes=[wg])
                state[('wg', hd)] = wg
                state[('fst', hd)] = fstp.get()
                return lru_front(hd, pre_xr_c, 0, CTX)
            return lru_front(hd, pre_xr, p * PF, PF)

        def back(job, xcv):
            hd, kind, p = job
            wg = state[('wg', hd)]
            fst = state[('fst', hd)]
            dp = hd % 2
            ds = 1 - dp
            if kind == 'ctx':
                af = afp.get(); uf = ufp.get(); ab = afp.get(); ub = ufp.get()
                lru_back(hd, wg, xcv, CTX, [(af[:, 0:CTX], uf[:, 0:CTX], [af, uf]), (ab[:, 0:CTX], ub[:, 0:CTX], [ab, ub])])
                hb = hbtp.get()
                kb.op('dve', lambda g: g.tensor_tensor_scan(out=hb[:, 0:CTX], data0=af[:, 0:CTX], data1=uf[:, 0:CTX], initial=0.0,
                                                            op0=ALU.mult, op1=ALU.add), reads=[af, uf], writes=[hb])
                kb.op('act', lambda g: g.copy(out=fst[:, 0:1], in_=hb[:, CTX - 1:CTX]), reads=[hb], writes=[fst])
                hb2 = hbtp.get()
                kb.op('dve', lambda g: g.tensor_tensor_scan(out=hb2[:, 0:CTX][:, ::-1], data0=ab[:, 0:CTX][:, ::-1],
                                                            data1=ub[:, 0:CTX][:, ::-1], initial=0.0, op0=ALU.mult, op1=ALU.add),
                      reads=[ab, ub], writes=[hb2])
                kb.op('act', lambda g: g.copy(out=fst[:, 1:2], in_=hb2[:, 0:1]), reads=[hb2], writes=[fst])
                return
            cs = slice(p * PF, (p + 1) * PF)
            af = afp.get(); uf = ufp.get()
            dsts = [None, None]
            dsts[dp] = (af[:], uf[:], [af, uf])
            dsts[ds] = (abuf[:, cs], ubuf[:, cs], [abuf.part(p), ubuf.part(p)])
            lru_back(hd, wg, xcv, PF, dsts)
            if dp == 0:
                init = fst[:, 0:1] if p == 0 else rbuf[:, p * PF - 1:p * PF]
                rd = [af, uf, fst] + ([rbuf.part(p - 1)] if p > 0 else [])
                kb.op('dve', lambda g: g.tensor_tensor_scan(out=rbuf[:, cs], data0=af[:], data1=uf[:], initial=init,
                                                            op0=ALU.mult, op1=ALU.add), reads=rd, writes=[rbuf.part(p)])
                last = (p == npc - 1)
            else:
                init = fst[:, 1:2] if p == npc - 1 else rbuf[:, (p + 1) * PF:(p + 1) * PF + 1]
                rd = [af, uf, fst] + ([rbuf.part(p + 1)] if p < npc - 1 else [])
                kb.op('dve', lambda g: g.tensor_tensor_scan(out=rbuf[:, cs][:, ::-1], data0=af[:][:, ::-1], data1=uf[:][:, ::-1],
                                                            initial=init, op0=ALU.mult, op1=ALU.add), reads=rd, writes=[rbuf.part(p)])
                last = (p == 0)
            if last:
                prev = None
                order2 = list(range(npc - 1, -1, -1)) if ds == 1 else list(range(npc))
                for p2 in order2:
                    cs2 = slice(p2 * PF, (p2 + 1) * PF)
                    hb = hbtp.get()
                    rd = [abuf.part(p2), ubuf.part(p2), fst] + ([prev] if prev is not None else [])
                    if ds == 1:
                        init = fst[:, 1:2] if prev is None else prev[:, 0:1]
                        kb.op('dve', lambda g: g.tensor_tensor_scan(out=hb[:][:, ::-1], data0=abuf[:, cs2][:, ::-1], data1=ubuf[:, cs2][:, ::-1],
                                                                    initial=init, op0=ALU.mult, op1=ALU.add), reads=rd, writes=[hb])
                    else:
                        init = fst[:, 0:1] if prev is None else prev[:, PF - 1:PF]
                        kb.op('dve', lambda g: g.tensor_tensor_scan(out=hb[:], data0=abuf[:, cs2], data1=ubuf[:, cs2],
                                                                    initial=init, op0=ALU.mult, op1=ALU.add), reads=rd, writes=[hb])
                    kb.op('dve', lambda g: g.tensor_tensor(out=rbuf[:, cs2], in0=rbuf[:, cs2], in1=hb[:], op=ALU.add),
                          reads=[rbuf.part(p2), hb], writes=[rbuf.part(p2)])
                    prev = hb
                ro = rop.get()
                r4 = rbuf[:].rearrange("p (j b i) -> p j b i", j=64, b=4)
                rov = ro[:].rearrange("p (i j) -> p j i", j=64)
                kb.op('dve', lambda g: g.tensor_scalar(out=rov, in0=r4[:, :, 0, :], scalar1=selb[:, 0:1], scalar2=None, op0=ALU.mult),
                      reads=[rbuf, selb], writes=[ro])
                for b_ in range(1, 4):
                    kb.op('dve', lambda g: g.scalar_tensor_tensor(out=rov, in0=r4[:, :, b_, :], scalar=selb[:, b_:b_ + 1], in1=rov,
                                                                  op0=ALU.mult, op1=ALU.add), reads=[rbuf, selb, ro], writes=[ro])
                kb.dma('pool', r_own.t[hd * 128:(hd + 1) * 128, :], ro[:], ro, reads=[ro], writes=[r_own])

        cur = front(jobs[0])
        for ji in range(len(jobs)):
            nxt = front(jobs[ji + 1]) if ji + 1 < len(jobs) else None
            back(jobs[ji], cur)
            cur = nxt
        kb.barrier()
        kb.release(prep_.bufs + wgsp.bufs + rop.bufs + [lcwb, lgbb, lamb, selb])

    def fm_view(scr):
        return scr.t.rearrange("(kc p) t -> p kc t", p=128)

    with ExitStack() as st:
        st.enter_context(nc.named_scope("st_G"))
        pools = norm_pools(st, "G", nxn=2, npt=2, nxt=4)
        hTo = kb.sb(st, "hTo", [128, NKC, 2048], BF16)
        build_hT(pools, xo, hTo, 16, 0, 1)
        wstg = Pool(kb, st, "Gwstg", [128, NKC, 128], F32, 3)
        pg = Pool(kb, st, "Gpg", [128, 512], F32, 4, psum=True)
        wzp = Pool(kb, st, "Gwz", [128, NKC, 512], BF16, 2)
        szp = Pool(kb, st, "Gsz", [128, 512], F32, 3)
        for cg in range(4):
            wz = wzp.get()
            load_weight(wstg, w_in, OZ + cg * 512, 512, wz, 0)
            for tt in range(16):
                pp = pg.get()
                for kc in range(NKC):
                    kb.op('pe', lambda g: g.matmul(pp[:], hTo[:, kc, tt * 128:(tt + 1) * 128], wz[:, kc, :],
                                                   start=(kc == 0), stop=(kc == NKC - 1)), reads=[hTo.part(kc), wz], writes=[pp])
                sz = szp.get()
                kb.op('act', lambda g: g.activation(out=sz[:], in_=pp[:], func=AF.Silu), reads=[pp], writes=[sz])
                kb.dma('pool', sz_scr.t[tt * 128:(tt + 1) * 128, cg * 512:(cg + 1) * 512], sz[:], sz, reads=[sz], writes=[sz_scr])
        bgb = kb.sb(st, "bgb", [128, 32], F32)
        kb.dma('sp', bgb[:], bgate[:, :], bgb, reads=[bgate], writes=[bgb])
        wbp = Pool(kb, st, "Gwb", [128, NKC, 128], BF16, 2)
        gop = Pool(kb, st, "Ggo", [128, 512], BF16, 3)
        for cb in range(32):
            wb = wbp.get()
            load_weight(wstg, w_gate, cb * 128, 128, wb, 0)
            for tg in range(4):
                pp = pg.get()
                for kc in range(NKC):
                    kb.op('pe', lambda g: g.matmul(pp[:], wb[:, kc, :], hTo[:, kc, tg * 512:(tg + 1) * 512],
                                                   start=(kc == 0), stop=(kc == NKC - 1)), reads=[hTo.part(kc), wb], writes=[pp])
                go = gop.get()
                kb.op('act', lambda g: g.activation(out=go[:], in_=pp[:], func=AF.Sigmoid, bias=bgb[:, cb:cb + 1], scale=1.0),
                      reads=[pp, bgb], writes=[go])
                kb.dma('pool', gT_scr.t[cb * 128:(cb + 1) * 128, tg * 512:(tg + 1) * 512], go[:], go, reads=[go], writes=[gT_scr])
        rlp = Pool(kb, st, "Grl", [128, 512], F32, 2)
        t1p = Pool(kb, st, "Gt1", [128, 512], F32, 2)
        t2p = Pool(kb, st, "Gt2", [128, 512], F32, 2)
        for cb in range(16):
            wb = wbp.get()
            load_weight(wstg, w_in, OYR + cb * 128, 128, wb, 0)
            for tg in range(4):
                ts_ = slice(tg * 512, (tg + 1) * 512)
                pp = pg.get()
                for kc in range(NKC):
                    kb.op('pe', lambda g: g.matmul(pp[:], wb[:, kc, :], hTo[:, kc, ts_],
                                                   start=(kc == 0), stop=(kc == NKC - 1)), reads=[hTo.part(kc), wb], writes=[pp])
                rl = rlp.get()
                kb.dma('sp', rl[:], r_own.t[cb * 128:(cb + 1) * 128, ts_], rl, reads=[r_own], writes=[rl])
                t1 = t1p.get()
                t2 = t2p.get()
                kb.op('act', lambda g: g.activation(out=t1[:], in_=pp[:], func=AF.Square), reads=[pp], writes=[t1])
                kb.op('dve', lambda g: g.tensor_scalar(out=t1[:], in0=t1[:], scalar1=0.044715, scalar2=1.0, op0=ALU.mult, op1=ALU.add),
                      reads=[t1], writes=[t1])
                kb.op('dve', lambda g: g.tensor_tensor(out=t1[:], in0=t1[:], in1=pp[:], op=ALU.mult), reads=[t1, pp], writes=[t1])
                kb.op('act', lambda g: g.activation(out=t1[:], in_=t1[:], func=AF.Tanh, scale=0.7978845608028654), reads=[t1], writes=[t1])
                kb.op('dve', lambda g: g.scalar_tensor_tensor(out=t2[:], in0=t1[:], scalar=1.0, in1=pp[:], op0=ALU.add, op1=ALU.mult),
                      reads=[t1, pp], writes=[t2])
                go = gop.get()
                kb.op('dve', lambda g: g.scalar_tensor_tensor(out=go[:], in0=t2[:], scalar=0.5, in1=rl[:], op0=ALU.mult, op1=ALU.mult),
                      reads=[t2, rl], writes=[go])
                kb.dma('pool', rgT_scr.t[cb * 128:(cb + 1) * 128, ts_], go[:], go, reads=[go], writes=[rgT_scr])
        kb.barrier()
        kb.release(pools['xt'].bufs + wstg.bufs + szp.bufs + gop.bufs + rlp.bufs + [bgb])

    with ExitStack() as st:
        st.enter_context(nc.named_scope("st_G1b"))
        pools = norm_pools(st, "N")
        selb = kb.sb(st, "Nsel", [128, 4], F32)
        kb.dma('sp', selb[:], sel[:, :], selb, reads=[sel], writes=[selb])
        yop = Pool(kb, st, "Nyo", [128, D], F32, 2)
        ynp = Pool(kb, st, "Nyn", [128, NKC, 128], BF16, 2)
        for tt in range(16):
            szt = pools['xt'].get()
            kb.dma('sp', szt[:], sz_scr.t[tt * 128:(tt + 1) * 128, :], szt, reads=[sz_scr], writes=[szt])
            yo = yop.get()
            kb.dma('sp', yo[:], y_own.t[tt * 128:(tt + 1) * 128, :], yo, reads=[y_own], writes=[yo])
            kb.op('dve', lambda g: g.tensor_tensor(out=yo[:], in0=yo[:], in1=szt[:], op=ALU.mult), reads=[yo, szt], writes=[yo])
            yn = ynp.get()
            norm_T(pools, yo, yn, 0, lambda kc: nrmb[:, 2, kc:kc + 1], None, [nrmb])
            kb.dma('pool', fm_view(ynT_scr)[:, :, tt * 128:(tt + 1) * 128], yn[:], yn, reads=[yn], writes=[ynT_scr])
        kb.barrier()
        kb.release(pools['xt'].bufs + yop.bufs + ynp.bufs + [selb])

    with ExitStack() as st:
        st.enter_context(nc.named_scope("st_H"))
        ynT = kb.sb(st, "HynT", [128, NKC, 2048], BF16)
        rgT = kb.sb(st, "HrgT", [128, NKC, 2048], BF16)
        for kc in range(NKC):
            kb.dma('sp', ynT[:, kc, :], ynT_scr.t[kc * 128:(kc + 1) * 128, :], ynT, reads=[ynT_scr], writes=[ynT])
            kb.dma('sp', rgT[:, kc, :], rgT_scr.t[kc * 128:(kc + 1) * 128, :], rgT, reads=[rgT_scr], writes=[rgT])
        wstg = Pool(kb, st, "Hwstg", [128, NKC, 128], F32, 3)
        wbp = Pool(kb, st, "Hwb", [128, NKC, 128], BF16, 4)
        pgs = Pool(kb, st, "Hpgs", [128, 512], F32, 3, psum=True)
        pgr = Pool(kb, st, "Hpgr", [128, 512], F32, 3, psum=True)
        gsp = Pool(kb, st, "Hgs", [128, 512], BF16, 2)
        grp = Pool(kb, st, "Hgr", [128, 512], BF16, 2)
        m1p = Pool(kb, st, "Hm1", [128, 512], F32, 2)
        m2p = Pool(kb, st, "Hm2", [128, 512], F32, 2)
        mop = Pool(kb, st, "Hmo", [128, 512], BF16, 3)
        for cb in range(16):
            ws = wbp.get()
            load_weight(wstg, w_out_ssd, cb * 128, 128, ws, 0)
            wr = wbp.get()
            load_weight(wstg, w_out_lru, cb * 128, 128, wr, 0)
            for tg in range(4):
                ts_ = slice(tg * 512, (tg + 1) * 512)
                p1 = pgs.get()
                p2 = pgr.get()
                for kc in range(NKC):
                    kb.op('pe', lambda g: g.matmul(p1[:], ws[:, kc, :], ynT[:, kc, ts_], start=(kc == 0), stop=(kc == NKC - 1)),
                          reads=[ws, ynT], writes=[p1])
                for kc in range(NKC):
                    kb.op('pe', lambda g: g.matmul(p2[:], wr[:, kc, :], rgT[:, kc, ts_], start=(kc == 0), stop=(kc == NKC - 1)),
                          reads=[wr, rgT], writes=[p2])
                gs_ = gsp.get()
                gr_ = grp.get()
                kb.dma('sp', gs_[:], gT_scr.t[cb * 128:(cb + 1) * 128, ts_], gs_, reads=[gT_scr], writes=[gs_])
                kb.dma('sp', gr_[:], gT_scr.t[D + cb * 128:D + (cb + 1) * 128, ts_], gr_, reads=[gT_scr], writes=[gr_])
                m1 = m1p.get()
                m2 = m2p.get()
                kb.op('dve', lambda g: g.tensor_tensor(out=m1[:], in0=p1[:], in1=gs_[:], op=ALU.mult), reads=[p1, gs_], writes=[m1])
                kb.op('dve', lambda g: g.tensor_tensor(out=m2[:], in0=p2[:], in1=gr_[:], op=ALU.mult), reads=[p2, gr_], writes=[m2])
                mo = mop.get()
                kb.op('dve', lambda g: g.tensor_tensor(out=mo[:], in0=m1[:], in1=m2[:], op=ALU.add), reads=[m1, m2], writes=[mo])
                kb.dma('pool', mT_scr.t[cb * 128:(cb + 1) * 128, ts_], mo[:], mo, reads=[mo], writes=[mT_scr])
        kb.barrier()
        kb.release([ynT, rgT] + wstg.bufs + gsp.bufs + grp.bufs + mop.bufs)

    def gate_bcast(st, tag):
        gmb = kb.sb(st, tag + "gmb", [128, 2, D], F32)
        for a_ in range(2):
            kb.dma('sp', gmb[:, a_, :], gvec.t[a_:a_ + 1, :].partition_broadcast(128), gmb, reads=[gvec], writes=[gmb])
        return gmb

    def tm_gemm_residual(tag, AT, nkc, w_dram, res_src, gidx, dst, tok0, ntt, wstg, cwid=512):
        with ExitStack() as st2:
            st2.enter_context(nc.named_scope("st_" + tag))
            gmb = gate_bcast(st2, tag)
            wsp = Pool(kb, st2, tag + "ws", [128, nkc, cwid], BF16, 2)
            pgp = Pool(kb, st2, tag + "pg", [128, 512], F32, 4, psum=True)
            xrp = Pool(kb, st2, tag + "xr", [128, 512], F32, 3)
            xop = Pool(kb, st2, tag + "xo", [128, 512], F32, 3)
            for cg in range(D // cwid):
                cs = slice(cg * cwid, (cg + 1) * cwid)
                wsl = wsp.get()
                k0 = 0
                while k0 < nkc:
                    nk = min(16, nkc - k0)
                    load_weight(wstg, w_dram, cg * cwid, cwid, wsl, 0, row_lo=k0 * 128, nkc=nk, kc_dst=k0)
                    k0 += nk
                for tt in range(ntt):
                    pp = pgp.get()
                    for kc in range(nkc):
                        kb.op('pe', lambda g: g.matmul(pp[:, 0:cwid], AT[:, kc, tt * 128:(tt + 1) * 128], wsl[:, kc, :],
                                                       start=(kc == 0), stop=(kc == nkc - 1)), reads=[AT, wsl], writes=[pp])
                    xr_ = xrp.get()
                    r0 = tok0 + tt * 128
                    kb.dma('sp', xr_[:, 0:cwid], res_src.t[r0:r0 + 128, cs], xr_, reads=[res_src], writes=[xr_])
                    xo_ = xop.get()
                    kb.op('dve', lambda g: g.tensor_tensor(out=xo_[:, 0:cwid], in0=pp[:, 0:cwid], in1=gmb[:, gidx, cs], op=ALU.mult),
                          reads=[pp, gmb], writes=[xo_])
                    kb.op('dve', lambda g: g.tensor_tensor(out=xo_[:, 0:cwid], in0=xo_[:, 0:cwid], in1=xr_[:, 0:cwid], op=ALU.add),
                          reads=[xo_, xr_], writes=[xo_])
                    kb.dma('pool', dst.t[r0:r0 + 128, cs], xo_[:, 0:cwid], xo_, reads=[xo_], writes=[dst])
            kb.barrier()
            kb.release([gmb] + xrp.bufs + xop.bufs)

    with ExitStack() as st:
        mT = kb.sb(st, "HmT", [128, NKC, 2048], BF16)
        for kc in range(NKC):
            kb.dma('sp', mT[:, kc, :], mT_scr.t[kc * 128:(kc + 1) * 128, :], mT, reads=[mT_scr], writes=[mT])
        wstg = Pool(kb, st, "H3wstg", [128, NKC, 128], F32, 3)
        tm_gemm_residual("H3", mT, NKC, w_o, xo, 0, x1_scr, 0, 16, wstg)
        kb.release([mT] + wstg.bufs)

    with ExitStack() as st:
        st.enter_context(nc.named_scope("st_I"))
        pools = norm_pools(st, "I", nxn=2, npt=2, nxt=4)
        h2T = kb.sb(st, "h2T", [128, NKC, 2048], BF16)
        build_hT(pools, x1_scr, h2T, 16, 4, 5)
        wstg = Pool(kb, st, "Iwstg", [128, NKC, 128], F32, 3)
        wbp = Pool(kb, st, "Iwb", [128, NKC, 128], BF16, 4)
        pgg = Pool(kb, st, "Ipgg", [128, 512], F32, 3, psum=True)
        pgu = Pool(kb, st, "Ipgu", [128, 512], F32, 3, psum=True)
        sgp = Pool(kb, st, "Isg", [128, 512], F32, 3)
        aop = Pool(kb, st, "Iao", [128, 512], BF16, 3)
        for cb in range(FFN // 128):
            wg_ = wbp.get()
            load_weight(wstg, w13, cb * 128, 128, wg_, 0)
            wu_ = wbp.get()
            load_weight(wstg, w13, FFN + cb * 128, 128, wu_, 0)
            for tg in range(4):
                ts_ = slice(tg * 512, (tg + 1) * 512)
                p1 = pgg.get()
                p2 = pgu.get()
                for kc in range(NKC):
                    kb.op('pe', lambda g: g.matmul(p1[:], wg_[:, kc, :], h2T[:, kc, ts_], start=(kc == 0), stop=(kc == NKC - 1)),
                          reads=[wg_, h2T.part(kc)], writes=[p1])
                for kc in range(NKC):
                    kb.op('pe', lambda g: g.matmul(p2[:], wu_[:, kc, :], h2T[:, kc, ts_], start=(kc == 0), stop=(kc == NKC - 1)),
                          reads=[wu_, h2T.part(kc)], writes=[p2])
                sg = sgp.get()
                kb.op('act', lambda g: g.activation(out=sg[:], in_=p1[:], func=AF.Silu), reads=[p1], writes=[sg])
                ao = aop.get()
                kb.op('dve', lambda g: g.tensor_tensor(out=ao[:], in0=p2[:], in1=sg[:], op=ALU.mult), reads=[p2, sg], writes=[ao])
                kb.dma('pool', aT_scr.t[cb * 128:(cb + 1) * 128, ts_], ao[:], ao, reads=[ao], writes=[aT_scr])
        kb.barrier()
        kb.release(pools['xt'].bufs + wstg.bufs + aop.bufs)

    NKF = FFN // 128
    for half in range(2):
        with ExitStack() as st:
            aT = kb.sb(st, "IaT%d" % half, [128, NKF, 1024], BF16)
            for kc in range(NKF):
                kb.dma('sp', aT[:, kc, :], aT_scr.t[kc * 128:(kc + 1) * 128, half * 1024:(half + 1) * 1024], aT, reads=[aT_scr], writes=[aT])
            wstg = Pool(kb, st, "I2wstg%d" % half, [128, NKC, 128], F32, 3)
            tm_gemm_residual("I2%d" % half, aT, NKF, w2, x1_scr, 1, x2_scr, half * 1024, 8, wstg, cwid=256)
            kb.release([aT] + wstg.bufs)

    with ExitStack() as st:
        st.enter_context(nc.named_scope("st_J"))
        fnb = kb.sb(st, "fnb", [128, D], F32)
        kb.dma('sp', fnb[:], fin_norm.t.partition_broadcast(128), fnb, reads=[fin_norm], writes=[fnb])
        xtp = Pool(kb, st, "Jxt", [128, D], F32, 3)
        sqp = Pool(kb, st, "Jsq", [128, D], BF16, 1)
        ssp = Pool(kb, st, "Jss", [128, 4], F32, 2)
        otp = Pool(kb, st, "Jot", [128, D], F32, 3)
        for tt in range(16):
            xt = xtp.get()
            kb.dma('sp', xt[:], x2_scr.t[tt * 128:(tt + 1) * 128, :], xt, reads=[x2_scr], writes=[xt])
            sq = sqp.get()
            ss = ssp.get()
            kb.op('dve', lambda g: g.memset(ss[:], 0.0), writes=[ss])
            kb.op('act', lambda g: g.activation(out=sq[:], in_=xt[:], func=AF.Square, accum_out=ss[:, 0:1]), reads=[xt], writes=[sq, ss])
            kb.op('dve', lambda g: g.tensor_scalar(out=ss[:, 1:2], in0=ss[:, 0:1], scalar1=1.0 / D, scalar2=EPS,
                                                   op0=ALU.mult, op1=ALU.add), reads=[ss], writes=[ss])
            kb.op('act', lambda g: g.activation(out=ss[:, 3:4], in_=ss[:, 1:2], func=AF.Sqrt), reads=[ss], writes=[ss])
            kb.op('dve', lambda g: g.reciprocal(out=ss[:, 2:3], in_=ss[:, 3:4]), reads=[ss], writes=[ss])
            ot = otp.get()
            kb.op('dve', lambda g: g.scalar_tensor_tensor(out=ot[:], in0=xt[:], scalar=ss[:, 2:3], in1=fnb[:], op0=ALU.mult, op1=ALU.mult),
                  reads=[xt, ss, fnb], writes=[ot])
            kb.dma('pool', out.t[tt * 128:(tt + 1) * 128, :], ot[:], ot, reads=[ot], writes=[out])
        kb.barrier()
        kb.release([fnb] + xtp.bufs + otp.bufs)

    kb.barrier()
    gs.close()
    return nc


def _fm(v, n=16):
    return np.ascontiguousarray(np.asarray(v, np.float32).reshape(n, 128).T)


def _endsel():
    e = np.zeros((128, 2, 128), np.float32)
    e[127, 0, :] = 1.0
    e[0, 1, :] = 1.0
    return e


def prep_core(inp, core):
    b, k = core // 4, core % 4
    f = lambda a: np.ascontiguousarray(np.asarray(a, np.float32))
    m = {
        "xb": f(inp["x"][b]), "xo": f(inp["x"][b, 2048 * k:2048 * (k + 1)]), "ctxb": f(inp["ctx"][b]),
        "cvec": f(np.stack([_fm(inp["c"][b]), _fm(inp["c_ctx"])], axis=-1)),
        "w_ada": f(inp["w_ada"][0]), "b_ada": _fm(inp["b_ada"][0], 96),
        "nrm": f(np.stack([_fm(inp["norm_mix"][0]), _fm(inp["norm_ffn"][0]), _fm(inp["ssd_norm"][0])], axis=1)),
        "fin_norm": f(inp["final_norm"][None, :]), "ident": np.eye(128, dtype=np.float32), "w_in": f(inp["w_in"][0]),
        "cw_ssd": f(np.concatenate([inp["ssd_conv_w"][0], inp["ssd_conv_b"][0][None]], 0).reshape(5, 24, 128).transpose(2, 1, 0)),
        "dtp": f(np.tile(np.stack([inp["ssd_dt_bias"][0], inp["ssd_a_log"][0]], -1).transpose(1, 0, 2), (4, 1, 1))),
        "dskb": f(np.tile(inp["ssd_d"][0][None, :], (128, 1))),
        "masks": f(np.stack([np.triu(np.ones((128, 128), np.float32)), np.tril(np.ones((128, 128), np.float32))], 1)),
        "sel": f(np.tile(np.eye(4, dtype=np.float32)[k][None, :], (128, 1))),
        "cmask": f(np.tile(np.stack([(np.arange(64) < 16 * k), (np.arange(64) >= 16 * (k + 1))], 0).astype(np.float32)[None], (128, 1, 1))),
        "endsel": _endsel(),
        "dmask": np.eye(32, dtype=np.float32).reshape(32, 4, 8),
        "lcw": f(np.concatenate([inp["lru_conv_w"][0], inp["lru_conv_b"][0][None]], 0).reshape(5, 16, 128).transpose(2, 1, 0)),
        "lgw": f(np.stack([inp["lru_w_a"][0][0], inp["lru_w_x"][0][0], inp["lru_w_a"][0][1], inp["lru_w_x"][0][1]], axis=2)),
        "lgb": f(np.stack([inp["lru_b_a"][0][0], inp["lru_b_x"][0][0], inp["lru_b_a"][0][1], inp["lru_b_x"][0][1]], axis=-1)
                 .reshape(16, 128, 4).transpose(1, 0, 2)),
        "llam": f(np.asarray(inp["lru_lambda"][0]).T.reshape(16, 128, 2).transpose(1, 0, 2)),
        "w_out_ssd": f(inp["w_out_ssd"][0]), "w_out_lru": f(inp["w_out_lru"][0]), "w_gate": f(inp["w_gate"][0]),
        "bgate": _fm(inp["b_gate"][0], 32), "w_o": f(inp["w_o"][0]), "w13": f(inp["ffn_w13"][0]), "w2": f(inp["ffn_w2"][0]),
    }
    return m


def kernel(**inputs):
    inp = {k_: np.asarray(v) for k_, v in inputs.items()}
    nc = build_program(None)
    in_maps = [prep_core(inp, c) for c in range(8)]
    res = run_bass_kernel_spmd(nc, in_maps, core_ids=list(range(8)))
    out = np.zeros((2, SEQ, D), np.float32)
    for c in range(8):
        b, k = c // 4, c % 4
        out[b, 2048 * k:2048 * (k + 1)] = res.results[c]["out"]
    return out
```

```python
import numpy as np
from contextlib import ExitStack
import concourse.bass as bass
import concourse.mybir as mybir
from concourse.bass_utils import run_bass_kernel_spmd

F32 = mybir.dt.float32
BF16 = mybir.dt.bfloat16
AF = mybir.ActivationFunctionType
ALU = mybir.AluOpType
AX = mybir.AxisListType

D = 2048
SEQ = 8192
CTX = 256
NKC = 16
FFN = 5632
EPS = 1e-6
OZ, OXS, OB, OC, ODT, OXR, OYR = 0, 2048, 4096, 4608, 5120, 5184, 7232


class Buf:
    def __init__(self, t, parent=None):
        self.t = t
        self.w = {}
        self.r = {}
        self.dsem = None
        self.dcnt = 0
        self.parent = parent
        self.kids = {}

    def __getitem__(self, k):
        return self.t[k]

    def part(self, key):
        if key not in self.kids:
            self.kids[key] = Buf(self.t, parent=self)
        return self.kids[key]

    def parts(self, keys):
        return [self.part(k) for k in keys]

    def wsets(self):
        out = [self.w]
        if self.parent is not None:
            out.append(self.parent.w)
        out.extend(k.w for k in self.kids.values())
        return out

    def rsets(self):
        out = [self.r]
        if self.parent is not None:
            out.append(self.parent.r)
        out.extend(k.r for k in self.kids.values())
        return out


class KB:
    def __init__(self, nc, stack):
        self.nc = nc
        self.gstack = stack
        self.eng = {'pe': nc.tensor, 'act': nc.scalar, 'dve': nc.vector, 'pool': nc.gpsimd, 'sp': nc.sync}
        self.sem = {}
        self.cnt = {}
        self.allsems = {}
        for e in ['pe', 'act', 'dve', 'pool']:
            self.sem[e] = stack.enter_context(nc.semaphore("s_" + e))
            self.cnt[e] = 0
        self.waited = {e: {} for e in self.eng}
        self.free_dsems = []
        self.nd = 0
        self.rr = 0

    def sb(self, st, name, shape, dt):
        return Buf(st.enter_context(self.nc.sbuf_tensor(name, list(shape), dt)))

    def ps(self, st, name, shape, dt=F32):
        return Buf(st.enter_context(self.nc.psum_tensor(name, list(shape), dt)))

    def _dsem(self, b):
        if b.dsem is None:
            if self.free_dsems:
                b.dsem, b.dcnt = self.free_dsems.pop()
            else:
                self.nd += 1
                b.dsem = self.gstack.enter_context(self.nc.semaphore("d%d" % self.nd))
                b.dcnt = 0
        return b.dsem

    def release(self, bufs):
        for b in bufs:
            if b.dsem is not None:
                self.free_dsems.append((b.dsem, b.dcnt))
                b.dsem = None

    def _deps(self, e, reads, writes):
        deps = {}
        for b in reads:
            for ws in b.wsets():
                for s, v in ws.items():
                    if deps.get(s, 0) < v:
                        deps[s] = v
        for b in writes:
            for ws in b.wsets() + b.rsets():
                for s, v in ws.items():
                    if deps.get(s, 0) < v:
                        deps[s] = v
        own = self.sem.get(e)
        wd = self.waited[e]
        for s, v in deps.items():
            if e == 'pe' and s is own:
                continue
            if wd.get(s, 0) < v:
                self.eng[e].wait_ge(s, v)
                wd[s] = v

    def op(self, e, fn, reads=(), writes=()):
        self._deps(e, reads, writes)
        ins = fn(self.eng[e])
        self.cnt[e] += 1
        s = self.sem[e]
        ins.then_inc(s, 1)
        v = self.cnt[e]
        self.allsems[s] = v
        for b in reads:
            b.r[s] = v
        for b in writes:
            b.w[s] = v
        return ins

    def dma(self, q, out, in_, sembuf, reads=(), writes=(), **kw):
        self._deps(q, reads, writes)
        ins = self.eng[q].dma_start(out=out, in_=in_, **kw)
        s = self._dsem(sembuf)
        sembuf.dcnt += 16
        ins.then_inc(s, 16)
        v = sembuf.dcnt
        self.allsems[s] = v
        for b in reads:
            b.r[s] = v
        for b in writes:
            b.w[s] = v
        return ins

    def barrier(self, engines=None):
        for e in (engines or list(self.eng)):
            wd = self.waited[e]
            for s, v in self.allsems.items():
                if wd.get(s, 0) < v:
                    self.eng[e].wait_ge(s, v)
                    wd[s] = v

    def alt(self, choices=('act', 'dve')):
        self.rr += 1
        return choices[self.rr % len(choices)]


class Pool:
    def __init__(self, kb, st, name, shape, dt, n, psum=False):
        mk = kb.ps if psum else kb.sb
        self.bufs = [mk(st, "%s%d" % (name, i), shape, dt) for i in range(n)]
        self.i = 0

    def get(self):
        b = self.bufs[self.i % len(self.bufs)]
        self.i += 1
        return b


def evac_affine(kb, e, out_ap, in_ap, scale_ap, bias_ap, reads, writes):
    if e == 'act':
        kb.op('act', lambda g: g.activation(out=out_ap, in_=in_ap, func=AF.Identity, bias=bias_ap, scale=scale_ap),
              reads=reads, writes=writes)
    else:
        kb.op('dve', lambda g: g.tensor_scalar(out=out_ap, in0=in_ap, scalar1=scale_ap, scalar2=bias_ap,
                                               op0=ALU.mult, op1=ALU.add), reads=reads, writes=writes)


def build_program(dbg=None):
    nc = bass.Bass("TRN2", target_bir_lowering=False)
    gs = ExitStack()
    kb = KB(nc, gs)

    def din(name, shape, dt=F32):
        return Buf(nc.dram_tensor(name, list(shape), dt, kind="ExternalInput").ap())

    def dscr(name, shape, dt=F32):
        kind = "ExternalOutput" if (dbg and name in dbg) else "Internal"
        return Buf(nc.dram_tensor(name, list(shape), dt, kind=kind).ap())

    xb = din("xb", [SEQ, D])
    xo = din("xo", [2048, D])
    ctxb = din("ctxb", [CTX, D])
    cvec = din("cvec", [128, NKC, 2])
    w_ada = din("w_ada", [D, 6 * D])
    b_ada = din("b_ada", [128, 96])
    nrm = din("nrm", [128, 3, NKC])
    fin_norm = din("fin_norm", [1, D])
    ident = din("ident", [128, 128])
    w_in = din("w_in", [D, 9280])
    cw_ssd = din("cw_ssd", [128, 24, 5])
    dtp = din("dtp", [128, 2, 2])
    dskb = din("dskb", [128, 32])
    masks = din("masks", [128, 2, 128])
    sel = din("sel", [128, 4])
    cmask = din("cmask", [128, 2, 64])
    endsel = din("endsel", [128, 2, 128])
    dmask = din("dmask", [32, 4, 8])
    lcw = din("lcw", [128, 16, 5])
    lgw = din("lgw", [16, 128, 4, 128])
    lgb = din("lgb", [128, 16, 4])
    llam = din("llam", [128, 16, 2])
    w_out_ssd = din("w_out_ssd", [D, D])
    w_out_lru = din("w_out_lru", [D, D])
    w_gate = din("w_gate", [D, 2 * D])
    bgate = din("bgate", [128, 32])
    w_o = din("w_o", [D, D])
    w13 = din("w13", [D, 2 * FFN])
    w2 = din("w2", [FFN, D])
    out = Buf(nc.dram_tensor("out", [2048, D], F32, kind="ExternalOutput").ap())

    gvec = dscr("gvec", [2, D])
    pre_ssd = dscr("pre_ssd", [3136, SEQ + 4])
    pre_ssd_c = dscr("pre_ssd_c", [3136, CTX + 4])
    xsb_tm = dscr("xsb_tm", [SEQ, 2560], BF16)
    xsb_tm_c = dscr("xsb_tm_c", [CTX, 2560], BF16)
    bc_fm = dscr("bc_fm", [1024, SEQ], BF16)
    bc_fm_c = dscr("bc_fm_c", [1024, CTX], BF16)
    ac_scr = dscr("ac_scr", [2, 4096])
    y_own = dscr("y_own", [2048, D])
    xsb_own = dscr("xsb_own", [2048, 2560], BF16)
    bc_own = dscr("bc_own", [1024, 2048], BF16)
    pdt_own = dscr("pdt_own", [64, 2048])
    pre_xr = dscr("pre_xr", [2048, SEQ + 4])
    pre_xr_c = dscr("pre_xr_c", [2048, CTX + 4])
    r_own = dscr("r_own", [2048, 2048])
    sz_scr = dscr("sz_scr", [2048, D])
    ynT_scr = dscr("ynT_scr", [D, 2048], BF16)
    rgT_scr = dscr("rgT_scr", [D, 2048], BF16)
    gT_scr = dscr("gT_scr", [2 * D, 2048], BF16)
    mT_scr = dscr("mT_scr", [D, 2048], BF16)
    x1_scr = dscr("x1_scr", [2048, D])
    aT_scr = dscr("aT_scr", [FFN, 2048], BF16)
    x2_scr = dscr("x2_scr", [2048, D])

    identb = kb.sb(gs, "identb", [128, 128], F32)
    modT = kb.sb(gs, "modT", [128, 96, 2], F32)
    scsh = kb.sb(gs, "scsh", [128, 6, NKC], F32)
    nrmb = kb.sb(gs, "nrmb", [128, 3, NKC], F32)
    kb.dma('sp', identb[:], ident[:, :], identb, reads=[ident], writes=[identb])
    kb.dma('sp', nrmb[:], nrm[:, :, :], nrmb, reads=[nrm], writes=[nrmb])

    with ExitStack() as st:
        st.enter_context(nc.named_scope("st_A"))
        sv = kb.sb(st, "sv", [128, NKC, 2], F32)
        svs = kb.sb(st, "svs", [128, NKC, 2], F32)
        bad = kb.sb(st, "bad", [128, 96], F32)
        wpool = Pool(kb, st, "wada", [128, NKC, 512], F32, 2)
        pm = kb.ps(st, "pm", [128, 96, 2], F32)
        kb.dma('sp', sv[:], cvec[:, :, :], sv, reads=[cvec], writes=[sv])
        kb.dma('sp', bad[:], b_ada[:, :], bad, reads=[b_ada], writes=[bad])
        kb.op('act', lambda g: g.activation(out=svs[:], in_=sv[:], func=AF.Silu), reads=[sv], writes=[svs])
        wv = w_ada.t.rearrange("(kc p) n -> p kc n", p=128)
        for cg in range(24):
            wt = wpool.get()
            kb.dma('sp', wt[:], wv[:, :, cg * 512:(cg + 1) * 512], wt, reads=[w_ada], writes=[wt])
            for j in range(4):
                cb = cg * 4 + j
                for kc in range(NKC):
                    kb.op('pe', lambda g: g.matmul(pm[:, cb, :], wt[:, kc, j * 128:(j + 1) * 128], svs[:, kc, :],
                                                   start=(kc == 0), stop=(kc == NKC - 1)),
                          reads=[wt, svs], writes=[pm])
        kb.op('dve', lambda g: g.tensor_tensor(out=modT[:], in0=pm[:], in1=bad[:].unsqueeze(2).to_broadcast([128, 96, 2]),
                                               op=ALU.add), reads=[pm, bad], writes=[modT])
        for (dst, nidx, scblk, v) in ((0, 0, 16, 0), (2, 0, 16, 1), (4, 1, 64, 0)):
            kb.op('dve', lambda g: g.scalar_tensor_tensor(out=scsh[:, dst, :], in0=modT[:, scblk:scblk + 16, v], scalar=1.0,
                                                          in1=nrmb[:, nidx, :], op0=ALU.add, op1=ALU.mult),
                  reads=[modT, nrmb], writes=[scsh])
        for (dst, shblk, v) in ((1, 0, 0), (3, 0, 1), (5, 48, 0)):
            kb.op('dve', lambda g: g.tensor_copy(out=scsh[:, dst, :], in_=modT[:, shblk:shblk + 16, v]),
                  reads=[modT], writes=[scsh])
        gv = gvec.t.rearrange("g (kc p) -> p g kc", p=128)
        gtmp = kb.sb(st, "gtmp", [128, 2, NKC], F32)
        kb.op('dve', lambda g: g.tensor_copy(out=gtmp[:, 0, :], in_=modT[:, 32:48, 0]), reads=[modT], writes=[gtmp])
        kb.op('dve', lambda g: g.tensor_copy(out=gtmp[:, 1, :], in_=modT[:, 80:96, 0]), reads=[modT], writes=[gtmp])
        kb.dma('sp', gv, gtmp[:], gtmp, reads=[gtmp], writes=[gvec], allow_slow_non_contiguous=True)
        kb.barrier()
        kb.release([sv, svs, bad, gtmp] + wpool.bufs)

    def make_hT(pools, src_ap, src_buf, hT, col0, sci, shi):
        xn = make_xn(pools, src_ap, src_buf)
        xn_T(pools, xn, hT, col0, lambda kc: scsh[:, sci, kc:kc + 1], lambda kc: scsh[:, shi, kc:kc + 1], [scsh])

    def make_xn(pools, src_ap, src_buf):
        xt = pools['xt'].get()
        kb.dma('sp', xt[:], src_ap, xt, reads=[src_buf], writes=[xt])
        return norm_rows(pools, xt)

    def norm_rows(pools, xt):
        sq = pools['sq'].get()
        ss = pools['ss'].get()
        kb.op('dve', lambda g: g.memset(ss[:], 0.0), writes=[ss])
        kb.op('act', lambda g: g.activation(out=sq[:], in_=xt[:], func=AF.Square, accum_out=ss[:, 0:1]),
              reads=[xt], writes=[sq, ss])
        kb.op('dve', lambda g: g.tensor_scalar(out=ss[:, 1:2], in0=ss[:, 0:1], scalar1=1.0 / D, scalar2=EPS,
                                               op0=ALU.mult, op1=ALU.add), reads=[ss], writes=[ss])
        kb.op('act', lambda g: g.activation(out=ss[:, 3:4], in_=ss[:, 1:2], func=AF.Sqrt), reads=[ss], writes=[ss])
        kb.op('dve', lambda g: g.reciprocal(out=ss[:, 2:3], in_=ss[:, 3:4]), reads=[ss], writes=[ss])
        xn = pools['xn'].get()
        kb.op('dve', lambda g: g.tensor_scalar(out=xn[:], in0=xt[:], scalar1=ss[:, 2:3], scalar2=None, op0=ALU.mult),
              reads=[xt, ss], writes=[xn])
        return xn

    def xn_T(pools, xn, hT, col0, scale_fn, shift_fn, pbufs):
        for q in range(4):
            pt = pools['pt'].get()
            for j in range(4):
                kc = q * 4 + j
                kb.op('pe', lambda g: g.transpose(out=pt[:, j * 128:(j + 1) * 128], in_=xn[:, kc * 128:(kc + 1) * 128],
                                                  identity=identb[:]), reads=[xn, identb], writes=[pt])
            e = kb.alt()
            for j in range(4):
                kc = q * 4 + j
                dst = hT[:, kc, col0:col0 + 128]
                src = pt[:, j * 128:(j + 1) * 128]
                if shift_fn is not None:
                    evac_affine(kb, e, dst, src, scale_fn(kc), shift_fn(kc), reads=[pt] + pbufs, writes=[hT.part(kc)])
                elif e == 'dve':
                    kb.op('dve', lambda g: g.tensor_scalar(out=dst, in0=src, scalar1=scale_fn(kc), scalar2=None, op0=ALU.mult),
                          reads=[pt] + pbufs, writes=[hT.part(kc)])
                else:
                    kb.op('act', lambda g: g.activation(out=dst, in_=src, func=AF.Copy, scale=scale_fn(kc)),
                          reads=[pt] + pbufs, writes=[hT.part(kc)])

    def norm_T(pools, xt, hT, col0, scale_fn, shift_fn, pbufs):
        xn = norm_rows(pools, xt)
        xn_T(pools, xn, hT, col0, scale_fn, shift_fn, pbufs)

    def build_hT(pools, src, hT, ntiles, sci, shi):
        xn = make_xn(pools, src.t[0:128, :], src)
        for tt in range(ntiles):
            nxt = make_xn(pools, src.t[(tt + 1) * 128:(tt + 2) * 128, :], src) if tt + 1 < ntiles else None
            xn_T(pools, xn, hT, tt * 128, lambda kc: scsh[:, sci, kc:kc + 1], lambda kc: scsh[:, shi, kc:kc + 1], [scsh])
            xn = nxt

    def norm_pools(st, tag, nxn=1, npt=2, nxt=2):
        return {
            'xt': Pool(kb, st, tag + "xt", [128, D], F32, nxt),
            'sq': Pool(kb, st, tag + "sq", [128, D], BF16, 1),
            'ss': Pool(kb, st, tag + "ss", [128, 4], F32, 3),
            'xn': Pool(kb, st, tag + "xn", [128, D], F32, nxn),
            'pt': Pool(kb, st, tag + "pt", [128, 512], F32, npt, psum=True),
        }

    def load_weight(st_pool, w_dram, col_lo, ncols, wsb, dst_lo, row_lo=0, nkc=NKC, kc_dst=0):
        wv_ = w_dram.t[row_lo:row_lo + nkc * 128, :].rearrange("(kc p) n -> p kc n", p=128)
        c = 0
        while c < ncols:
            n = min(128, ncols - c)
            stg = st_pool.get()
            kb.dma('sp', stg[:, 0:nkc, 0:n], wv_[:, :, col_lo + c:col_lo + c + n], stg, reads=[w_dram], writes=[stg])
            e = kb.alt(('act', 'dve'))
            dst = wsb[:, kc_dst:kc_dst + nkc, dst_lo + c:dst_lo + c + n]
            wpart = wsb.part((dst_lo + c) // 128)
            if e == 'act':
                kb.op('act', lambda g: g.copy(out=dst, in_=stg[:, 0:nkc, 0:n]), reads=[stg], writes=[wpart])
            else:
                kb.op(e, lambda g: g.tensor_copy(out=dst, in_=stg[:, 0:nkc, 0:n]), reads=[stg], writes=[wpart])
            c += n

    NT_C = dbg.get("nt_c", 16) if dbg else 16

    def proj_stage(tag, wcol0, ncols, jobs, zero_pads):
        nblk = (ncols + 127) // 128
        with ExitStack() as st:
            st.enter_context(nc.named_scope("st_" + tag))
            pools = norm_pools(st, tag, nxn=2, npt=4, nxt=3)
            wstg = Pool(kb, st, tag + "wstg", [128, NKC, 128], F32, 2)
            wsb = kb.sb(st, tag + "w", [128, NKC, ncols], BF16)
            load_weight(wstg, w_in, wcol0, ncols, wsb, 0)
            hTp = Pool(kb, st, tag + "hT", [128, NKC, 512], BF16, 2)
            pg = Pool(kb, st, tag + "pg", [128, 512], F32, 4, psum=True)
            ostg = Pool(kb, st, tag + "ostg", [128, 512], F32, 4)
            zt = kb.sb(st, tag + "zt", [128, 4], F32)
            kb.op('dve', lambda g: g.memset(zt[:], 0.0), writes=[zt])
            for (dst, L) in zero_pads:
                for rb in range(nblk):
                    r0 = rb * 128
                    nr = min(128, ncols - r0)
                    kb.dma('pool', dst.t[r0:r0 + nr, 0:2], zt[0:nr, 0:2], zt, reads=[zt], writes=[dst])
                    kb.dma('pool', dst.t[r0:r0 + nr, L + 2:L + 4], zt[0:nr, 2:4], zt, reads=[zt], writes=[dst])
            hTs = {}

            tiles = []
            for ji, jb in enumerate(jobs):
                for i in range(jb[2] // 128):
                    tiles.append((ji, i))
            xns = {}

            def prep_a(k):
                if k < len(tiles):
                    ji, i = tiles[k]
                    xns[k] = make_xn(pools, jobs[ji][1](i), jobs[ji][0])

            def prep_b(k):
                ji, i = tiles[k]
                (src, apfn, nt, dst, dcol, sci, shi) = jobs[ji]
                if i == 0:
                    hTs[ji] = hTp.get()
                xn_T(pools, xns.pop(k), hTs[ji], i * 128, lambda kc: scsh[:, sci, kc:kc + 1], lambda kc: scsh[:, shi, kc:kc + 1], [scsh])

            tk = 0
            prep_a(0)
            n0 = jobs[0][2] // 128
            for k in range(n0):
                prep_a(k + 1)
                prep_b(k)
            tk = n0
            for ji, (src, apfn, nt, dst, dcol, sci, shi) in enumerate(jobs):
                hT = hTs[ji]
                nxt_tiles = (jobs[ji + 1][2] // 128) if ji + 1 < len(jobs) else 0
                step = max(1, nblk // max(1, nxt_tiles))
                emitted = 0
                for cb in range(nblk):
                    c0 = cb * 128
                    ncol = min(128, ncols - c0)
                    pp = pg.get()
                    for kc in range(NKC):
                        kb.op('pe', lambda g: g.matmul(pp[0:ncol, 0:nt], wsb[:, kc, c0:c0 + ncol], hT[:, kc, 0:nt],
                                                       start=(kc == 0), stop=(kc == NKC - 1)), reads=[wsb.part(cb), hT.part(kc)], writes=[pp])
                    og = ostg.get()
                    if kb.alt() == 'act':
                        kb.op('act', lambda g: g.copy(out=og[0:ncol, 0:nt], in_=pp[0:ncol, 0:nt]), reads=[pp], writes=[og])
                    else:
                        kb.op('dve', lambda g: g.tensor_copy(out=og[0:ncol, 0:nt], in_=pp[0:ncol, 0:nt]), reads=[pp], writes=[og])
                    kb.dma('pool', dst.t[c0:c0 + ncol, dcol:dcol + nt], og[0:ncol, 0:nt], og, reads=[og], writes=[dst])
                    if emitted < nxt_tiles and (cb + 1) % step == 0:
                        prep_a(tk + 1)
                        prep_b(tk)
                        tk += 1
                        emitted += 1
                while emitted < nxt_tiles:
                    prep_a(tk + 1)
                    prep_b(tk)
                    tk += 1
                    emitted += 1
            kb.barrier()
            kb.release(sum([p.bufs for p in pools.values()], []) + wstg.bufs + hTp.bufs + ostg.bufs + [zt])

    def rm_tile(src, t0):
        return lambda i: src.t[t0 + i * 128:t0 + (i + 1) * 128, :]
    jobsC = [(ctxb, rm_tile(ctxb, 0), 256, pre_ssd_c, 2, 2, 3)]
    jobsC += [(xb, rm_tile(xb, s_ * 512), 512, pre_ssd, 2 + s_ * 512, 0, 1) for s_ in range(NT_C)]
    proj_stage("C", OXS, 3136, jobsC, [(pre_ssd, SEQ), (pre_ssd_c, CTX)])

    with ExitStack() as st:
        st.enter_context(nc.named_scope("st_C2"))
        cw = kb.sb(st, "cw", [128, 24, 5], F32)
        kb.dma('sp', cw[:], cw_ssd[:, :, :], cw, reads=[cw_ssd], writes=[cw])
        identbf = kb.sb(st, "identbf", [128, 128], BF16)
        kb.op('dve', lambda g: g.tensor_copy(out=identbf[:], in_=identb[:]), reads=[identb], writes=[identbf])
        prep = Pool(kb, st, "cpre", [128, 2051], F32, 3)
        accp = Pool(kb, st, "cacc", [128, 2048], F32, 3)
        ctmp = kb.sb(st, "ctmp", [128, 2048], F32)
        xcp = Pool(kb, st, "cxc", [128, 2048], BF16, 3)
        ptp = Pool(kb, st, "cpt", [128, 1024], BF16, 4, psum=True)
        tmp_ = Pool(kb, st, "ctm", [128, 16, 128], BF16, 3)
        seqs = [(pre_ssd_c, CTX, xsb_tm_c, bc_fm_c), (pre_ssd, min(SEQ, NT_C * 512), xsb_tm, bc_fm)]
        units = []
        for (pre, L, xsb, bcf) in seqs:
            TG = min(L, 2048)
            for tg in range(L // TG):
                for blk in range(24):
                    units.append((pre, xsb, bcf, TG, tg * TG, blk))

        def c2_front(u):
            (pre, xsb, bcf, TG, t0, blk) = u
            pt_ = prep.get()
            kb.dma('sp', pt_[:, 0:TG + 3], pre.t[blk * 128:(blk + 1) * 128, t0:t0 + TG + 3], pt_, reads=[pre], writes=[pt_])
            acc = accp.get()
            kb.op('act', lambda g: g.activation(out=acc[:, 0:TG], in_=pt_[:, 0:TG], func=AF.Identity,
                                                scale=cw[:, blk, 0:1], bias=cw[:, blk, 4:5]), reads=[pt_, cw], writes=[acc])
            return (pt_, acc)

        def c2_back(u, fr):
            (pre, xsb, bcf, TG, t0, blk) = u
            pt_, acc = fr
            for k in range(1, 4):
                kb.op('dve', lambda g: g.scalar_tensor_tensor(out=acc[:, 0:TG], in0=pt_[:, k:k + TG], scalar=cw[:, blk, k:k + 1],
                                                              in1=acc[:, 0:TG], op0=ALU.mult, op1=ALU.add),
                      reads=[pt_, cw, acc], writes=[acc])
            xc = xcp.get()
            kb.op('act', lambda g: g.activation(out=xc[:, 0:TG], in_=acc[:, 0:TG], func=AF.Silu), reads=[acc], writes=[xc])
            if blk >= 16:
                kb.dma('pool', bcf.t[(blk - 16) * 128:(blk - 15) * 128, t0:t0 + TG], xc[:, 0:TG], xc, reads=[xc], writes=[bcf])
            if blk < 20:
                tm = tmp_.get()
                ntt = TG // 128
                for q in range((ntt + 3) // 4):
                    pp = ptp.get()
                    nj = min(4, ntt - q * 4)
                    for j in range(nj):
                        tt = q * 4 + j
                        kb.op('pe', lambda g: g.transpose(out=pp[:, j * 128:(j + 1) * 128], in_=xc[:, tt * 128:(tt + 1) * 128],
                                                          identity=identbf[:]), reads=[xc, identbf], writes=[pp])
                    src_v = pp[:, 0:nj * 128].rearrange("p (a b) -> p a b", a=nj)
                    if kb.alt() == 'act':
                        kb.op('act', lambda g: g.copy(out=tm[:, q * 4:q * 4 + nj, :], in_=src_v), reads=[pp], writes=[tm.part(q)])
                    else:
                        kb.op('dve', lambda g: g.tensor_copy(out=tm[:, q * 4:q * 4 + nj, :], in_=src_v), reads=[pp], writes=[tm.part(q)])
                kb.dma('pool', xsb.t[t0:t0 + TG, blk * 128:(blk + 1) * 128].rearrange("(tt p) c -> p tt c", p=128),
                       tm[:, 0:ntt, :], tm, reads=[tm], writes=[xsb])

        fr = c2_front(units[0])
        for ui in range(len(units)):
            nfr = c2_front(units[ui + 1]) if ui + 1 < len(units) else None
            c2_back(units[ui], fr)
            fr = nfr
        kb.barrier()
        kb.release([cw] + prep.bufs + xcp.bufs + tmp_.bufs)

    with ExitStack() as st:
        st.enter_context(nc.named_scope("st_D0"))
        selb = kb.sb(st, "D0sel", [128, 4], F32)
        kb.dma('sp', selb[:], sel[:, :], selb, reads=[sel], writes=[selb])
        idsel = kb.sb(st, "idsel", [128, 4, 128], BF16)
        idsel32 = kb.sb(st, "idsel32", [128, 4, 128], F32)
        for b_ in range(4):
            kb.op('dve', lambda g: g.tensor_scalar(out=idsel[:, b_, :], in0=identb[:], scalar1=selb[:, b_:b_ + 1], scalar2=None, op0=ALU.mult),
                  reads=[identb, selb], writes=[idsel])
            kb.op('dve', lambda g: g.tensor_scalar(out=idsel32[:, b_, :], in0=identb[:], scalar1=selb[:, b_:b_ + 1], scalar2=None, op0=ALU.mult),
                  reads=[identb, selb], writes=[idsel32])
        xcp_ = Pool(kb, st, "D0xc", [128, 2560], BF16, 8)
        bcp_ = Pool(kb, st, "D0bc", [128, 8, 128], BF16, 8)
        pdp_ = Pool(kb, st, "D0pd", [64, 128], F32, 8)
        xop_ = Pool(kb, st, "D0xo", [128, 2560], BF16, 2)
        bop_ = Pool(kb, st, "D0bo", [128, 8, 128], BF16, 2)
        pop_ = Pool(kb, st, "D0po", [64, 128], F32, 2)
        psel = Pool(kb, st, "D0ps", [128, 512], F32, 4, psum=True)
        bc_v = bc_fm.t.rearrange("(b p) t -> p b t", p=128)
        bco_v = bc_own.t.rearrange("(b p) t -> p b t", p=128)

        def evac(dst_ap, src_ap, rd, wr):
            if kb.alt() == 'act':
                kb.op('act', lambda g: g.copy(out=dst_ap, in_=src_ap), reads=rd, writes=wr)
            else:
                kb.op('dve', lambda g: g.tensor_copy(out=dst_ap, in_=src_ap), reads=rd, writes=wr)

        for i in range(16):
            xcs = []; bcs = []; pds = []
            for b_ in range(4):
                c = 16 * b_ + i
                xc = xcp_.get(); bc = bcp_.get(); pd_ = pdp_.get()
                kb.dma('sp', xc[:], xsb_tm.t[c * 128:(c + 1) * 128, :], xc, reads=[xsb_tm], writes=[xc])
                kb.dma('sp', bc[:], bc_v[:, :, c * 128:(c + 1) * 128], bc, reads=[bc_fm], writes=[bc])
                kb.dma('sp', pd_[:], pre_ssd.t[3072:3136, 2 + c * 128:2 + (c + 1) * 128], pd_, reads=[pre_ssd], writes=[pd_])
                xcs.append(xc); bcs.append(bc); pds.append(pd_)
            xo_ = xop_.get()
            for q in range(5):
                pp = psel.get()
                for b_ in range(4):
                    kb.op('pe', lambda g: g.matmul(pp[:], idsel[:, b_, :], xcs[b_][:, q * 512:(q + 1) * 512], start=(b_ == 0), stop=(b_ == 3)),
                          reads=[idsel, xcs[b_]], writes=[pp])
                evac(xo_[:, q * 512:(q + 1) * 512], pp[:], [pp], [xo_.part(q)])
            kb.dma('pool', xsb_own.t[i * 128:(i + 1) * 128, :], xo_[:], xo_, reads=[xo_], writes=[xsb_own])
            bo_ = bop_.get()
            for q in range(2):
                pp = psel.get()
                for b_ in range(4):
                    kb.op('pe', lambda g: g.matmul(pp[:], idsel[:, b_, :], bcs[b_][:, q * 4:(q + 1) * 4, :].rearrange("p a t -> p (a t)"),
                                                   start=(b_ == 0), stop=(b_ == 3)), reads=[idsel, bcs[b_]], writes=[pp])
                evac(bo_[:, q * 4:(q + 1) * 4, :].rearrange("p a t -> p (a t)"), pp[:], [pp], [bo_.part(q)])
            kb.dma('pool', bco_v[:, :, i * 128:(i + 1) * 128], bo_[:], bo_, reads=[bo_], writes=[bc_own])
            po_ = pop_.get()
            pp = psel.get()
            for b_ in range(4):
                kb.op('pe', lambda g: g.matmul(pp[0:64, 0:128], idsel32[0:64, b_, 0:64], pds[b_][:], start=(b_ == 0), stop=(b_ == 3)),
                      reads=[idsel32, pds[b_]], writes=[pp])
            evac(po_[:], pp[0:64, 0:128], [pp], [po_])
            kb.dma('pool', pdt_own.t[:, i * 128:(i + 1) * 128], po_[:], po_, reads=[po_], writes=[pdt_own])
        kb.barrier()
        kb.release([selb] + xcp_.bufs + bcp_.bufs + pdp_.bufs + xop_.bufs + bop_.bufs + pop_.bufs)

    NCH_D = dbg.get("nch_d", 64) if dbg else 64
    with ExitStack() as st:
        st.enter_context(nc.named_scope("st_D"))
        dtpb = kb.sb(st, "dtpb", [128, 2, 2], F32)
        dsk = kb.sb(st, "dsk", [128, 32], F32)
        msk = kb.sb(st, "msk", [128, 2, 128], F32)
        cmk = kb.sb(st, "cmk", [128, 2, 64], F32)
        esel = kb.sb(st, "esel", [128, 2, 128], F32)
        kb.dma('sp', dtpb[:], dtp[:, :, :], dtpb, reads=[dtp], writes=[dtpb])
        kb.dma('sp', dsk[:], dskb[:, :], dsk, reads=[dskb], writes=[dsk])
        kb.dma('sp', msk[:], masks[:, :, :], msk, reads=[masks], writes=[msk])
        kb.dma('sp', cmk[:], cmask[:, :, :], cmk, reads=[cmask], writes=[cmk])
        kb.dma('sp', esel[:], endsel[:, :, :], esel, reads=[endsel], writes=[esel])
        dmk = kb.sb(st, "dmk", [32, 4, 8], F32)
        kb.dma('sp', dmk[:], dmask[:, :, :], dmk, reads=[dmask], writes=[dmk])
        arowp = Pool(kb, st, "darow", [128, 32, 128], F32, 2)
        aneg = kb.sb(st, "aneg", [128, 2], F32)
        ones = kb.sb(st, "ones", [128, 128], F32)
        kb.op('dve', lambda g: g.memset(ones[:], 1.0), writes=[ones])
        kb.op('act', lambda g: g.activation(out=aneg[:], in_=dtpb[:, :, 1], func=AF.Exp), reads=[dtpb], writes=[aneg])
        kb.op('dve', lambda g: g.tensor_scalar(out=aneg[:], in0=aneg[:], scalar1=-1.0, scalar2=None, op0=ALU.mult),
              reads=[aneg], writes=[aneg])
        Hs = [kb.sb(st, "Hst%d" % d_, [128, 2048], F32) for d_ in range(2)]
        Hbfs = [kb.sb(st, "Hbf%d" % d_, [128, 2048], BF16) for d_ in range(2)]
        xsbp = Pool(kb, st, "dxsb", [128, 2560], BF16, 6)
        bcp = Pool(kb, st, "dbc", [128, 8, 128], BF16, 2)
        pdtp = Pool(kb, st, "dpdt", [128, 128], F32, 6)
        smallp = Pool(kb, st, "dsm", [128, 128], F32, 18)
        T4p = Pool(kb, st, "dT4", [128, 128], F32, 3)
        tm4p = Pool(kb, st, "dtm4", [128, 128], F32, 3)
        decp = Pool(kb, st, "ddec", [128, 32], F32, 3)
        xdtp = Pool(kb, st, "dxdt", [128, 2048], BF16, 2)
        xwp = Pool(kb, st, "dxw", [128, 2048], BF16, 3)
        xsdp = Pool(kb, st, "dxsd", [128, 2048], BF16, 2)
        identbf = kb.sb(st, "identbfD", [128, 128], BF16)
        kb.op('dve', lambda g: g.tensor_copy(out=identbf[:], in_=identb[:]), reads=[identb], writes=[identbf])
        cbmp = Pool(kb, st, "dcbm", [128, 4, 128], F32, 2)
        Ep = Pool(kb, st, "dE", [128, 8, 128], F32, 2)
        MTp = Pool(kb, st, "dMT", [128, 8, 128], BF16, 2)
        ychp = Pool(kb, st, "dych", [128, 2048], F32, 2)
        yprp = Pool(kb, st, "dypr", [128, 2048], F32, 2)
        tep = Pool(kb, st, "dte", [128, 512], F32, 3)
        pT = Pool(kb, st, "pT", [128, 512], F32, 2, psum=True)
        PS = {}
        slot = [0]
        Ptile = kb.sb(st, "Ptile", [128, 32], F32)
        dtesp = Pool(kb, st, "ddtes", [128, 32], F32, 3)

        def dt_chain(d, pdt_ap, pdt_buf, mask_ap):
            pdt = pdtp.get()
            for r in range(4):
                kb.dma('sp', pdt[r * 32:(r + 1) * 32, :], pdt_ap, pdt, reads=[pdt_buf], writes=[pdt])
            yield
            e1 = smallp.get(); dt_ = smallp.get(); dtA = smallp.get(); cum = smallp.get(); ac = smallp.get()
            ex = smallp.get()
            kb.op('act', lambda g: g.activation(out=e1[:], in_=pdt[:], func=AF.Exp, bias=dtpb[:, d, 0:1], scale=1.0),
                  reads=[pdt, dtpb], writes=[e1])
            yield
            kb.op('act', lambda g: g.activation(out=dt_[:], in_=e1[:], func=AF.Ln, bias=1.0, scale=1.0), reads=[e1], writes=[dt_])
            yield
            if mask_ap is not None:
                kb.op('dve', lambda g: g.tensor_scalar(out=dt_[:], in0=dt_[:], scalar1=mask_ap, scalar2=None, op0=ALU.mult),
                      reads=[dt_, cmk], writes=[dt_])
                yield
            kb.op('dve', lambda g: g.tensor_scalar(out=dtA[:], in0=dt_[:], scalar1=aneg[:, d:d + 1], scalar2=None, op0=ALU.mult),
                  reads=[dt_, aneg], writes=[dtA])
            yield
            kb.op('dve', lambda g: g.tensor_tensor_scan(out=cum[:], data0=ones[:], data1=dtA[:], initial=0.0,
                                                        op0=ALU.mult, op1=ALU.add), reads=[ones, dtA], writes=[cum])
            yield
            if d == 0:
                ac = cum
            else:
                kb.op('dve', lambda g: g.scalar_tensor_tensor(out=ac[:], in0=dtA[:], scalar=cum[:, 127:128], in1=cum[:],
                                                              op0=ALU.add, op1=ALU.subtract), reads=[dtA, cum], writes=[ac])
                yield
            kb.op('act', lambda g: g.activation(out=ex[:], in_=ac[:], func=AF.Exp, bias=cum[:, 127:128], scale=-1.0),
                  reads=[ac, cum], writes=[ex])
            yield
            T4 = T4p.get()
            kb.op('act', lambda g: g.copy(out=T4[0:32, :], in_=ac[0:32, :]), reads=[ac], writes=[T4])
            yield
            kb.op('act', lambda g: g.copy(out=T4[32:64, :], in_=dt_[32:64, :]), reads=[dt_], writes=[T4])
            yield
            kb.op('dve', lambda g: g.tensor_tensor(out=T4[64:96, :], in0=dt_[64:96, :], in1=ex[64:96, :], op=ALU.mult),
                  reads=[dt_, ex], writes=[T4])
            yield
            kb.op('act', lambda g: g.activation(out=T4[96:128, :], in_=ac[96:128, :], func=AF.Exp), reads=[ac], writes=[T4])
            yield
            ptt = pT.get()
            kb.op('pe', lambda g: g.transpose(out=ptt[:, 0:128], in_=T4[:], identity=identb[:]), reads=[T4, identb], writes=[ptt])
            yield
            tm4 = tm4p.get()
            kb.op('act', lambda g: g.copy(out=tm4[:], in_=ptt[:, 0:128]), reads=[ptt], writes=[tm4])
            yield
            kb.op('pe', lambda g: g.matmul(ptt[:, 128:160], esel[:, d, :], tm4[:, 0:32], start=True, stop=True),
                  reads=[esel, tm4], writes=[ptt])
            yield
            dec = decp.get()
            kb.op('act', lambda g: g.activation(out=dec[:], in_=ptt[:, 128:160], func=AF.Exp), reads=[ptt], writes=[dec])
            yield
            return ac, tm4, dec

        def ssd_state_chunk(d, pdt_ap, pdt_buf, xsb, c, mask_ap, first, last):
            t0 = c * 128
            xsbt = xsbp.get()
            kb.dma('sp', xsbt[:], xsb.t[t0:t0 + 128, :], xsbt, reads=[xsb], writes=[xsbt])
            yield
            ac, tm4, dec = yield from dt_chain(d, pdt_ap, pdt_buf, mask_ap)
            dtes = dtesp.get()
            kb.op('dve', lambda g: g.tensor_tensor(out=dtes[:], in0=tm4[:, 64:96], in1=Ptile[:], op=ALU.mult), reads=[tm4, Ptile], writes=[dtes])
            yield
            kb.op('dve', lambda g: g.tensor_tensor(out=Ptile[:], in0=Ptile[:], in1=dec[:], op=ALU.mult), reads=[Ptile, dec], writes=[Ptile])
            yield
            xw = xwp.get()
            kb.op('dve', lambda g: g.tensor_tensor(out=xw[:].rearrange("p (h e) -> p h e", h=32),
                                                   in0=xsbt[:, 0:2048].rearrange("p (h e) -> p h e", h=32),
                                                   in1=dtes[:].unsqueeze(2).to_broadcast([128, 32, 64]), op=ALU.mult),
                  reads=[xsbt, dtes], writes=[xw])
            yield
            for g_ in range(4):
                gc = slice(g_ * 512, (g_ + 1) * 512)
                kb.op('pe', lambda g: g.matmul(PS['H'][g_][:], xsbt[:, 2048 + g_ * 128:2048 + (g_ + 1) * 128], xw[:, gc],
                                               start=first, stop=last), reads=[xsbt, xw], writes=[PS['H'][g_]])
                yield

        def run_gen(gen):
            try:
                while True:
                    next(gen)
            except StopIteration as e:
                return e.value

        def run_chains(gens, nactive, stagger):
            pending = list(gens)
            active = []
            steps = 0
            while pending or active:
                if pending and len(active) < nactive and (not active or steps >= stagger):
                    active.append(pending.pop(0))
                    steps = 0
                for gch in list(active):
                    try:
                        next(gch)
                    except StopIteration:
                        active.remove(gch)
                steps += 1

        def pdt_of(pre, d, c):
            return pre.t[3072 + 32 * d:3072 + 32 * d + 32, 2 + c * 128:2 + (c + 1) * 128]

        def ssd_chunk(d, pdt_ap, pdt_buf, xsb, bcf, c, with_y, mask_ap, rmw=False):
            H = Hs[d]
            Hbf = Hbfs[d]
            pCB, pOFF, pDG, pS = PS['CB'], PS['OFF'], PS['DG'], PS['S']
            t0 = c * 128
            xsbt = xsbp.get()
            kb.dma('sp', xsbt[:], xsb.t[t0:t0 + 128, :], xsbt, reads=[xsb], writes=[xsbt])
            ac, tm4, dec = run_gen(dt_chain(d, pdt_ap, pdt_buf, mask_ap))
            xs3 = xsbt[:, 0:2048].rearrange("p (h e) -> p h e", h=32)
            xw = xwp.get()
            kb.op('dve', lambda g: g.tensor_tensor(out=xw[:].rearrange("p (h e) -> p h e", h=32), in0=xs3,
                                                    in1=tm4[:, 64:96].unsqueeze(2).to_broadcast([128, 32, 64]), op=ALU.mult),
                  reads=[xsbt, tm4], writes=[xw])
            if with_y:
                sl = slot[0] % 2
                slot[0] += 1
                kb.dma('pool', ac_scr.t[sl:sl + 1, :].rearrange("o (h t) -> (o h) t", h=32), ac[0:32, :], ac, reads=[ac], writes=[ac_scr])
                arow = arowp.get()
                kb.dma('sp', arow[:].rearrange("p h t -> p (h t)"), ac_scr.t[sl:sl + 1, :].partition_broadcast(128), arow,
                       reads=[ac_scr], writes=[arow])
                kb.op('act', lambda g: g.copy(out=Hbf[:], in_=H[:]), reads=[H], writes=[Hbf])
                bct = bcp.get()
                kb.dma('sp', bct[:], bcf.t.rearrange("(b p) t -> p b t", p=128)[:, :, t0:t0 + 128], bct, reads=[bcf], writes=[bct])
                xdt = xdtp.get()
                kb.op('dve', lambda g: g.tensor_tensor(out=xdt[:].rearrange("p (h e) -> p h e", h=32), in0=xs3,
                                                       in1=tm4[:, 32:64].unsqueeze(2).to_broadcast([128, 32, 64]), op=ALU.mult),
                      reads=[xsbt, tm4], writes=[xdt])
                if d == 0:
                    xsd = xsdp.get()
                    kb.op('dve', lambda g: g.tensor_tensor(out=xsd[:].rearrange("p (h e) -> p h e", h=32), in0=xs3,
                                                           in1=dsk[:].unsqueeze(2).to_broadcast([128, 32, 64]), op=ALU.mult),
                          reads=[xsbt, dsk], writes=[xsd])
                for g_ in range(4):
                    kb.op('pe', lambda g: g.matmul(pCB[:, g_ * 128:(g_ + 1) * 128], bct[:, g_, :], bct[:, 4 + g_, :],
                                                   start=True, stop=True), reads=[bct], writes=[pCB])
                cbm = cbmp.get()
                kb.op('dve', lambda g: g.tensor_tensor(out=cbm[:], in0=pCB[:].rearrange("p (a b) -> p a b", a=4),
                                                       in1=msk[:, d:d + 1, :].to_broadcast([128, 4, 128]), op=ALU.mult),
                      reads=[pCB, msk], writes=[cbm])
                ych = ychp.get()
                if rmw:
                    ypr = yprp.get()
                    kb.dma('sp', ypr[:], y_own.t[t0:t0 + 128, :], ypr, reads=[y_own], writes=[ypr])
            for g_ in range(4):
                gc = slice(g_ * 512, (g_ + 1) * 512)
                if with_y:
                    E = Ep.get()
                    for h in range(8):
                        gh = g_ * 8 + h
                        kb.op('act', lambda g: g.activation(out=E[:, h, :], in_=arow[:, gh, :], func=AF.Relu, bias=tm4[:, gh:gh + 1], scale=-1.0),
                              reads=[arow, tm4], writes=[E])
                    kb.op('act', lambda g: g.activation(out=E[:], in_=E[:], func=AF.Exp, scale=-1.0), reads=[E], writes=[E])
                    MT = MTp.get()
                    kb.op('dve', lambda g: g.tensor_tensor(out=MT[:], in0=E[:], in1=cbm[:, g_:g_ + 1, :].to_broadcast([128, 8, 128]),
                                                           op=ALU.mult), reads=[E, cbm], writes=[MT])
                    po = pOFF.get()
                    kb.op('pe', lambda g: g.matmul(po[:], bct[:, 4 + g_, :], Hbf[:, gc], start=True, stop=True),
                          reads=[bct, Hbf], writes=[po])
                    pd = pDG.get()
                    if d == 0:
                        kb.op('pe', lambda g: g.matmul(pd[:], identbf[:], xsd[:, gc], start=True, stop=False), reads=[identbf, xsd], writes=[pd])
                    for h in range(8):
                        gh = g_ * 8 + h
                        kb.op('pe', lambda g: g.matmul(pd[:, h * 64:(h + 1) * 64], MT[:, h, :], xdt[:, gh * 64:(gh + 1) * 64],
                                                       start=(d == 1), stop=True), reads=[MT, xdt], writes=[pd])
                    te = tep.get()
                    kb.op('dve', lambda g: g.tensor_tensor(out=te[:].rearrange("p (h e) -> p h e", h=8),
                                                           in0=po[:].rearrange("p (h e) -> p h e", h=8),
                                                           in1=tm4[:, 96 + g_ * 8:96 + g_ * 8 + 8].unsqueeze(2).to_broadcast([128, 8, 64]),
                                                           op=ALU.mult), reads=[po, tm4], writes=[te])
                    if not rmw:
                        kb.op('dve', lambda g: g.tensor_tensor(out=ych[:, gc], in0=te[:], in1=pd[:], op=ALU.add), reads=[te, pd], writes=[ych])
                    else:
                        kb.op('dve', lambda g: g.tensor_tensor(out=te[:], in0=te[:], in1=pd[:], op=ALU.add), reads=[te, pd], writes=[te])
                        kb.op('dve', lambda g: g.tensor_tensor(out=ych[:, gc], in0=te[:], in1=ypr[:, gc], op=ALU.add),
                              reads=[te, ypr], writes=[ych])
                psb = pS.get()
                kb.op('pe', lambda g: g.matmul(psb[:], xsbt[:, 2048 + g_ * 128:2048 + (g_ + 1) * 128], xw[:, gc], start=True, stop=True),
                      reads=[xsbt, xw], writes=[psb])
                kb.op('dve', lambda g: g.tensor_tensor(out=H[:, gc].rearrange("p (h e) -> p h e", h=8),
                                                       in0=H[:, gc].rearrange("p (h e) -> p h e", h=8),
                                                       in1=dec[:, g_ * 8:g_ * 8 + 8].unsqueeze(2).to_broadcast([128, 8, 64]),
                                                       op=ALU.mult), reads=[H, dec], writes=[H])
                kb.op('dve', lambda g: g.tensor_tensor(out=H[:, gc], in0=H[:, gc], in1=psb[:], op=ALU.add), reads=[H, psb], writes=[H])
            if with_y:
                kb.dma('pool', y_own.t[t0:t0 + 128, :], ych[:], ych, reads=[ych], writes=[y_own])

        for d in range(2):
            with ExitStack() as st3:
                st3.enter_context(nc.named_scope("st_Dst%d" % d))
                PS['H'] = [kb.ps(st3, "psH%d_%d" % (d, g_), [128, 512], F32) for g_ in range(4)]
                kb.op('dve', lambda g: g.memset(Ptile[:], 1.0), writes=[Ptile])
                if d == 0:
                    seq = [('lat', c) for c in range(NCH_D - 17, -1, -1)] + [('ctx', 1), ('ctx', 0)]
                else:
                    seq = [('lat', c) for c in range(16, NCH_D)] + [('ctx', 0), ('ctx', 1)]
                gens = []
                for si, (kind, c) in enumerate(seq):
                    if kind == 'lat':
                        gens.append(ssd_state_chunk(d, pdt_of(pre_ssd, d, c), pre_ssd, xsb_tm, c, cmk[:, d, c:c + 1], si == 0, si == len(seq) - 1))
                    else:
                        gens.append(ssd_state_chunk(d, pdt_of(pre_ssd_c, d, c), pre_ssd_c, xsb_tm_c, c, None, si == 0, si == len(seq) - 1))
                run_chains(gens, 3, 8)
                for g_ in range(4):
                    gc = slice(g_ * 512, (g_ + 1) * 512)
                    if g_ % 2 == 0:
                        kb.op('act', lambda g: g.copy(out=Hs[d][:, gc], in_=PS['H'][g_][:]), reads=[PS['H'][g_]], writes=[Hs[d]])
                    else:
                        kb.op('dve', lambda g: g.tensor_copy(out=Hs[d][:, gc], in_=PS['H'][g_][:]), reads=[PS['H'][g_]], writes=[Hs[d]])
                kb.barrier(['pe', 'act', 'dve'])
        with ExitStack() as st3:
            st3.enter_context(nc.named_scope("st_Down"))
            PS['CB'] = kb.ps(st3, "pCB", [128, 512], F32)
            PS['OFF'] = Pool(kb, st3, "pOFF", [128, 512], F32, 1, psum=True)
            PS['DG'] = Pool(kb, st3, "pDG", [128, 512], F32, 2, psum=True)
            PS['S'] = Pool(kb, st3, "pS", [128, 512], F32, 2, psum=True)
            for i in range(16):
                for d in range(2):
                    c = i if d == 0 else 15 - i
                    ssd_chunk(d, pdt_own.t[32 * d:32 * d + 32, c * 128:(c + 1) * 128], pdt_own, xsb_own, bc_own, c, True, None, rmw=(i >= 8))
            kb.barrier(['pe', 'act', 'dve'])
        kb.barrier()
        kb.release(xsbp.bufs + bcp.bufs + pdtp.bufs + smallp.bufs + arowp.bufs + ychp.bufs + yprp.bufs + [dtpb, dsk, msk, cmk, esel])

    NT_E = dbg.get("nt_e", 16) if dbg else 16
    xb_cm = xb.t.rearrange("(r w) d -> w r d", w=64)
    def cm_tile(s_):
        return lambda i: xb_cm[4 * s_ + i]
    jobsE = [(ctxb, rm_tile(ctxb, 0), 256, pre_xr_c, 2, 2, 3)]
    jobsE += [(xb, cm_tile(s_), 512, pre_xr, 2 + s_ * 512, 0, 1) for s_ in range(NT_E)]
    proj_stage("E", OXR, 2048, jobsE, [(pre_xr, SEQ), (pre_xr_c, CTX)])

    NH_F = dbg.get("nh_f", 16) if dbg else 16
    with ExitStack() as st:
        st.enter_context(nc.named_scope("st_F"))
        lcwb = kb.sb(st, "lcwb", [128, 16, 5], F32)
        lgbb = kb.sb(st, "lgbb", [128, 16, 4], F32)
        lamb = kb.sb(st, "lamb", [128, 16, 2], F32)
        selb = kb.sb(st, "selb", [128, 4], F32)
        kb.dma('sp', lcwb[:], lcw[:, :, :], lcwb, reads=[lcw], writes=[lcwb])
        kb.dma('sp', lgbb[:], lgb[:, :, :], lgbb, reads=[lgb], writes=[lgbb])
        kb.dma('sp', lamb[:], llam[:, :, :], lamb, reads=[llam], writes=[lamb])
        kb.dma('sp', selb[:], sel[:, :], selb, reads=[sel], writes=[selb])
        chalf = kb.sb(st, "chalf", [128, 16, 2], F32)
        hbias = kb.sb(st, "hbias", [128, 16, 4], F32)
        kb.op('act', lambda g: g.activation(out=chalf[:], in_=lamb[:], func=AF.Exp, scale=-1.0), reads=[lamb], writes=[chalf])
        kb.op('act', lambda g: g.activation(out=chalf[:], in_=chalf[:], func=AF.Ln, bias=1.0, scale=1.0), reads=[chalf], writes=[chalf])
        kb.op('dve', lambda g: g.tensor_scalar(out=chalf[:], in0=chalf[:], scalar1=-4.0, scalar2=None, op0=ALU.mult),
              reads=[chalf], writes=[chalf])
        kb.op('dve', lambda g: g.tensor_scalar(out=hbias[:], in0=lgbb[:], scalar1=0.5, scalar2=None, op0=ALU.mult),
              reads=[lgbb], writes=[hbias])
        rbuf = kb.sb(st, "rbuf", [128, SEQ], F32)
        abuf = kb.sb(st, "abuf", [128, SEQ], F32)
        ubuf = kb.sb(st, "ubuf", [128, SEQ], F32)
        PF = 1024
        prep_ = Pool(kb, st, "fpre", [128, PF + 3], F32, 3)
        xcvp = Pool(kb, st, "fxcv", [128, PF], F32, 3)
        xcvbp = Pool(kb, st, "fxcvb", [128, PF], BF16, 3)
        afp = Pool(kb, st, "faf", [128, PF], F32, 3)
        ufp = Pool(kb, st, "fuf", [128, PF], F32, 3)
        thap = Pool(kb, st, "ftha", [128, PF], F32, 2)
        thxp = Pool(kb, st, "fthx", [128, PF], F32, 2)
        na2p = Pool(kb, st, "fna2", [128, PF], F32, 2)
        vp = Pool(kb, st, "fv", [128, PF], F32, 2)
        hbtp = Pool(kb, st, "fhbt", [128, PF], F32, 2)
        wgsp = Pool(kb, st, "fwgs", [128, 4, 128], F32, 2)
        wgp = Pool(kb, st, "fwg", [128, 4, 128], BF16, 2)
        fstp = Pool(kb, st, "ffst", [128, 4], F32, 2)
        rop = Pool(kb, st, "fro", [128, 2048], F32, 1)
        pga = Pool(kb, st, "pga", [128, 512], F32, 4, psum=True)
        pgx = Pool(kb, st, "pgx", [128, 512], F32, 4, psum=True)

        def lru_front(hd, pre, t0, P):
            pt_ = prep_.get()
            kb.dma('sp', pt_[:, 0:P + 3], pre.t[hd * 128:(hd + 1) * 128, t0:t0 + P + 3], pt_, reads=[pre], writes=[pt_])
            xcv = xcvp.get()
            kb.op('dve', lambda g: g.tensor_scalar(out=xcv[:, 0:P], in0=pt_[:, 0:P], scalar1=lcwb[:, hd, 0:1], scalar2=lcwb[:, hd, 4:5],
                                                   op0=ALU.mult, op1=ALU.add), reads=[pt_, lcwb], writes=[xcv])
            for k in range(1, 4):
                kb.op('dve', lambda g: g.scalar_tensor_tensor(out=xcv[:, 0:P], in0=pt_[:, k:k + P], scalar=lcwb[:, hd, k:k + 1],
                                                              in1=xcv[:, 0:P], op0=ALU.mult, op1=ALU.add),
                      reads=[pt_, lcwb, xcv], writes=[xcv])
            xcvb = xcvbp.get()
            kb.op('dve', lambda g: g.tensor_copy(out=xcvb[:, 0:P], in_=xcv[:, 0:P]), reads=[xcv], writes=[xcvb])
            return (xcv, xcvb)

        def lru_back(hd, wg, xcv, P, dsts):
            xcv, xcvb = xcv
            sqjobs = []
            for d in range(2):
                tha = thap.get()
                thx = thxp.get()
                for q in range((P + 511) // 512):
                    n = min(512, P - q * 512)
                    cs = slice(q * 512, q * 512 + n)
                    pa = pga.get()
                    px = pgx.get()
                    kb.op('pe', lambda g: g.matmul(pa[:, 0:n], wg[:, 2 * d, :], xcvb[:, cs], start=True, stop=True),
                          reads=[wg, xcvb], writes=[pa])
                    kb.op('pe', lambda g: g.matmul(px[:, 0:n], wg[:, 2 * d + 1, :], xcvb[:, cs], start=True, stop=True),
                          reads=[wg, xcvb], writes=[px])
                    kb.op('act', lambda g: g.activation(out=tha[:, cs], in_=pa[:, 0:n], func=AF.Tanh,
                                                        bias=hbias[:, hd, 2 * d:2 * d + 1], scale=0.5), reads=[pa, hbias], writes=[tha])
                    kb.op('act', lambda g: g.activation(out=thx[:, cs], in_=px[:, 0:n], func=AF.Tanh,
                                                        bias=hbias[:, hd, 2 * d + 1:2 * d + 2], scale=0.5), reads=[px, hbias], writes=[thx])
                a_ap, u_ap, dbufs = dsts[d]
                kb.op('act', lambda g: g.activation(out=a_ap, in_=tha[:, 0:P], func=AF.Exp, bias=chalf[:, hd, d:d + 1],
                                                    scale=chalf[:, hd, d:d + 1]), reads=[tha, chalf], writes=dbufs)
                na2 = na2p.get()
                kb.op('act', lambda g: g.activation(out=na2[:, 0:P], in_=a_ap, func=AF.Square), reads=dbufs, writes=[na2])
                sqjobs.append((na2, thx, u_ap, dbufs))
            for (na2, thx, u_ap, dbufs) in sqjobs:
                kb.op('act', lambda g: g.activation(out=na2[:, 0:P], in_=na2[:, 0:P], func=AF.Sqrt, bias=0.25, scale=-0.25),
                      reads=[na2], writes=[na2])
            for (na2, thx, u_ap, dbufs) in sqjobs:
                v = vp.get()
                kb.op('dve', lambda g: g.scalar_tensor_tensor(out=v[:, 0:P], in0=thx[:, 0:P], scalar=1.0, in1=xcv[:, 0:P],
                                                              op0=ALU.add, op1=ALU.mult), reads=[thx, xcv], writes=[v])
                kb.op('dve', lambda g: g.tensor_tensor(out=u_ap, in0=v[:, 0:P], in1=na2[:, 0:P], op=ALU.mult),
                      reads=[v, na2], writes=dbufs)

        lgw_v = lgw.t
        npc = (NT_E * 512) // PF
        jobs = []
        for hd in range(NH_F):
            jobs.append((hd, 'ctx', 0))
            for p in (range(npc) if hd % 2 == 0 else range(npc - 1, -1, -1)):
                jobs.append((hd, 'lat', p))
        state = {}

        def front(job):
            hd, kind, p = job
            if kind == 'ctx':
                wgs = wgsp.get()
                kb.dma('sp', wgs[:], lgw_v[hd], wgs, reads=[lgw], writes=[wgs])
                wg = wgp.get()
                kb.op('act', lambda g: g.copy(out=wg[:], in_=wgs[:]), reads=[wgs], writes=[wg])
                state[('wg', hd)] = wg
                state[('fst', hd)] = fstp.get()
                return lru_front(hd, pre_xr_c, 0, CTX)
            return lru_front(hd, pre_xr, p * PF, PF)

        def back(job, xcv):
            hd, kind, p = job
            wg = state[('wg', hd)]
            fst = state[('fst', hd)]
            dp = hd % 2
            ds = 1 - dp
            if kind == 'ctx':
                af = afp.get(); uf = ufp.get(); ab = afp.get(); ub = ufp.get()
                lru_back(hd, wg, xcv, CTX, [(af[:, 0:CTX], uf[:, 0:CTX], [af, uf]), (ab[:, 0:CTX], ub[:, 0:CTX], [ab, ub])])
                hb = hbtp.get()
                kb.op('dve', lambda g: g.tensor_tensor_scan(out=hb[:, 0:CTX], data0=af[:, 0:CTX], data1=uf[:, 0:CTX], initial=0.0,
                                                            op0=ALU.mult, op1=ALU.add), reads=[af, uf], writes=[hb])
                kb.op('act', lambda g: g.copy(out=fst[:, 0:1], in_=hb[:, CTX - 1:CTX]), reads=[hb], writes=[fst])
                hb2 = hbtp.get()
                kb.op('dve', lambda g: g.tensor_tensor_scan(out=hb2[:, 0:CTX][:, ::-1], data0=ab[:, 0:CTX][:, ::-1],
                                                            data1=ub[:, 0:CTX][:, ::-1], initial=0.0, op0=ALU.mult, op1=ALU.add),
                      reads=[ab, ub], writes=[hb2])
                kb.op('act', lambda g: g.copy(out=fst[:, 1:2], in_=hb2[:, 0:1]), reads=[hb2], writes=[fst])
                return
            cs = slice(p * PF, (p + 1) * PF)
            af = afp.get(); uf = ufp.get()
            dsts = [None, None]
            dsts[dp] = (af[:], uf[:], [af, uf])
            dsts[ds] = (abuf[:, cs], ubuf[:, cs], [abuf.part(p), ubuf.part(p)])
            lru_back(hd, wg, xcv, PF, dsts)
            if dp == 0:
                init = fst[:, 0:1] if p == 0 else rbuf[:, p * PF - 1:p * PF]
                rd = [af, uf, fst] + ([rbuf.part(p - 1)] if p > 0 else [])
                kb.op('dve', lambda g: g.tensor_tensor_scan(out=rbuf[:, cs], data0=af[:], data1=uf[:], initial=init,
                                                            op0=ALU.mult, op1=ALU.add), reads=rd, writes=[rbuf.part(p)])
                last = (p == npc - 1)
            else:
                init = fst[:, 1:2] if p == npc - 1 else rbuf[:, (p + 1) * PF:(p + 1) * PF + 1]
                rd = [af, uf, fst] + ([rbuf.part(p + 1)] if p < npc - 1 else [])
                kb.op('dve', lambda g: g.tensor_tensor_scan(out=rbuf[:, cs][:, ::-1], data0=af[:][:, ::-1], data1=uf[:][:, ::-1],
                                                            initial=init, op0=ALU.mult, op1=ALU.add), reads=rd, writes=[rbuf.part(p)])
                last = (p == 0)
            if last:
                prev = None
                order2 = list(range(npc - 1, -1, -1)) if ds == 1 else list(range(npc))
                for p2 in order2:
                    cs2 = slice(p2 * PF, (p2 + 1) * PF)
                    hb = hbtp.get()
                    rd = [abuf.part(p2), ubuf.part(p2), fst] + ([prev] if prev is not None else [])
                    if ds == 1:
                        init = fst[:, 1:2] if prev is None else prev[:, 0:1]
                        kb.op('dve', lambda g: g.tensor_tensor_scan(out=hb[:][:, ::-1], data0=abuf[:, cs2][:, ::-1], data1=ubuf[:, cs2][:, ::-1],
                                                                    initial=init, op0=ALU.mult, op1=ALU.add), reads=rd, writes=[hb])
                    else:
                        init = fst[:, 0:1] if prev is None else prev[:, PF - 1:PF]
                        kb.op('dve', lambda g: g.tensor_tensor_scan(out=hb[:], data0=abuf[:, cs2], data1=ubuf[:, cs2],
                                                                    initial=init, op0=ALU.mult, op1=ALU.add), reads=rd, writes=[hb])
                    kb.op('dve', lambda g: g.tensor_tensor(out=rbuf[:, cs2], in0=rbuf[:, cs2], in1=hb[:], op=ALU.add),
                          reads=[rbuf.part(p2), hb], writes=[rbuf.part(p2)])
                    prev = hb
                ro = rop.get()
                r4 = rbuf[:].rearrange("p (j b i) -> p j b i", j=64, b=4)
                rov = ro[:].rearrange("p (i j) -> p j i", j=64)
                kb.op('dve', lambda g: g.tensor_scalar(out=rov, in0=r4[:, :, 0, :], scalar1=selb[:, 0:1], scalar2=None, op0=ALU.mult),
                      reads=[rbuf, selb], writes=[ro])
                for b_ in range(1, 4):
                    kb.op('dve', lambda g: g.scalar_tensor_tensor(out=rov, in0=r4[:, :, b_, :], scalar=selb[:, b_:b_ + 1], in1=rov,
                                                                  op0=ALU.mult, op1=ALU.add), reads=[rbuf, selb, ro], writes=[ro])
                kb.dma('pool', r_own.t[hd * 128:(hd + 1) * 128, :], ro[:], ro, reads=[ro], writes=[r_own])

        cur = front(jobs[0])
        for ji in range(len(jobs)):
            nxt = front(jobs[ji + 1]) if ji + 1 < len(jobs) else None
            back(jobs[ji], cur)
            cur = nxt
        kb.barrier()
        kb.release(prep_.bufs + wgsp.bufs + rop.bufs + [lcwb, lgbb, lamb, selb])

    def fm_view(scr):
        return scr.t.rearrange("(kc p) t -> p kc t", p=128)

    with ExitStack() as st:
        st.enter_context(nc.named_scope("st_G"))
        pools = norm_pools(st, "G", nxn=2, npt=4, nxt=4)
        hTo = kb.sb(st, "hTo", [128, NKC, 2048], BF16)
        build_hT(pools, xo, hTo, 16, 0, 1)
        wstg = Pool(kb, st, "Gwstg", [128, NKC, 128], F32, 3)
        pg = Pool(kb, st, "Gpg", [128, 512], F32, 4, psum=True)
        wzp = Pool(kb, st, "Gwz", [128, NKC, 512], BF16, 2)
        szp = Pool(kb, st, "Gsz", [128, 512], F32, 3)
        for cg in range(4):
            wz = wzp.get()
            load_weight(wstg, w_in, OZ + cg * 512, 512, wz, 0)
            for tt in range(16):
                pp = pg.get()
                for kc in range(NKC):
                    kb.op('pe', lambda g: g.matmul(pp[:], hTo[:, kc, tt * 128:(tt + 1) * 128], wz[:, kc, :],
                                                   start=(kc == 0), stop=(kc == NKC - 1)), reads=[hTo.part(kc), wz], writes=[pp])
                sz = szp.get()
                kb.op('act', lambda g: g.activation(out=sz[:], in_=pp[:], func=AF.Silu), reads=[pp], writes=[sz])
                kb.dma('pool', sz_scr.t[tt * 128:(tt + 1) * 128, cg * 512:(cg + 1) * 512], sz[:], sz, reads=[sz], writes=[sz_scr])
        bgb = kb.sb(st, "bgb", [128, 32], F32)
        kb.dma('sp', bgb[:], bgate[:, :], bgb, reads=[bgate], writes=[bgb])
        wbp = Pool(kb, st, "Gwb", [128, NKC, 128], BF16, 2)
        gop = Pool(kb, st, "Ggo", [128, 512], BF16, 3)
        for cb in range(32):
            wb = wbp.get()
            load_weight(wstg, w_gate, cb * 128, 128, wb, 0)
            for tg in range(4):
                pp = pg.get()
                for kc in range(NKC):
                    kb.op('pe', lambda g: g.matmul(pp[:], wb[:, kc, :], hTo[:, kc, tg * 512:(tg + 1) * 512],
                                                   start=(kc == 0), stop=(kc == NKC - 1)), reads=[hTo.part(kc), wb], writes=[pp])
                go = gop.get()
                kb.op('act', lambda g: g.activation(out=go[:], in_=pp[:], func=AF.Sigmoid, bias=bgb[:, cb:cb + 1], scale=1.0),
                      reads=[pp, bgb], writes=[go])
                kb.dma('pool', gT_scr.t[cb * 128:(cb + 1) * 128, tg * 512:(tg + 1) * 512], go[:], go, reads=[go], writes=[gT_scr])
        rlp = Pool(kb, st, "Grl", [128, 512], F32, 2)
        t1p = Pool(kb, st, "Gt1", [128, 512], F32, 2)
        t2p = Pool(kb, st, "Gt2", [128, 512], F32, 2)
        for cb in range(16):
            wb = wbp.get()
            load_weight(wstg, w_in, OYR + cb * 128, 128, wb, 0)
            for tg in range(4):
                ts_ = slice(tg * 512, (tg + 1) * 512)
                pp = pg.get()
                for kc in range(NKC):
                    kb.op('pe', lambda g: g.matmul(pp[:], wb[:, kc, :], hTo[:, kc, ts_],
                                                   start=(kc == 0), stop=(kc == NKC - 1)), reads=[hTo.part(kc), wb], writes=[pp])
                rl = rlp.get()
                kb.dma('sp', rl[:], r_own.t[cb * 128:(cb + 1) * 128, ts_], rl, reads=[r_own], writes=[rl])
                t1 = t1p.get()
                t2 = t2p.get()
                kb.op('act', lambda g: g.activation(out=t1[:], in_=pp[:], func=AF.Square), reads=[pp], writes=[t1])
                kb.op('dve', lambda g: g.tensor_scalar(out=t1[:], in0=t1[:], scalar1=0.044715, scalar2=1.0, op0=ALU.mult, op1=ALU.add),
                      reads=[t1], writes=[t1])
                kb.op('dve', lambda g: g.tensor_tensor(out=t1[:], in0=t1[:], in1=pp[:], op=ALU.mult), reads=[t1, pp], writes=[t1])
                kb.op('act', lambda g: g.activation(out=t1[:], in_=t1[:], func=AF.Tanh, scale=0.7978845608028654), reads=[t1], writes=[t1])
                kb.op('dve', lambda g: g.scalar_tensor_tensor(out=t2[:], in0=t1[:], scalar=1.0, in1=pp[:], op0=ALU.add, op1=ALU.mult),
                      reads=[t1, pp], writes=[t2])
                go = gop.get()
                kb.op('dve', lambda g: g.scalar_tensor_tensor(out=go[:], in0=t2[:], scalar=0.5, in1=rl[:], op0=ALU.mult, op1=ALU.mult),
                      reads=[t2, rl], writes=[go])
                kb.dma('pool', rgT_scr.t[cb * 128:(cb + 1) * 128, ts_], go[:], go, reads=[go], writes=[rgT_scr])
        kb.barrier()
        kb.release(pools['xt'].bufs + wstg.bufs + szp.bufs + gop.bufs + rlp.bufs + [bgb])

    with ExitStack() as st:
        st.enter_context(nc.named_scope("st_G1b"))
        pools = norm_pools(st, "N", nxn=2, npt=4)
        selb = kb.sb(st, "Nsel", [128, 4], F32)
        kb.dma('sp', selb[:], sel[:, :], selb, reads=[sel], writes=[selb])
        yop = Pool(kb, st, "Nyo", [128, D], F32, 2)
        ynp = Pool(kb, st, "Nyn", [128, NKC, 128], BF16, 2)
        for tt in range(16):
            szt = pools['xt'].get()
            kb.dma('sp', szt[:], sz_scr.t[tt * 128:(tt + 1) * 128, :], szt, reads=[sz_scr], writes=[szt])
            yo = yop.get()
            kb.dma('sp', yo[:], y_own.t[tt * 128:(tt + 1) * 128, :], yo, reads=[y_own], writes=[yo])
            kb.op('dve', lambda g: g.tensor_tensor(out=yo[:], in0=yo[:], in1=szt[:], op=ALU.mult), reads=[yo, szt], writes=[yo])
            yn = ynp.get()
            norm_T(pools, yo, yn, 0, lambda kc: nrmb[:, 2, kc:kc + 1], None, [nrmb])
            kb.dma('pool', fm_view(ynT_scr)[:, :, tt * 128:(tt + 1) * 128], yn[:], yn, reads=[yn], writes=[ynT_scr])
        kb.barrier()
        kb.release(pools['xt'].bufs + yop.bufs + ynp.bufs + [selb])

    with ExitStack() as st:
        st.enter_context(nc.named_scope("st_H"))
        ynT = kb.sb(st, "HynT", [128, NKC, 2048], BF16)
        rgT = kb.sb(st, "HrgT", [128, NKC, 2048], BF16)
        for kc in range(NKC):
            kb.dma('sp', ynT[:, kc, :], ynT_scr.t[kc * 128:(kc + 1) * 128, :], ynT, reads=[ynT_scr], writes=[ynT])
            kb.dma('sp', rgT[:, kc, :], rgT_scr.t[kc * 128:(kc + 1) * 128, :], rgT, reads=[rgT_scr], writes=[rgT])
        wstg = Pool(kb, st, "Hwstg", [128, NKC, 128], F32, 3)
        wbp = Pool(kb, st, "Hwb", [128, NKC, 128], BF16, 4)
        pgs = Pool(kb, st, "Hpgs", [128, 512], F32, 3, psum=True)
        pgr = Pool(kb, st, "Hpgr", [128, 512], F32, 3, psum=True)
        gsp = Pool(kb, st, "Hgs", [128, 512], BF16, 2)
        grp = Pool(kb, st, "Hgr", [128, 512], BF16, 2)
        m1p = Pool(kb, st, "Hm1", [128, 512], F32, 2)
        m2p = Pool(kb, st, "Hm2", [128, 512], F32, 2)
        mop = Pool(kb, st, "Hmo", [128, 512], BF16, 3)
        for cb in range(16):
            ws = wbp.get()
            load_weight(wstg, w_out_ssd, cb * 128, 128, ws, 0)
            wr = wbp.get()
            load_weight(wstg, w_out_lru, cb * 128, 128, wr, 0)
            for tg in range(4):
                ts_ = slice(tg * 512, (tg + 1) * 512)
                p1 = pgs.get()
                p2 = pgr.get()
                for kc in range(NKC):
                    kb.op('pe', lambda g: g.matmul(p1[:], ws[:, kc, :], ynT[:, kc, ts_], start=(kc == 0), stop=(kc == NKC - 1)),
                          reads=[ws, ynT], writes=[p1])
                for kc in range(NKC):
                    kb.op('pe', lambda g: g.matmul(p2[:], wr[:, kc, :], rgT[:, kc, ts_], start=(kc == 0), stop=(kc == NKC - 1)),
                          reads=[wr, rgT], writes=[p2])
                gs_ = gsp.get()
                gr_ = grp.get()
                kb.dma('sp', gs_[:], gT_scr.t[cb * 128:(cb + 1) * 128, ts_], gs_, reads=[gT_scr], writes=[gs_])
                kb.dma('sp', gr_[:], gT_scr.t[D + cb * 128:D + (cb + 1) * 128, ts_], gr_, reads=[gT_scr], writes=[gr_])
                m1 = m1p.get()
                m2 = m2p.get()
                kb.op('dve', lambda g: g.tensor_tensor(out=m1[:], in0=p1[:], in1=gs_[:], op=ALU.mult), reads=[p1, gs_], writes=[m1])
                kb.op('dve', lambda g: g.tensor_tensor(out=m2[:], in0=p2[:], in1=gr_[:], op=ALU.mult), reads=[p2, gr_], writes=[m2])
                mo = mop.get()
                kb.op('dve', lambda g: g.tensor_tensor(out=mo[:], in0=m1[:], in1=m2[:], op=ALU.add), reads=[m1, m2], writes=[mo])
                kb.dma('pool', mT_scr.t[cb * 128:(cb + 1) * 128, ts_], mo[:], mo, reads=[mo], writes=[mT_scr])
        kb.barrier()
        kb.release([ynT, rgT] + wstg.bufs + gsp.bufs + grp.bufs + mop.bufs)

    def gate_bcast(st, tag):
        gmb = kb.sb(st, tag + "gmb", [128, 2, D], F32)
        for a_ in range(2):
            kb.dma('sp', gmb[:, a_, :], gvec.t[a_:a_ + 1, :].partition_broadcast(128), gmb, reads=[gvec], writes=[gmb])
        return gmb

    def tm_gemm_residual(tag, AT, nkc, w_dram, res_src, gidx, dst, tok0, ntt, wstg, cwid=512):
        with ExitStack() as st2:
            st2.enter_context(nc.named_scope("st_" + tag))
            gmb = gate_bcast(st2, tag)
            wsp = Pool(kb, st2, tag + "ws", [128, nkc, cwid], BF16, 2)
            pgp = Pool(kb, st2, tag + "pg", [128, 512], F32, 4, psum=True)
            xrp = Pool(kb, st2, tag + "xr", [128, 512], F32, 3)
            xop = Pool(kb, st2, tag + "xo", [128, 512], F32, 3)
            for cg in range(D // cwid):
                cs = slice(cg * cwid, (cg + 1) * cwid)
                wsl = wsp.get()
                k0 = 0
                while k0 < nkc:
                    nk = min(16, nkc - k0)
                    load_weight(wstg, w_dram, cg * cwid, cwid, wsl, 0, row_lo=k0 * 128, nkc=nk, kc_dst=k0)
                    k0 += nk
                for tt in range(ntt):
                    pp = pgp.get()
                    for kc in range(nkc):
                        kb.op('pe', lambda g: g.matmul(pp[:, 0:cwid], AT[:, kc, tt * 128:(tt + 1) * 128], wsl[:, kc, :],
                                                       start=(kc == 0), stop=(kc == nkc - 1)), reads=[AT, wsl], writes=[pp])
                    xr_ = xrp.get()
                    r0 = tok0 + tt * 128
                    kb.dma('sp', xr_[:, 0:cwid], res_src.t[r0:r0 + 128, cs], xr_, reads=[res_src], writes=[xr_])
                    xo_ = xop.get()
                    kb.op('dve', lambda g: g.tensor_tensor(out=xo_[:, 0:cwid], in0=pp[:, 0:cwid], in1=gmb[:, gidx, cs], op=ALU.mult),
                          reads=[pp, gmb], writes=[xo_])
                    kb.op('dve', lambda g: g.tensor_tensor(out=xo_[:, 0:cwid], in0=xo_[:, 0:cwid], in1=xr_[:, 0:cwid], op=ALU.add),
                          reads=[xo_, xr_], writes=[xo_])
                    kb.dma('pool', dst.t[r0:r0 + 128, cs], xo_[:, 0:cwid], xo_, reads=[xo_], writes=[dst])
            kb.barrier()
            kb.release([gmb] + xrp.bufs + xop.bufs)

    with ExitStack() as st:
        mT = kb.sb(st, "HmT", [128, NKC, 2048], BF16)
        for kc in range(NKC):
            kb.dma('sp', mT[:, kc, :], mT_scr.t[kc * 128:(kc + 1) * 128, :], mT, reads=[mT_scr], writes=[mT])
        wstg = Pool(kb, st, "H3wstg", [128, NKC, 128], F32, 3)
        tm_gemm_residual("H3", mT, NKC, w_o, xo, 0, x1_scr, 0, 16, wstg)
        kb.release([mT] + wstg.bufs)

    with ExitStack() as st:
        st.enter_context(nc.named_scope("st_I"))
        pools = norm_pools(st, "I", nxn=2, npt=2, nxt=4)
        h2T = kb.sb(st, "h2T", [128, NKC, 2048], BF16)
        build_hT(pools, x1_scr, h2T, 16, 4, 5)
        wstg = Pool(kb, st, "Iwstg", [128, NKC, 128], F32, 3)
        wbp = Pool(kb, st, "Iwb", [128, NKC, 128], BF16, 4)
        pgg = Pool(kb, st, "Ipgg", [128, 512], F32, 3, psum=True)
        pgu = Pool(kb, st, "Ipgu", [128, 512], F32, 3, psum=True)
        sgp = Pool(kb, st, "Isg", [128, 512], F32, 3)
        aop = Pool(kb, st, "Iao", [128, 512], BF16, 3)
        for cb in range(FFN // 128):
            wg_ = wbp.get()
            load_weight(wstg, w13, cb * 128, 128, wg_, 0)
            wu_ = wbp.get()
            load_weight(wstg, w13, FFN + cb * 128, 128, wu_, 0)
            for tg in range(4):
                ts_ = slice(tg * 512, (tg + 1) * 512)
                p1 = pgg.get()
                p2 = pgu.get()
                for kc in range(NKC):
                    kb.op('pe', lambda g: g.matmul(p1[:], wg_[:, kc, :], h2T[:, kc, ts_], start=(kc == 0), stop=(kc == NKC - 1)),
                          reads=[wg_, h2T.part(kc)], writes=[p1])
                for kc in range(NKC):
                    kb.op('pe', lambda g: g.matmul(p2[:], wu_[:, kc, :], h2T[:, kc, ts_], start=(kc == 0), stop=(kc == NKC - 1)),
                          reads=[wu_, h2T.part(kc)], writes=[p2])
                sg = sgp.get()
                kb.op('act', lambda g: g.activation(out=sg[:], in_=p1[:], func=AF.Silu), reads=[p1], writes=[sg])
                ao = aop.get()
                kb.op('dve', lambda g: g.tensor_tensor(out=ao[:], in0=p2[:], in1=sg[:], op=ALU.mult), reads=[p2, sg], writes=[ao])
                kb.dma('pool', aT_scr.t[cb * 128:(cb + 1) * 128, ts_], ao[:], ao, reads=[ao], writes=[aT_scr])
        kb.barrier()
        kb.release(pools['xt'].bufs + wstg.bufs + aop.bufs)

    NKF = FFN // 128
    for half in range(2):
        with ExitStack() as st:
            aT = kb.sb(st, "IaT%d" % half, [128, NKF, 1024], BF16)
            for kc in range(NKF):
                kb.dma('sp', aT[:, kc, :], aT_scr.t[kc * 128:(kc + 1) * 128, half * 1024:(half + 1) * 1024], aT, reads=[aT_scr], writes=[aT])
            wstg = Pool(kb, st, "I2wstg%d" % half, [128, NKC, 128], F32, 3)
            tm_gemm_residual("I2%d" % half, aT, NKF, w2, x1_scr, 1, x2_scr, half * 1024, 8, wstg, cwid=256)
            kb.release([aT] + wstg.bufs)

    with ExitStack() as st:
        st.enter_context(nc.named_scope("st_J"))
        fnb = kb.sb(st, "fnb", [128, D], F32)
        kb.dma('sp', fnb[:], fin_norm.t.partition_broadcast(128), fnb, reads=[fin_norm], writes=[fnb])
        xtp = Pool(kb, st, "Jxt", [128, D], F32, 3)
        sqp = Pool(kb, st, "Jsq", [128, D], BF16, 1)
        ssp = Pool(kb, st, "Jss", [128, 4], F32, 2)
        otp = Pool(kb, st, "Jot", [128, D], F32, 3)
        for tt in range(16):
            xt = xtp.get()
            kb.dma('sp', xt[:], x2_scr.t[tt * 128:(tt + 1) * 128, :], xt, reads=[x2_scr], writes=[xt])
            sq = sqp.get()
            ss = ssp.get()
            kb.op('dve', lambda g: g.memset(ss[:], 0.0), writes=[ss])
            kb.op('act', lambda g: g.activation(out=sq[:], in_=xt[:], func=AF.Square, accum_out=ss[:, 0:1]), reads=[xt], writes=[sq, ss])
            kb.op('dve', lambda g: g.tensor_scalar(out=ss[:, 1:2], in0=ss[:, 0:1], scalar1=1.0 / D, scalar2=EPS,
                                                   op0=ALU.mult, op1=ALU.add), reads=[ss], writes=[ss])
            kb.op('act', lambda g: g.activation(out=ss[:, 3:4], in_=ss[:, 1:2], func=AF.Sqrt), reads=[ss], writes=[ss])
            kb.op('dve', lambda g: g.reciprocal(out=ss[:, 2:3], in_=ss[:, 3:4]), reads=[ss], writes=[ss])
            ot = otp.get()
            kb.op('dve', lambda g: g.scalar_tensor_tensor(out=ot[:], in0=xt[:], scalar=ss[:, 2:3], in1=fnb[:], op0=ALU.mult, op1=ALU.mult),
                  reads=[xt, ss, fnb], writes=[ot])
            kb.dma('pool', out.t[tt * 128:(tt + 1) * 128, :], ot[:], ot, reads=[ot], writes=[out])
        kb.barrier()
        kb.release([fnb] + xtp.bufs + otp.bufs)

    kb.barrier()
    gs.close()
    return nc


def _fm(v, n=16):
    return np.ascontiguousarray(np.asarray(v, np.float32).reshape(n, 128).T)


def _endsel():
    e = np.zeros((128, 2, 128), np.float32)
    e[127, 0, :] = 1.0
    e[0, 1, :] = 1.0
    return e


def prep_core(inp, core):
    b, k = core // 4, core % 4
    f = lambda a: np.ascontiguousarray(np.asarray(a, np.float32))
    m = {
        "xb": f(inp["x"][b]), "xo": f(inp["x"][b, 2048 * k:2048 * (k + 1)]), "ctxb": f(inp["ctx"][b]),
        "cvec": f(np.stack([_fm(inp["c"][b]), _fm(inp["c_ctx"])], axis=-1)),
        "w_ada": f(inp["w_ada"][0]), "b_ada": _fm(inp["b_ada"][0], 96),
        "nrm": f(np.stack([_fm(inp["norm_mix"][0]), _fm(inp["norm_ffn"][0]), _fm(inp["ssd_norm"][0])], axis=1)),
        "fin_norm": f(inp["final_norm"][None, :]), "ident": np.eye(128, dtype=np.float32), "w_in": f(inp["w_in"][0]),
        "cw_ssd": f(np.concatenate([inp["ssd_conv_w"][0], inp["ssd_conv_b"][0][None]], 0).reshape(5, 24, 128).transpose(2, 1, 0)),
        "dtp": f(np.tile(np.stack([inp["ssd_dt_bias"][0], inp["ssd_a_log"][0]], -1).transpose(1, 0, 2), (4, 1, 1))),
        "dskb": f(np.tile(inp["ssd_d"][0][None, :], (128, 1))),
        "masks": f(np.stack([np.triu(np.ones((128, 128), np.float32)), np.tril(np.ones((128, 128), np.float32))], 1)),
        "sel": f(np.tile(np.eye(4, dtype=np.float32)[k][None, :], (128, 1))),
        "cmask": f(np.tile(np.stack([(np.arange(64) < 16 * k), (np.arange(64) >= 16 * (k + 1))], 0).astype(np.float32)[None], (128, 1, 1))),
        "endsel": _endsel(),
        "dmask": np.eye(32, dtype=np.float32).reshape(32, 4, 8),
        "lcw": f(np.concatenate([inp["lru_conv_w"][0], inp["lru_conv_b"][0][None]], 0).reshape(5, 16, 128).transpose(2, 1, 0)),
        "lgw": f(np.stack([inp["lru_w_a"][0][0], inp["lru_w_x"][0][0], inp["lru_w_a"][0][1], inp["lru_w_x"][0][1]], axis=2)),
        "lgb": f(np.stack([inp["lru_b_a"][0][0], inp["lru_b_x"][0][0], inp["lru_b_a"][0][1], inp["lru_b_x"][0][1]], axis=-1)
                 .reshape(16, 128, 4).transpose(1, 0, 2)),
        "llam": f(np.asarray(inp["lru_lambda"][0]).T.reshape(16, 128, 2).transpose(1, 0, 2)),
        "w_out_ssd": f(inp["w_out_ssd"][0]), "w_out_lru": f(inp["w_out_lru"][0]), "w_gate": f(inp["w_gate"][0]),
        "bgate": _fm(inp["b_gate"][0], 32), "w_o": f(inp["w_o"][0]), "w13": f(inp["ffn_w13"][0]), "w2": f(inp["ffn_w2"][0]),
    }
    return m


def kernel(**inputs):
    inp = {k_: np.asarray(v) for k_, v in inputs.items()}
    nc = build_program(None)
    in_maps = [prep_core(inp, c) for c in range(8)]
    res = run_bass_kernel_spmd(nc, in_maps, core_ids=list(range(8)))
    out = np.zeros((2, SEQ, D), np.float32)
    for c in range(8):
        b, k = c // 4, c % 4
        out[b, 2048 * k:2048 * (k + 1)] = res.results[c]["out"]
    return out
```
